# Optimizing a Trainium2 kernel written in Bass

```python
import jax, jax.numpy as jnp
from jax import lax
import numpy as np

D_MODEL = 1024
BATCH = 8
SEQ = 8192
DEPTH = 2
DEC_BATCH = 32
DEC_SEQ = 32
PAST_LEN = 4096

CHUNK = 64
N_META = 16
N_HEADS = 8
D_NOPE = 64
D_ROPE = 32
D_QK = D_NOPE + D_ROPE
D_V = 64
Q_RANK = 384
KV_RANK = 256
ATTN_WIDTH = N_HEADS * D_V
POOL_WIDTH = D_MODEL - ATTN_WIDTH
POOL_WINDOWS = (2, 4, 8, 16)
N_POOL_GROUPS = len(POOL_WINDOWS)
POOL_GROUP = POOL_WIDTH // N_POOL_GROUPS
POOL_STATE = max(POOL_WINDOWS) - 1
D_IN = POOL_WIDTH + Q_RANK + KV_RANK + D_ROPE
D_FF = ((8 * D_MODEL + 2) // 3 + 255) // 256 * 256
ROPE_THETA = 10000.0
EPS = 1e-6
Q_BLOCK = 128

kernel_name = 'hybrid_pool_mla_stream_step'


def rms_norm(x, g):
    xf = x.astype(jnp.float32)
    y = xf * lax.rsqrt(jnp.mean(xf * xf, axis=-1, keepdims=True) + EPS)
    return (y * g.astype(jnp.float32)).astype(x.dtype)


def rope(x, pos):
    half = D_ROPE // 2
    inv = ROPE_THETA ** (-jnp.arange(half, dtype=jnp.float32) / half)
    ang = pos.astype(jnp.float32)[:, None] * inv[None, :]
    shape = (ang.shape[0],) + (1,) * (x.ndim - 3) + (half,)
    cos = jnp.cos(ang).reshape(shape)
    sin = jnp.sin(ang).reshape(shape)
    xf = x.astype(jnp.float32)
    x1, x2 = xf[..., :half], xf[..., half:]
    return jnp.concatenate([x1 * cos - x2 * sin, x1 * sin + x2 * cos], axis=-1).astype(x.dtype)


def pool_mix(u_ext, n_prev, w_pool, pool_scale):
    B, L, _ = u_ext.shape
    T = L - n_prev
    uf = u_ext.astype(jnp.float32)
    csum = jnp.concatenate([jnp.zeros((B, 1, POOL_WIDTH), jnp.float32), jnp.cumsum(uf, axis=1)], axis=1)
    t = jnp.arange(n_prev, L)
    means = []
    for g, w in enumerate(POOL_WINDOWS):
        sl = slice(g * POOL_GROUP, (g + 1) * POOL_GROUP)
        lo = jnp.maximum(t + 1 - w, 0)
        cnt = jnp.minimum(t + 1, w).astype(jnp.float32)
        means.append((csum[:, t + 1, sl] - csum[:, lo, sl]) / cnt[None, :, None])
    d = (jnp.concatenate(means, axis=-1) - uf[:, n_prev:]).reshape(B, T, N_POOL_GROUPS, POOL_GROUP)
    y = jnp.einsum('btgc,gcd->btgd', d, w_pool.astype(jnp.float32)).reshape(B, T, POOL_WIDTH)
    return (y * pool_scale.astype(jnp.float32)).astype(u_ext.dtype)


def block_causal_attention(q, k, v, q_chunk, k_chunk):
    scale = D_QK ** -0.5

    def attend(qb, qc):
        s = jnp.einsum('bqhd,bkhd->bhqk', qb, k, preferred_element_type=jnp.float32) * scale
        mask = qc[:, None] >= k_chunk[None, :]
        s = jnp.where(mask[None, None], s, -jnp.inf)
        p = jax.nn.softmax(s, axis=-1)
        return jnp.einsum('bhqk,bkhd->bqhd', p.astype(v.dtype), v)

    B, Lq, H, D = q.shape
    if Lq <= Q_BLOCK:
        return attend(q, q_chunk)
    nb = -(-Lq // Q_BLOCK)
    pad = nb * Q_BLOCK - Lq
    qp = jnp.pad(q, ((0, 0), (0, pad), (0, 0), (0, 0)))
    cp = jnp.pad(q_chunk, (0, pad), mode='edge')
    qb = qp.reshape(B, nb, Q_BLOCK, H, D).transpose(1, 0, 2, 3, 4)
    cb = cp.reshape(nb, Q_BLOCK)
    out = lax.map(lambda a: attend(a[0], a[1]), (qb, cb))
    out = out.transpose(1, 0, 2, 3, 4).reshape(B, nb * Q_BLOCK, H, D_V)
    return out[:, :Lq]


def mixer(h, pos, q_chunk, k_chunk, past_lat, past_kpe, past_pool,
          w_in, q_a_norm, w_uq, kv_a_norm, w_uk, w_uv, q_norm, k_norm, w_pool, pool_scale, w_o):
    B, T, _ = h.shape
    z = h @ w_in
    u, q_lat, kv_lat, k_pe = jnp.split(z, [POOL_WIDTH, POOL_WIDTH + Q_RANK, POOL_WIDTH + Q_RANK + KV_RANK], axis=-1)
    u_ext = jnp.concatenate([past_pool.astype(u.dtype), u], axis=1)
    pool_out = pool_mix(u_ext, u_ext.shape[1] - T, w_pool, pool_scale)
    new_pool = u_ext[:, -POOL_STATE:]
    q = (rms_norm(q_lat, q_a_norm) @ w_uq).reshape(B, T, N_HEADS, D_QK)
    q = rms_norm(jnp.concatenate([q[..., :D_NOPE], rope(q[..., D_NOPE:], pos)], axis=-1), q_norm)
    c_kv = rms_norm(kv_lat, kv_a_norm)
    k_pe = rope(k_pe, pos)
    lat_all = jnp.concatenate([past_lat.astype(c_kv.dtype), c_kv], axis=1)
    kpe_all = jnp.concatenate([past_kpe.astype(k_pe.dtype), k_pe], axis=1)
    Lk = lat_all.shape[1]
    k_nope = (lat_all @ w_uk).reshape(B, Lk, N_HEADS, D_NOPE)
    v = (lat_all @ w_uv).reshape(B, Lk, N_HEADS, D_V)
    k = jnp.concatenate([k_nope, jnp.broadcast_to(kpe_all[:, :, None, :], (B, Lk, N_HEADS, D_ROPE))], axis=-1)
    k = rms_norm(k, k_norm)
    attn = block_causal_attention(q, k, v, q_chunk, k_chunk).reshape(B, T, ATTN_WIDTH)
    y = jnp.concatenate([pool_out, attn], axis=-1) @ w_o
    return y, c_kv, k_pe, new_pool


def swiglu(h, w_gate, w_up, w_down):
    return (jax.nn.silu(h @ w_gate) * (h @ w_up)) @ w_down


def trunk(x, pos, q_chunk, k_chunk, past_lat, past_kpe, past_pool, weights):
    (norm_mix, w_in, q_a_norm, w_uq, kv_a_norm, w_uk, w_uv, q_norm, k_norm,
     w_pool, pool_scale, w_o, norm_ffn, w_gate, w_up, w_down) = weights
    lat_rows, kpe_rows, pool_rows = [], [], []
    for l in range(DEPTH):
        y, c_kv, k_pe, new_pool = mixer(
            rms_norm(x, norm_mix[l]), pos, q_chunk, k_chunk, past_lat[l], past_kpe[l], past_pool[l],
            w_in[l], q_a_norm[l], w_uq[l], kv_a_norm[l], w_uk[l], w_uv[l], q_norm[l], k_norm[l],
            w_pool[l], pool_scale[l], w_o[l])
        x = x + y
        x = x + swiglu(rms_norm(x, norm_ffn[l]), w_gate[l], w_up[l], w_down[l])
        lat_rows.append(c_kv)
        kpe_rows.append(k_pe)
        pool_rows.append(new_pool)
    return x, jnp.stack(lat_rows), jnp.stack(kpe_rows), jnp.stack(pool_rows)


def setup_inputs(seed: int = 0) -> dict:
    key = jax.random.key(seed)
    ks = jax.random.split(key, 22)
    f32 = jnp.float32

    def normal(k, shape):
        return jax.random.normal(k, shape, f32)

    def dense(k, shape, fan_in):
        return jax.random.normal(k, shape, f32) * fan_in ** -0.5

    def gain(k, shape):
        return 1.0 + 0.05 * jax.random.normal(k, shape, f32)

    return {
        'x_prompt': normal(ks[0], (BATCH, SEQ, D_MODEL)),
        'x_sample': normal(ks[1], (DEC_BATCH, DEC_SEQ, D_MODEL)),
        'cache_latent': normal(ks[2], (DEPTH, DEC_BATCH, N_META + PAST_LEN, KV_RANK)),
        'cache_krope': normal(ks[3], (DEPTH, DEC_BATCH, N_META + PAST_LEN, D_ROPE)),
        'state_pool': normal(ks[4], (DEPTH, DEC_BATCH, POOL_STATE, POOL_WIDTH)),
        'meta_tokens': normal(ks[5], (N_META, D_MODEL)),
        'norm_mix': gain(ks[6], (DEPTH, D_MODEL)),
        'w_in': dense(ks[7], (DEPTH, D_MODEL, D_IN), D_MODEL),
        'q_a_norm': gain(ks[8], (DEPTH, Q_RANK)),
        'w_uq': dense(ks[9], (DEPTH, Q_RANK, N_HEADS * D_QK), Q_RANK),
        'kv_a_norm': gain(ks[10], (DEPTH, KV_RANK)),
        'w_uk': dense(ks[11], (DEPTH, KV_RANK, N_HEADS * D_NOPE), KV_RANK),
        'w_uv': dense(ks[12], (DEPTH, KV_RANK, N_HEADS * D_V), KV_RANK),
        'q_norm': gain(ks[13], (DEPTH, D_QK)),
        'k_norm': gain(ks[14], (DEPTH, D_QK)),
        'w_pool': dense(ks[15], (DEPTH, N_POOL_GROUPS, POOL_GROUP, POOL_GROUP), POOL_GROUP),
        'pool_scale': gain(ks[16], (DEPTH, POOL_WIDTH)),
        'w_o': dense(ks[17], (DEPTH, D_MODEL, D_MODEL), D_MODEL),
        'norm_ffn': gain(ks[18], (DEPTH, D_MODEL)),
        'w_gate': dense(ks[19], (DEPTH, D_MODEL, D_FF), D_MODEL),
        'w_up': dense(ks[20], (DEPTH, D_MODEL, D_FF), D_MODEL),
        'w_down': dense(ks[21], (DEPTH, D_FF, D_MODEL), D_FF),
    }


def reference(x_prompt, x_sample, cache_latent, cache_krope, state_pool, meta_tokens,
              norm_mix, w_in, q_a_norm, w_uq, kv_a_norm, w_uk, w_uv, q_norm, k_norm,
              w_pool, pool_scale, w_o, norm_ffn, w_gate, w_up, w_down):
    weights = (norm_mix, w_in, q_a_norm, w_uq, kv_a_norm, w_uk, w_uv, q_norm, k_norm,
               w_pool, pool_scale, w_o, norm_ffn, w_gate, w_up, w_down)
    dt = x_prompt.dtype
    Bp, S, _ = x_prompt.shape
    meta = jnp.broadcast_to(meta_tokens.astype(dt)[None], (Bp, N_META, D_MODEL))
    xp = jnp.concatenate([meta, x_prompt], axis=1)
    pos_p = jnp.arange(N_META + S, dtype=jnp.int32)
    chunk_p = jnp.where(pos_p < N_META, -1, (pos_p - N_META) // CHUNK)
    empty_lat = jnp.zeros((DEPTH, Bp, 0, KV_RANK), dt)
    empty_kpe = jnp.zeros((DEPTH, Bp, 0, D_ROPE), dt)
    empty_pool = jnp.zeros((DEPTH, Bp, 0, POOL_WIDTH), dt)
    yp, lat_p, kpe_p, pool_p = trunk(xp, pos_p, chunk_p, chunk_p, empty_lat, empty_kpe, empty_pool, weights)
    y_prompt = yp[:, N_META:]
    T = x_sample.shape[1]
    past = cache_latent.shape[2] - N_META
    pos_s = N_META + past + jnp.arange(T, dtype=jnp.int32)
    k_idx = jnp.arange(N_META + past + T, dtype=jnp.int32)
    chunk_k = jnp.where(k_idx < N_META, -1, (k_idx - N_META) // CHUNK)
    chunk_q = (pos_s - N_META) // CHUNK
    y_sample, lat_s, kpe_s, pool_s = trunk(x_sample, pos_s, chunk_q, chunk_k,
                                           cache_latent, cache_krope, state_pool, weights)
    return (y_prompt, y_sample, lat_p, kpe_p, pool_p, lat_s, kpe_s, pool_s)
```

```python
import numpy as np
import concourse.bass as bass
import concourse.mybir as mybir
from concourse.bass_utils import run_bass_kernel_spmd

F32 = mybir.dt.float32
BF16 = mybir.dt.bfloat16
AF = mybir.ActivationFunctionType
ALU = mybir.AluOpType

D = 1024
NCH = 8
H = 8
DQK = 96
QR = 384
KVR = 256
DR = 32
DFF = 2816
NFF = 22
NMETA = 16
TS = 256
EPS = 1e-6
WINS = (2, 4, 8, 16)
SCALE = DQK ** -0.5

O_WIN = 0
WIN_W = 1216
O_WUQ = O_WIN + 8 * WIN_W
O_WUK = O_WUQ + 3 * 8 * 192
O_WUV = O_WUK + 2 * 8 * 64
O_WPOOL = O_WUV + 2 * 512
WA_TOT = O_WPOOL + 4 * 128
O_WO = WA_TOT
O_GU = O_WO + 8 * 1024
GU_PIECE = 8 * 2 * 128
NGU = 4
O_DN = O_GU + NFF * GU_PIECE
DN_PIECE = 1024
NDN = 4
WTOT = O_DN + NFF * DN_PIECE
CAST_PIECE = 1448
assert WTOT % CAST_PIECE == 0
NG = 54


class Buf:
    __slots__ = ("name", "w", "r_eng", "r_dma", "excl")

    def __init__(self, name):
        self.name = name
        self.excl = name.startswith("ps")
        self.w = None
        self.r_eng = {}
        self.r_dma = []


class Op:
    __slots__ = ("eng", "is_dma", "fn", "deps", "signal", "sem", "val")


class Prog:
    ENGS = ("pe", "act", "dve", "pool", "sp")

    def __init__(self):
        self.ops = []

    def add(self, eng, fn, reads=(), writes=(), dma=False):
        op = Op()
        op.eng = eng
        op.is_dma = dma
        op.fn = fn
        op.signal = False
        op.sem = None
        op.val = 0
        deps = set()
        for b in reads:
            if b.w is not None:
                deps.add(b.w)
            if b.excl:
                deps.update(o for e2, o in b.r_eng.items() if e2 != eng)
        for b in writes:
            if b.w is not None:
                deps.add(b.w)
            deps.update(b.r_eng.values())
            deps.update(b.r_dma)
        deps.discard(op)
        if eng == "pe" and not dma:
            deps = [d for d in deps if d.is_dma or d.eng != "pe"]
        op.deps = list(deps)
        wset = set(id(b) for b in writes)
        for b in writes:
            b.w = op
            b.r_eng = {}
            b.r_dma = []
        for b in reads:
            if id(b) in wset:
                continue
            if dma:
                b.r_dma.append(op)
            else:
                b.r_eng[eng] = op
        self.ops.append(op)
        return op

    def emit(self, nc, block_engines, eng_sems, dma_sems):
        ops = self.ops
        npool = len(dma_sems)
        last_on_sem = [None] * npool
        cnt_on_sem = [0] * npool
        k = 0
        for op in ops:
            if op.is_dma:
                i = k % npool
                k += 1
                cnt_on_sem[i] += 1
                op.sem = dma_sems[i]
                op.val = 16 * cnt_on_sem[i]
                if last_on_sem[i] is not None:
                    op.deps.append(last_on_sem[i])
                last_on_sem[i] = op
        for op in ops:
            for d in op.deps:
                d.signal = True
        cnt = {e: 0 for e in self.ENGS}
        for op in ops:
            if not op.is_dma and op.signal:
                cnt[op.eng] += 1
                op.sem = eng_sems[op.eng]
                op.val = cnt[op.eng]
        self.final_dma = [(dma_sems[i], 16 * cnt_on_sem[i]) for i in range(npool) if cnt_on_sem[i]]
        self.counts = cnt
        by_eng = {e: [] for e in self.ENGS}
        for op in ops:
            by_eng[op.eng].append(op)

        def make_stream(eng):
            def stream(e):
                known = {}
                for op in by_eng[eng]:
                    for d in op.deps:
                        key = id(d.sem)
                        if known.get(key, 0) < d.val:
                            e.wait_ge(d.sem, d.val)
                            known[key] = d.val
                    ins = op.fn(e)
                    if op.is_dma:
                        ins.then_inc(op.sem, 16)
                    elif op.signal:
                        ins.then_inc(op.sem, 1)
                if eng == "sp":
                    for sem, v in self.final_dma:
                        if known.get(id(sem), 0) < v:
                            e.wait_ge(sem, v)
            return stream

        for eng in self.ENGS:
            if by_eng[eng] or eng == "sp":
                block_engines[eng](make_stream(eng))


class Builder:
    def __init__(self, SEQ, NROWS, nc):
        self.nc = nc
        self.P = Prog()
        self.SEQ = SEQ
        self.NROWS = NROWS
        self.NSUP = SEQ // TS
        self.LP = NMETA + SEQ
        self.bufs = {}
        self._psrot = 0
        self._ps8 = 0
        self._ps4 = 0
        self._deferred = []

    def B(self, name):
        b = self.bufs.get(name)
        if b is None:
            b = Buf(name)
            self.bufs[name] = b
        return b

    def sb(self, name, shape, dt):
        return self.nc.alloc_sbuf_tensor(name, list(shape), dt)

    def pe(self, fn, r, w):
        return self.P.add("pe", fn, r, w)

    def act(self, fn, r, w):
        return self.P.add("act", fn, r, w)

    def dve(self, fn, r, w):
        return self.P.add("dve", fn, r, w)

    def pool(self, fn, r, w):
        return self.P.add("pool", fn, r, w)

    def dma(self, out, in_, r, w, eng="sp"):
        return self.P.add(eng, lambda e: e.dma_start(out=out, in_=in_), r, w, dma=True)

    def dbg(self, name, ap, shape, dt, bufs):
        import os
        if not os.environ.get("MK_DEBUG"):
            return
        t = self.nc.dram_tensor("dbg_" + name, list(shape), dt, kind="ExternalOutput").ap()
        self.dma(t, ap, bufs, [self.B("dbg_" + name)])

    def psum8(self):
        i = self._ps8 % 8
        self._ps8 += 1
        return self.ps[i], self.psb[i]

    def psum(self):
        i = self._psrot % 6
        self._psrot += 1
        return self.ps[i], self.psb[i]

    def declare(self):
        nc = self.nc
        SEQ, NROWS, LP = self.SEQ, self.NROWS, self.LP
        di = lambda n, s: nc.dram_tensor(n, list(s), F32, kind="ExternalInput").ap()
        do = lambda n, s: nc.dram_tensor(n, list(s), F32, kind="ExternalOutput").ap()
        self.xp = di("xp", [SEQ, D])
        self.meta = di("meta", [NMETA, D])
        self.xs = di("xs", [128, D])
        self.clat = di("clat", [2, 4, NROWS, KVR])
        self.ckr = di("ckr", [2, 4, NROWS, DR])
        self.spool = di("spool", [2, 4, 15, 512])
        self.wblob = di("wblob", [2, 128, WTOT])
        self.gains_d = di("gains", [128, NG])
        self.NPOS = LP + 128
        self.rope_d = di("rope", [96, 2, self.NPOS])
        self.rcnt_d = di("rcnt", [128, 4, 16])
        self.ident_d = di("ident", [128, 128])
        self.yp = do("yp", [SEQ, D])
        self.ys = do("ys", [128, D])
        self.latp = do("latp", [2, LP, KVR])
        self.krp = do("krp", [2, LP, DR])
        self.poolp = do("poolp", [2, 15, 512])
        self.lats = do("lats", [2, 4, 32, KVR])
        self.krs = do("krs", [2, 4, 32, DR])
        self.pools = do("pools", [2, 4, 15, 512])
        self.wsc = nc.dram_tensor("wsc", [2, 128, WTOT], BF16, kind="Internal").ap()
        self.NKT = 1 + SEQ // 128
        self.kc = nc.dram_tensor("kc", [2, H, 96, self.NKT * 128], BF16, kind="Internal").ap()
        self.vc = nc.dram_tensor("vc", [2, H, 128, self.NKT, 128], BF16, kind="Internal").ap()

        self.ident = self.sb("identb", [128, 128], F32)
        self.ones = self.sb("ones", [128, 128], BF16)
        self.epsb = self.sb("epsb", [128, 1], F32)
        self.gains = self.sb("gainsb", [128, NG], F32)
        self.rcnt = self.sb("rcntb", [128, 4, 16], F32)
        self.ps = [nc.alloc_psum_tensor(f"ps{i}", [128, 512], F32) for i in range(8)]
        self.psb = [self.B(f"ps{i}") for i in range(8)]
        self.wA = self.sb("wA", [128, WA_TOT], BF16)
        self.wO = self.sb("wO", [128, 8 * 1024], BF16)
        self.wGU = [self.sb(f"wGU{i}", [128, GU_PIECE], BF16) for i in range(NGU)]
        self.wDN = [self.sb(f"wDN{i}", [128, DN_PIECE], BF16) for i in range(NDN)]
        T = TS
        self.xT = self.sb("xT", [128, 8, T], F32)
        self.hT = self.sb("hT", [128, 8, T], BF16)
        self.sq = self.sb("sq", [128, 8, T], BF16)
        self.rstd = self.sb("rstd", [128, 512], F32)
        self.rstd2 = self.sb("rstd2", [128, 512], F32)
        self.xin = self.sb("xin", [128, 2, D], F32)
        self.uext = self.sb("uext", [128, 4 * (15 + T + 60)], F32)
        self.ptmp = [self.sb(f"ptmp{i}", [128, 15 + T + 60], F32) for i in range(2)]
        self.hist = [self.sb(f"hist{l}", [128, 4, 15], F32) for l in range(2)]
        self.dT = self.sb("dT", [128, 4, T], BF16)
        self.qlat = self.sb("qlat", [128, 3, T], F32)
        self.qn = self.sb("qn", [128, 3, T], BF16)
        self.ckv = self.sb("ckv", [128, 2, T], F32)
        self.ckvb = self.sb("ckvb", [128, 2, T], BF16)
        self.kpt = self.sb("kpt", [32, 2, T], F32)
        self.kper = self.sb("kper", [32, T], F32)
        self.C2 = self.sb("C2", [96, 2, T], F32)
        self.S2 = self.sb("S2", [96, 2, T], F32)
        self.kC = self.sb("kC", [32, T], F32)
        self.kS = self.sb("kS", [32, T], F32)
        self.tq = [self.sb(f"tq{i}", [96, 2, T], F32) for i in range(2)]
        self.t2 = self.sb("t2", [96, 2, T], F32)
        self.qT = self.sb("qT", [96, 8, T], BF16)
        self.kT = self.sb("kT", [96, 8, T], BF16)
        self.vcur = self.sb("vcur", [128, 8, 2, 128], BF16)
        self.vsm = self.sb("vsm", [32, 8, 128], BF16)
        self.mixT = self.sb("mixT", [128, 8, T], BF16)
        self.aT = self.sb("aT", [128, NFF, T], BF16)
        self.sg = [self.sb(f"sg{i}", [128, T], F32) for i in range(2)]
        self.kblk = [self.sb(f"kblk{i}", [96, 1024], BF16) for i in range(3)]
        self.vblk = [self.sb(f"vblk{i}", [128, 8, 128], BF16) for i in range(3)]
        self.pT = [self.sb(f"pT{i}", [128, 512], BF16) for i in range(4)]
        self.rl = [self.sb(f"rl{i}", [128, T], F32) for i in range(2)]
        self.otok = self.sb("otok", [128, 2, KVR + DR], F32)
        self.opool = self.sb("opool", [16, 512], F32)
        self.clin = self.sb("clin", [128, 2, KVR], F32)
        self.ckin = self.sb("ckin", [128, 2, DR], F32)
        self._pT = 0
        self._blk = 0
        self._sg = 0
        self._o = 0

    def init_consts(self):
        B = self.B
        self.dma(self.ident[:], self.ident_d[:, :], [], [B("ident")])
        self.dma(self.gains[:], self.gains_d[:, :], [], [B("gains")])
        self.dma(self.rcnt[:], self.rcnt_d[:, :, :], [], [B("rcnt")])
        self.pool(lambda e: e.memset(self.ones[:], 1.0), [], [B("ones")])
        self.pool(lambda e: e.memset(self.epsb[:], EPS), [], [B("epsb")])
        self.pool(lambda e: e.memset(self.vcur[:], 1.0), [], [B("vcur")])
        self.pool(lambda e: e.memset(self.vsm[:], 1.0), [], [B("vsm")])
        for i in range(2):
            self.pool(lambda e, l=i: e.memset(self.hist[l][:], 0.0), [], [B(f"hist{i}")])

    def cast_weights(self):
        B = self.B
        n = WTOT // CAST_PIECE
        fst = [(self.xin[:, :, :].rearrange("p a b -> p (a b)")[:, 0:CAST_PIECE], [B("xin0"), B("xin1")]),
               (self.xT[:, :, :].rearrange("p a b -> p (a b)")[:, 0:CAST_PIECE], [B("xT")])]
        aflat = self.aT[:, :, :].rearrange("p a b -> p (a b)")
        bst = []
        for i in range(3):
            c0, c1 = i * CAST_PIECE, (i + 1) * CAST_PIECE
            bst.append((aflat[:, c0:c1], [B(f"aT{c}") for c in range(c0 // TS, (c1 - 1) // TS + 1)]))
        k = 0
        for l in range(2):
            for i in range(n):
                fa, fb = fst[k % 2]
                ba, bb = bst[k % 3]
                c0 = i * CAST_PIECE
                self.dma(fa, self.wblob[l, :, c0:c0 + CAST_PIECE], [], fb, eng="pool")
                eng = ("dve", "act")[k % 2]
                if eng == "act":
                    self.act(lambda e, fa=fa, ba=ba: e.activation(out=ba, in_=fa, func=AF.Copy), fb, bb)
                else:
                    self.P.add(eng, lambda e, fa=fa, ba=ba: e.tensor_copy(out=ba, in_=fa), fb, bb)
                self.dma(self.wsc[l, :, c0:c0 + CAST_PIECE], ba, bb, [B(f"wsc{l}_{i}")], eng="pool")
                k += 1

    def wscb(self, l, c0, c1):
        return [self.B(f"wsc{l}_{i}") for i in range(c0 // CAST_PIECE, (c1 - 1) // CAST_PIECE + 1)]

    def load_wA(self, l):
        self.dma(self.wA[:], self.wsc[l, :, 0:WA_TOT], self.wscb(l, 0, WA_TOT), [self.B("wA")])

    def load_wO(self, l):
        self.dma(self.wO[:], self.wsc[l, :, O_WO:O_WO + 8192], self.wscb(l, O_WO, O_WO + 8192), [self.B("wO")])

    def load_gu(self, l, pc):
        s = pc % NGU
        o = O_GU + pc * GU_PIECE
        self.dma(self.wGU[s][:], self.wsc[l, :, o:o + GU_PIECE], self.wscb(l, o, o + GU_PIECE), [self.B(f"wGU{s}")])

    def load_dn(self, l, m):
        s = m % NDN
        o = O_DN + m * DN_PIECE
        self.dma(self.wDN[s][:], self.wsc[l, :, o:o + DN_PIECE], self.wscb(l, o, o + DN_PIECE), [self.B(f"wDN{s}")])

    def rstd_chain(self, ps_ap, psb, out_ap, outb, n, tmp_ap, tmpb):
        np_ = ps_ap.shape[0]
        self.act(lambda e: e.activation(out=tmp_ap, in_=ps_ap, func=AF.Ln, scale=1.0 / n, bias=self.epsb[0:np_, 0:1]),
                 [psb, self.B("epsb")], [tmpb])
        self.act(lambda e: e.activation(out=out_ap, in_=tmp_ap, func=AF.Exp, scale=-0.5), [tmpb], [outb])

    def norm_x(self, T, gcol):
        B = self.B
        xT, hT, sq = self.xT, self.hT, self.sq
        self.act(lambda e: e.activation(out=sq[:, :, 0:T], in_=xT[:, :, 0:T], func=AF.Square), [B("xT")], [B("sq")])
        ps, psb = self.psum()
        for k in range(8):
            self.pe(lambda e, k=k: e.matmul(ps[:, 0:T], self.ones[:, :], sq[:, k, 0:T], start=(k == 0), stop=(k == 7)),
                    [B("sq"), B("ones")], [psb])
        self.rstd_chain(ps[:, 0:T], psb, self.rstd[:, 0:T], B("rstd"), float(D), self.rstd2[:, 0:T], B("rstd2"))
        for k in range(8):
            self.dve(lambda e, k=k: e.scalar_tensor_tensor(
                out=hT[:, k, 0:T], in0=xT[:, k, 0:T], scalar=self.gains[:, gcol + k:gcol + k + 1],
                in1=self.rstd[:, 0:T], op0=ALU.mult, op1=ALU.mult), [B("xT"), B("rstd"), B("gains")], [B(f"hT{k}")])

    def hT_bufs(self):
        return [self.B(f"hT{k}") for k in range(8)]

    def load_x(self, src_rows, T):
        B = self.B
        col = 0
        for ti, (ap, n) in enumerate(src_rows):
            s = ti % 2
            self.dma(self.xin[0:n, s, :], ap, [], [B(f"xin{s}")])
            for k4 in range(2):
                ps, psb = self.psum()
                for kk in range(4):
                    k = k4 * 4 + kk
                    self.pe(lambda e, k=k, kk=kk, n=n, s=s, ps=ps: e.transpose(
                        ps[:, kk * 128:kk * 128 + n], self.xin[0:n, s, k * 128:(k + 1) * 128], self.ident[0:n, 0:n]),
                        [B(f"xin{s}"), B("ident")], [psb])
                for kk in range(4):
                    k = k4 * 4 + kk
                    self.act(lambda e, k=k, kk=kk, n=n, col=col, ps=ps: e.activation(
                        out=self.xT[:, k, col:col + n], in_=ps[:, kk * 128:kk * 128 + n], func=AF.Copy),
                        [psb], [B("xT")])
            col += n

    def store_y(self, dst_rows, T):
        B = self.B
        col = 0
        for ti, (ap, n) in enumerate(dst_rows):
            s = ti % 2
            for k4 in range(2):
                ps, psb = self.psum()
                for kk in range(4):
                    k = k4 * 4 + kk
                    self.pe(lambda e, k=k, kk=kk, n=n, col=col, ps=ps: e.transpose(
                        ps[0:n, kk * 128:(kk + 1) * 128], self.xT[:, k, col:col + n], self.ident[:, :]),
                        [B("xT"), B("ident")], [psb])
                self.act(lambda e, k4=k4, n=n, s=s, ps=ps: e.activation(
                    out=self.xin[0:n, s, k4 * 512:(k4 + 1) * 512], in_=ps[0:n, :], func=AF.Copy),
                    [psb], [B(f"xin{s}")])
            self.dbg(f"yst{self._psrot}", self.xin[0:n, s, :], [n, 1024], F32, [B(f"xin{s}")])
            self.dma(ap, self.xin[0:n, s, :], [B(f"xin{s}")], [B("yout")])
            col += n

    def mixer_front(self, l, T, segs, tok_tiles, pos0, kind, sup):
        B = self.B
        nseg, Tseg = segs
        L = 15 + Tseg
        wA, hT = self.wA, self.hT
        hb = self.hT_bufs()
        G = self.gains

        def win(k, c0, m):
            return wA[:, O_WIN + k * WIN_W + c0:O_WIN + k * WIN_W + c0 + m]

        for hh in range(2):
            self.dma(self.C2[:, hh, 0:T], self.rope_d[:, 0, pos0:pos0 + T], [], [B("C2")])
            self.dma(self.S2[:, hh, 0:T], self.rope_d[:, 1, pos0:pos0 + T], [], [B("S2")])
        self.dma(self.kC[:, 0:T], self.rope_d[64:96, 0, pos0:pos0 + T], [], [B("kC")])
        self.dma(self.kS[:, 0:T], self.rope_d[64:96, 1, pos0:pos0 + T], [], [B("kS")])

        uext = self.uext
        W4 = nseg * L

        def uview(g):
            return uext[:, g * W4:(g + 1) * W4].rearrange("p (s l) -> p s l", s=nseg)

        for g in range(4):
            ps, psb = self.psum()
            for k in range(8):
                self.pe(lambda e, k=k, g=g, ps=ps: e.matmul(ps[:, 0:T], win(k, g * 128, 128), hT[:, k, 0:T],
                                                            start=(k == 0), stop=(k == 7)), [hb[k], B("wA")], [psb])
            self.act(lambda e, g=g, ps=ps: e.activation(
                out=uview(g)[:, :, 15:L], in_=ps[:, 0:T].rearrange("p (s t) -> p s t", s=nseg), func=AF.Copy),
                [psb], [B(f"uext{g}")])
        ps_ssq, psb_ssq = self.psum()
        for j in range(3):
            ps, psb = self.psum()
            for k in range(8):
                self.pe(lambda e, k=k, j=j, ps=ps: e.matmul(ps[:, 0:T], win(k, 512 + j * 128, 128), hT[:, k, 0:T],
                                                            start=(k == 0), stop=(k == 7)), [hb[k], B("wA")], [psb])
            self.act(lambda e, j=j, ps=ps: e.activation(out=self.qlat[:, j, 0:T], in_=ps[:, 0:T], func=AF.Copy),
                     [psb], [B("qlat")])
            self.act(lambda e, j=j, ps=ps: e.activation(out=self.sq[:, j, 0:T], in_=ps[:, 0:T], func=AF.Square),
                     [psb], [B("sq")])
        for j in range(3):
            self.pe(lambda e, j=j, ps_ssq=ps_ssq: e.matmul(ps_ssq[:, 0:T], self.ones[:, :], self.sq[:, j, 0:T],
                                            start=(j == 0), stop=(j == 2)), [B("sq"), B("ones")], [psb_ssq])
        self.rstd_chain(ps_ssq[:, 0:T], psb_ssq, self.rstd[:, 0:T], B("rstd"), float(QR), self.rstd2[:, 0:T], B("rstd2"))
        for j in range(3):
            self.dve(lambda e, j=j: e.scalar_tensor_tensor(
                out=self.qn[:, j, 0:T], in0=self.qlat[:, j, 0:T], scalar=G[:, 32 + l * 3 + j:32 + l * 3 + j + 1],
                in1=self.rstd[:, 0:T], op0=ALU.mult, op1=ALU.mult), [B("qlat"), B("rstd"), B("gains")], [B("qn")])
        if kind == "meta" and l == 0:
            self.dbg("qlat", self.qlat[:, :, 0:16], [128, 3, 16], F32, [B("qlat")])
            self.dbg("rstdq", self.rstd[:, 0:16], [128, 16], F32, [B("rstd")])
            self.dbg("sqq", self.sq[:, 0:3, 0:16], [128, 3, 16], BF16, [B("sq")])
        ps_ssq, psb_ssq = self.psum()
        for j in range(2):
            ps, psb = self.psum()
            for k in range(8):
                self.pe(lambda e, k=k, j=j, ps=ps: e.matmul(ps[:, 0:T], win(k, 896 + j * 128, 128), hT[:, k, 0:T],
                                                            start=(k == 0), stop=(k == 7)), [hb[k], B("wA")], [psb])
            self.act(lambda e, j=j, ps=ps: e.activation(out=self.ckv[:, j, 0:T], in_=ps[:, 0:T], func=AF.Copy),
                     [psb], [B("ckv")])
            self.act(lambda e, j=j, ps=ps: e.activation(out=self.sq[:, 4 + j, 0:T], in_=ps[:, 0:T], func=AF.Square),
                     [psb], [B("sq")])
        for j in range(2):
            self.pe(lambda e, j=j, ps_ssq=ps_ssq: e.matmul(ps_ssq[:, 0:T], self.ones[:, :], self.sq[:, 4 + j, 0:T],
                                            start=(j == 0), stop=(j == 1)), [B("sq"), B("ones")], [psb_ssq])
        self.rstd_chain(ps_ssq[:, 0:T], psb_ssq, self.rstd[:, 0:T], B("rstd"), float(KVR), self.rstd2[:, 0:T], B("rstd2"))
        for j in range(2):
            self.dve(lambda e, j=j: e.scalar_tensor_tensor(
                out=self.ckv[:, j, 0:T], in0=self.ckv[:, j, 0:T], scalar=G[:, 38 + l * 2 + j:38 + l * 2 + j + 1],
                in1=self.rstd[:, 0:T], op0=ALU.mult, op1=ALU.mult), [B("rstd"), B("gains")], [B("ckv")])
        self.dve(lambda e: e.tensor_copy(out=self.ckvb[:, :, 0:T], in_=self.ckv[:, :, 0:T]), [B("ckv")], [B("ckvb")])
        ps, psb = self.psum()
        for j in range(2):
            for k in range(8):
                self.pe(lambda e, k=k, j=j, ps=ps: e.matmul(ps[0:32, j * T:(j + 1) * T], win(k, 1152 + 32 * j, 32),
                                                            hT[:, k, 0:T], start=(k == 0), stop=(k == 7)),
                        [hb[k], B("wA")], [psb])
        self.dve(lambda e, ps=ps: e.tensor_tensor(out=self.kpt[:, 0, 0:T], in0=ps[0:32, 0:T], in1=self.kC[:, 0:T],
                                                  op=ALU.mult), [psb, B("kC")], [B("kpt")])
        self.dve(lambda e, ps=ps: e.tensor_tensor(out=self.kpt[:, 1, 0:T], in0=ps[0:32, T:2 * T], in1=self.kS[:, 0:T],
                                                  op=ALU.mult), [psb, B("kS")], [B("kpt")])
        self.dve(lambda e: e.tensor_tensor(out=self.kper[:, 0:T], in0=self.kpt[:, 0, 0:T], in1=self.kpt[:, 1, 0:T],
                                            op=ALU.add), [B("kpt")], [B("kper")])

        def pool_section():
            if kind == "sample":
                for s in range(nseg):
                    self.dma(self.opool[0:15, :], self.spool[l, s, :, :], [], [B("opool")])
                    ps, psb = self.psum()
                    for g in range(4):
                        self.pe(lambda e, g=g, ps=ps: e.transpose(ps[:, g * 16:g * 16 + 15], self.opool[0:15, g * 128:(g + 1) * 128],
                                                                  self.ident[0:15, 0:15]), [B("opool"), B("ident")], [psb])
                    for g in range(4):
                        self.act(lambda e, g=g, s=s, ps=ps: e.activation(out=uview(g)[:, s, 0:15], in_=ps[:, g * 16:g * 16 + 15],
                                                                         func=AF.Copy), [psb], [B(f"uext{g}")])
            else:
                for g in range(4):
                    self.pool(lambda e, g=g: e.tensor_copy(out=uview(g)[:, 0, 0:15], in_=self.hist[l][:, g, :]),
                              [B(f"hist{l}")], [B(f"uext{g}")])
            for g in range(4):
                w = WINS[g]
                cur = uview(g)
                curb = B(f"uext{g}")
                sh = 1
                step = 0
                while sh < w:
                    lo = 2 * sh - 1
                    dst = self.ptmp[step % 2][:, 0:W4].rearrange("p (s l) -> p s l", s=nseg)
                    dstb = B(f"ptmp{step % 2}")
                    self.pool(lambda e, cur=cur, dst=dst, lo=lo, sh=sh: e.tensor_tensor(
                        out=dst[:, :, lo:L], in0=cur[:, :, lo:L], in1=cur[:, :, lo - sh:L - sh], op=ALU.add), [curb], [dstb])
                    cur, curb = dst, dstb
                    sh *= 2
                    step += 1
                dview = self.dT[:, g, 0:T].rearrange("p (s t) -> p s t", s=nseg)
                if kind == "meta":
                    self.pool(lambda e, cur=cur, g=g: e.tensor_tensor(out=cur[:, 0, 15:L], in0=cur[:, 0, 15:L],
                                                                       in1=self.rcnt[:, g, :], op=ALU.mult), [B("rcnt")], [curb])
                    self.pool(lambda e, cur=cur, g=g, dview=dview: e.tensor_tensor(
                        out=dview[:, :, :], in0=cur[:, :, 15:L], in1=uview(g)[:, :, 15:L], op=ALU.subtract),
                        [curb, B(f"uext{g}")], [B("dT")])
                else:
                    self.dve(lambda e, cur=cur, g=g, w=w, dview=dview: e.scalar_tensor_tensor(
                        out=dview[:, :, :], in0=cur[:, :, 15:L], scalar=1.0 / w, in1=uview(g)[:, :, 15:L],
                        op0=ALU.mult, op1=ALU.subtract), [curb, B(f"uext{g}")], [B("dT")])
                def pool_mm(g=g):
                    ps, psb = self.psum()
                    self.pe(lambda e, g=g, ps=ps: e.matmul(ps[:, 0:T], wA[:, O_WPOOL + g * 128:O_WPOOL + (g + 1) * 128],
                                                           self.dT[:, g, 0:T], start=True, stop=True), [B("dT"), B("wA")], [psb])
                    self.dve(lambda e, g=g, ps=ps: e.tensor_scalar(out=self.mixT[:, g, 0:T], in0=ps[:, 0:T],
                                                                   scalar1=G[:, 42 + l * 4 + g:42 + l * 4 + g + 1], scalar2=None,
                                                                   op0=ALU.mult), [psb, B("gains")], [B(f"mixT{g}")])
                self._deferred.append(pool_mm)
            if kind != "sample":
                for g in range(4):
                    self.pool(lambda e, g=g: e.tensor_copy(out=self.hist[l][:, g, :], in_=uview(g)[:, 0, L - 15:L]),
                              [B(f"uext{g}")], [B(f"hist{l}")])
            if kind == "sample" or (kind == "sup" and sup == self.NSUP - 1):
                for s in range(nseg):
                    ps, psb = self.psum()
                    for g in range(4):
                        self.pe(lambda e, g=g, s=s, ps=ps: e.transpose(ps[0:15, g * 128:(g + 1) * 128], uview(g)[:, s, L - 15:L],
                                                                       self.ident[:, :]), [B(f"uext{g}"), B("ident")], [psb])
                    self.act(lambda e, ps=ps: e.activation(out=self.opool[0:15, :], in_=ps[0:15, :], func=AF.Copy),
                             [psb], [B("opool")])
                    dst = self.pools[l, s, :, :] if kind == "sample" else self.poolp[l, :, :]
                    self.dma(dst, self.opool[0:15, :], [B("opool")], [B("pout")])

        self._deferred.append(pool_section)

        def wuq(j, h, sw):
            o = O_WUQ + (j * 8 + h) * 192 + sw * 96
            return wA[:, o:o + 96]

        v3 = lambda ap: ap[0:96, 0:2 * T].rearrange("p (a t) -> p a t", a=2)

        def q_mm(hp):
            a, b = (0, 1) if hp % 2 == 0 else (2, 3)
            psA, psAb, psB, psBb = self.ps[a], self.psb[a], self.ps[b], self.psb[b]
            for hh in range(2):
                h = hp * 2 + hh
                for j in range(3):
                    self.pe(lambda e, h=h, hh=hh, j=j, psA=psA: e.matmul(
                        psA[0:96, hh * T:(hh + 1) * T], wuq(j, h, 0), self.qn[:, j, 0:T], start=(j == 0), stop=(j == 2)),
                        [B("qn"), B("wA")], [psAb])
                for j in range(3):
                    self.pe(lambda e, h=h, hh=hh, j=j, psB=psB: e.matmul(
                        psB[0:96, hh * T:(hh + 1) * T], wuq(j, h, 1), self.qn[:, j, 0:T], start=(j == 0), stop=(j == 2)),
                        [B("qn"), B("wA")], [psBb])

        def q_chain(hp):
            a, b = (0, 1) if hp % 2 == 0 else (2, 3)
            psA, psAb, psB, psBb = self.ps[a], self.psb[a], self.ps[b], self.psb[b]
            tb = self.tq[0]
            tbb = B("tq0")
            self.dve(lambda e, psA=psA, tb=tb: e.tensor_tensor(out=tb[:, :, 0:T], in0=v3(psA), in1=self.C2[:, :, 0:T],
                                                               op=ALU.mult), [psAb, B("C2")], [tbb])
            self.dve(lambda e, psB=psB: e.tensor_tensor(out=self.t2[:, :, 0:T], in0=v3(psB), in1=self.S2[:, :, 0:T],
                                                        op=ALU.mult), [psBb, B("S2")], [B("t2")])
            self.dve(lambda e, tb=tb: e.tensor_tensor(out=tb[:, :, 0:T], in0=tb[:, :, 0:T], in1=self.t2[:, :, 0:T],
                                                      op=ALU.add), [B("t2")], [tbb])
            self.norm_heads(tb, tbb, self.qT, hp, T, 50 + l, B(f"qT{hp}"), 4, 0)

        k_mm, k_chain = self.kgen_fns(l, T, self.ckvb, B("ckvb"), self.kper, B("kper"), self.kT, "kT", (5, 6, 7), False)
        q_mm(0)
        k_mm(0)
        q_mm(1)
        k_mm(1)
        for p in range(4):
            q_chain(p)
            k_chain(p)
            if p + 2 < 4:
                q_mm(p + 2)
                k_mm(p + 2)

        if kind != "sample":
            for ti, (c0, n) in enumerate(tok_tiles):
                ps, psb = self.psum()
                for j in range(2):
                    self.pe(lambda e, j=j, c0=c0, n=n, ps=ps: e.matmul(ps[0:n, :], self.ckvb[:, j, c0:c0 + n],
                                                                       wA[:, O_WUV + j * 512:O_WUV + (j + 1) * 512],
                                                                       start=(j == 0), stop=(j == 1)), [B("ckvb"), B("wA")], [psb])
                self.vevac(ps, psb, n, lambda par, ti=ti, n=n: self.vcur[0:n, par::2, ti, par * 64:par * 64 + 64], B("vcur"))

        for ti, (c0, n) in enumerate(tok_tiles):
            s = ti % 2
            ps, psb = self.psum()
            for j in range(2):
                self.pe(lambda e, j=j, c0=c0, n=n, ps=ps: e.transpose(ps[0:n, j * 128:(j + 1) * 128], self.ckv[:, j, c0:c0 + n],
                                                                      self.ident[:, :]), [B("ckv"), B("ident")], [psb])
            self.pe(lambda e, c0=c0, n=n, ps=ps: e.transpose(ps[0:n, 256:288], self.kper[:, c0:c0 + n], self.ident[0:32, 0:32]),
                    [B("kper"), B("ident")], [psb])
            self.act(lambda e, n=n, s=s, ps=ps: e.activation(out=self.otok[0:n, s, :], in_=ps[0:n, 0:288], func=AF.Copy),
                     [psb], [B(f"otok{s}")])
            if kind == "sample":
                self.dma(self.lats[l, ti, :, :], self.otok[0:32, s, 0:KVR], [B(f"otok{s}")], [B("lout")])
                self.dma(self.krs[l, ti, :, :], self.otok[0:32, s, KVR:KVR + DR], [B(f"otok{s}")], [B("lout")])
            else:
                r0 = (0 if kind == "meta" else NMETA + sup * TS) + c0
                self.dma(self.latp[l, r0:r0 + n, :], self.otok[0:n, s, 0:KVR], [B(f"otok{s}")], [B("lout")])
                self.dma(self.krp[l, r0:r0 + n, :], self.otok[0:n, s, KVR:KVR + DR], [B(f"otok{s}")], [B("lout")])

    def vevac(self, ps, psb, n, dst_fn, dstb, on_dve=False):
        for par in range(2):
            src = ps[0:n, :].rearrange("p (h d) -> p h d", d=64)[:, par::2, :]
            if on_dve:
                self.dve(lambda e, src=src, par=par: e.tensor_copy(out=dst_fn(par), in_=src), [psb], [dstb])
            else:
                self.act(lambda e, src=src, par=par: e.activation(out=dst_fn(par), in_=src, func=AF.Copy), [psb], [dstb])

    def norm_heads(self, src, srcb, dst, hp, T, gcol, dstb, bank, side):
        B = self.B
        sl = 4 if side == 0 else 6
        sqb = B(f"sqh{side}")
        rs, rsb = (self.rstd, B("rstd")) if side == 0 else (self.rstd2, B("rstd2"))
        self.act(lambda e: e.activation(out=self.sq[0:96, sl:sl + 2, 0:T], in_=src[:, :, 0:T], func=AF.Square), [srcb], [sqb, B("sq")])
        ps, psb = self.ps[bank], self.psb[bank]
        for hh in range(2):
            self.pe(lambda e, hh=hh, ps=ps: e.matmul(ps[0:96, hh * T:(hh + 1) * T], self.ones[0:96, 0:96],
                                                     self.sq[0:96, sl + hh, 0:T], start=True, stop=True),
                    [sqb, B("ones")], [psb])
        self.act(lambda e, ps=ps: e.activation(out=rs[0:96, 0:2 * T], in_=ps[0:96, 0:2 * T], func=AF.Ln, scale=1.0 / DQK,
                                               bias=self.epsb[0:96, 0:1]), [psb, B("epsb")], [rsb])
        self.act(lambda e: e.activation(out=rs[0:96, 0:2 * T], in_=rs[0:96, 0:2 * T], func=AF.Exp, scale=-0.5), [], [rsb])
        self.dve(lambda e: e.scalar_tensor_tensor(
            out=dst[:, 2 * hp:2 * hp + 2, 0:T], in0=src[:, :, 0:T], scalar=self.gains[0:96, gcol:gcol + 1],
            in1=rs[0:96, 0:2 * T].rearrange("p (a t) -> p a t", a=2), op0=ALU.mult, op1=ALU.mult),
            [srcb, rsb, B("gains")], [dstb])

    def kgen_fns(self, l, T, latb_t, latb, kper_t, kperb, dst, dstname, banks, on_dve):
        B = self.B
        wA = self.wA

        def k_mm(hp):
            bk = banks[hp % 2]
            ps, psb = self.ps[bk], self.psb[bk]
            for hh in range(2):
                h = hp * 2 + hh
                for j in range(2):
                    o = O_WUK + (j * 8 + h) * 64
                    self.pe(lambda e, hh=hh, j=j, o=o, ps=ps: e.matmul(ps[0:64, hh * T:(hh + 1) * T], wA[:, o:o + 64],
                                                                       latb_t[:, j, 0:T], start=(j == 0), stop=(j == 1)),
                            [latb, B("wA")], [psb])

        def k_chain(hp):
            bk = banks[hp % 2]
            ps, psb = self.ps[bk], self.psb[bk]
            kr = self.tq[1]
            krb = B("tq1")
            src = ps[0:64, 0:2 * T].rearrange("p (a t) -> p a t", a=2)
            if on_dve:
                self.dve(lambda e, kr=kr, src=src: e.tensor_copy(out=kr[0:64, :, 0:T], in_=src), [psb], [krb])
            else:
                self.act(lambda e, kr=kr, src=src: e.activation(out=kr[0:64, :, 0:T], in_=src, func=AF.Copy), [psb], [krb])
            self.norm_heads(kr, krb, dst, hp, T, 52 + l, B(f"{dstname}{hp}"), banks[2], 1)

        for hh in range(2):
            self.dve(lambda e, hh=hh: e.tensor_copy(out=self.tq[1][64:96, hh, 0:T], in_=kper_t[0:32, 0:T]),
                     [kperb], [B("tq1")])
        return k_mm, k_chain

    def kgen(self, l, T, latb_t, latb, kper_t, kperb, dst, dstname):
        k_mm, k_chain = self.kgen_fns(l, T, latb_t, latb, kper_t, kperb, dst, dstname, (0, 1, 2), True)
        k_mm(0)
        k_mm(1)
        k_chain(0)
        k_mm(2)
        k_chain(1)
        k_mm(3)
        k_chain(2)
        k_chain(3)

    def next_pT(self):
        i = self._pT % 4
        self._pT += 1
        return self.pT[i], self.B(f"pT{i}")

    def attn_finish(self, h, T, po, pob):
        B = self.B
        par = h % 2
        lo, hi = par * 64, par * 64 + 64
        olo, ohi = (1 - par) * 64, (1 - par) * 64 + 64
        i = self._o % 2
        self._o += 1
        rl, rlb = self.rl[i], B(f"rl{i}")
        self.act(lambda e: e.activation(out=rl[olo:ohi, 0:T], in_=po[olo:ohi, 0:T], func=AF.Ln), [pob], [rlb])
        self.act(lambda e: e.activation(out=rl[olo:ohi, 0:T], in_=rl[olo:ohi, 0:T], func=AF.Exp, scale=-1.0), [], [rlb])
        self.dve(lambda e: e.tensor_tensor(out=self.mixT[lo:hi, 4 + h // 2, 0:T], in0=po[lo:hi, 0:T], in1=rl[olo:ohi, 0:T],
                                           op=ALU.mult), [pob, rlb], [B(f"mixT{4 + h // 2}")])

    def attn_prefetch(self, l, kind, sup):
        B = self.B
        nctx_tiles = 0 if kind == "meta" else 1 + sup * (TS // 128)
        NBUF = len(self.kblk)
        items = []
        for h in range(H):
            t0 = 0
            while t0 < nctx_tiles:
                nt = min(8, nctx_tiles - t0)
                items.append((h, t0, nt))
                t0 += nt
        loaded = {}

        def load(i):
            h, t0, nt = items[i]
            bi = self._blk % NBUF
            self._blk += 1
            kb, vb = self.kblk[bi], self.vblk[bi]
            kbb, vbb = B(f"kblk{bi}"), B(f"vblk{bi}")
            deps = [B(f"kc{l}_{t}") for t in range(t0, t0 + nt)]
            vdeps = [B(f"vc{l}_{t}") for t in range(t0, t0 + nt)]
            self.dma(kb[:, 0:nt * 128], self.kc[l, h, :, t0 * 128:(t0 + nt) * 128], deps, [kbb])
            self.dma(vb[:, 0:nt, :], self.vc[l, h, :, t0:t0 + nt, :], vdeps, [vbb])
            loaded[i] = (kb, vb, kbb, vbb)

        nxt = 0
        for i in range(min(NBUF - 1, len(items))):
            load(i)
            nxt = i + 1
        self._att = (items, loaded, load, nxt)

    def attn_prompt(self, l, T, kind, sup):
        B = self.B
        NBUF = len(self.kblk)
        PF = NBUF - 1
        items, loaded, load, nxt = self._att
        pending = []
        cur_blk = [None]

        def try_prefetch(i):
            nonlocal nxt
            while nxt < len(items) and nxt <= i + PF and (nxt - NBUF) not in {b for _, b in pending}:
                load(nxt)
                nxt += 1

        ii = 0
        for h in range(H):
            po, pob = self.ps[6 + h % 2], self.psb[6 + h % 2]
            qh = self.qT[:, h, 0:T]
            qb = B(f"qT{h // 2}")
            hblocks = [i for i, it in enumerate(items) if it[0] == h]
            nsteps = sum((items[i][2] + 1) // 2 for i in hblocks) + 1
            si = 0

            def do_step(grp, si):
                ps, psb = self.psum()
                pT, pTb = self.next_pT()
                for gi, (k_ap, kbuf, v_ap, vbuf, nk, m) in enumerate(grp):
                    self.pe(lambda e, k_ap=k_ap, gi=gi, nk=nk, ps=ps, qh=qh: e.matmul(ps[0:nk, gi * T:(gi + 1) * T], k_ap, qh,
                                                                                     start=True, stop=True), [kbuf, qb], [psb])
                nkmax = max(g[4] for g in grp)
                w = len(grp) * T
                if len(grp) == 2 and grp[0][4] != grp[1][4]:
                    for gi, g in enumerate(grp):
                        self.act(lambda e, gi=gi, nk=g[4], ps=ps, pT=pT: e.activation(
                            out=pT[0:nk, gi * T:(gi + 1) * T], in_=ps[0:nk, gi * T:(gi + 1) * T], func=AF.Exp, scale=SCALE),
                            [psb], [pTb])
                else:
                    self.act(lambda e, nk=nkmax, w=w, ps=ps, pT=pT: e.activation(out=pT[0:nk, 0:w], in_=ps[0:nk, 0:w],
                                                                                 func=AF.Exp, scale=SCALE), [psb], [pTb])
                for gi, (k_ap, kbuf, v_ap, vbuf, nk, m) in enumerate(grp):
                    if m == 0:
                        self.pool(lambda e, pT=pT: e.memset(pT[64:128, 0:64], 0.0), [], [pTb])
                    elif m == 1:
                        self.pool(lambda e, pT=pT: e.memset(pT[0:64, T:T + 128], 0.0), [], [pTb])
                        self.pool(lambda e, pT=pT: e.memset(pT[64:128, T:T + 192], 0.0), [], [pTb])
                prev = []
                while len(pending) > 1:
                    prev.extend(pending.pop(0)[0])

                def pv(grp=grp, si=si, pT=pT, pTb=pTb, po=po, pob=pob, nsteps=nsteps):
                    for gi, (k_ap, kbuf, v_ap, vbuf, nk, m) in enumerate(grp):
                        st = (si == 0 and gi == 0)
                        sp_ = (si == nsteps - 1 and gi == len(grp) - 1)
                        self.pe(lambda e, v_ap=v_ap, gi=gi, nk=nk, pT=pT, st=st, sp_=sp_, po=po: e.matmul(
                            po[:, 0:T], v_ap, pT[0:nk, gi * T:(gi + 1) * T], start=st, stop=sp_), [vbuf, pTb], [pob])
                for f in prev:
                    f()
                pending.append(([pv], cur_blk[0]))

            for i in hblocks:
                _, t0, nt = items[i]
                cur_blk[0] = i
                try_prefetch(i)
                kb, vb, kbb, vbb = loaded.pop(i)
                for tt in range(0, nt, 2):
                    grp = []
                    for t in range(tt, min(tt + 2, nt)):
                        nk = NMETA if (t0 + t) == 0 else 128
                        grp.append((kb[:, t * 128:t * 128 + nk], kbb, vb[0:nk, t, :], vbb, nk, None))
                    do_step(grp, si)
                    si += 1
                    try_prefetch(i)
            if kind == "meta":
                grp = [(self.kT[:, h, 0:16], B(f"kT{h // 2}"), self.vcur[0:16, h, 0, :], B("vcur"), 16, None)]
            else:
                grp = [(self.kT[:, h, 0:128], B(f"kT{h // 2}"), self.vcur[:, h, 0, :], B("vcur"), 128, 0),
                       (self.kT[:, h, 128:256], B(f"kT{h // 2}"), self.vcur[:, h, 1, :], B("vcur"), 128, 1)]
            cur_blk[0] = None
            do_step(grp, si)
            si += 1
            assert si == nsteps
            pending[-1][0].append(lambda h=h, po=po, pob=pob: self.attn_finish(h, T, po, pob))
        for grp_, _ in pending:
            for f in grp_:
                f()

    def attn_sample(self, l):
        B = self.B
        T = 128
        NR = self.NROWS
        wA = self.wA
        for s in range(4):
            po, pob = self.ps[6 + s % 2], self.psb[6 + s % 2]
            blocks = []
            r = 0
            while r < NR:
                n = min(256, NR - r)
                blocks.append((r, n))
                r += n
            nb = len(blocks)
            for bi, (r0, n) in enumerate(blocks + [("new", 32)]):
                if r0 == "new":
                    tiles = [(0, 32)]
                    ps, psb = self.psum()
                    for j in range(2):
                        self.pe(lambda e, j=j, ps=ps, s=s: e.matmul(ps[0:32, :], self.ckvb[:, j, s * 32:(s + 1) * 32],
                                                               wA[:, O_WUV + j * 512:O_WUV + (j + 1) * 512],
                                                               start=(j == 0), stop=(j == 1)), [B("ckvb"), B("wA")], [psb])
                    self.vevac(ps, psb, 32, lambda par: self.vsm[0:32, par::2, par * 64:par * 64 + 64], B("vsm"))
                    kt_fn = lambda h, c0, nk, s=s: self.kT[:, h, s * 32:s * 32 + 32]
                    ktb = [B(f"kT{hp}") for hp in range(4)]
                    v_fn = lambda h, ti, nk: self.vsm[0:32, h, :]
                    vb_ = B("vsm")
                else:
                    tiles = [(c0, min(128, n - c0)) for c0 in range(0, n, 128)]
                    for ti, (c0, nk) in enumerate(tiles):
                        self.dma(self.clin[0:nk, ti, :], self.clat[l, s, r0 + c0:r0 + c0 + nk, :], [], [B(f"clin{ti}")])
                        self.dma(self.ckin[0:nk, ti, :], self.ckr[l, s, r0 + c0:r0 + c0 + nk, :], [], [B(f"ckin{ti}")])
                        import os
                        AS = int(os.environ.get("MK_AS", 99))
                        if AS <= -1:
                            continue
                        ps, psb = self.psum()
                        for j in range(2):
                            self.pe(lambda e, j=j, ti=ti, nk=nk, ps=ps: e.transpose(
                                ps[:, j * 128:j * 128 + nk], self.clin[0:nk, ti, j * 128:(j + 1) * 128], self.ident[0:nk, 0:nk]),
                                [B(f"clin{ti}"), B("ident")], [psb])
                        self.pe(lambda e, ti=ti, nk=nk, ps=ps: e.transpose(ps[0:32, 256:256 + nk], self.ckin[0:nk, ti, :],
                                                                           self.ident[0:nk, 0:nk]), [B(f"ckin{ti}"), B("ident")], [psb])
                        if AS <= 0:
                            continue
                        self.dve(lambda e, c0=c0, nk=nk, ps=ps: e.tensor_copy(
                            out=self.ckvb2[:, :, c0:c0 + nk], in_=ps[:, 0:256].rearrange("p (j t) -> p j t", j=2)[:, :, 0:nk]),
                            [psb], [B("ckvb2")])
                        if os.environ.get("MK_NODVE"):
                            continue
                        self.dve(lambda e, c0=c0, nk=nk, ps=ps: e.tensor_copy(out=self.kper2[:, c0:c0 + nk],
                                                                             in_=ps[0:32, 256:256 + nk]), [psb], [B("kper2")])
                    import os
                    AS = int(os.environ.get("MK_AS", 99))
                    if AS <= 1:
                        continue
                    self.kgen(l, n, self.ckvb2, B("ckvb2"), self.kper2, B("kper2"), self.kT2, "kT2")
                    if AS <= 2:
                        continue
                    for ti, (c0, nk) in enumerate(tiles):
                        ps, psb = self.psum()
                        for j in range(2):
                            self.pe(lambda e, j=j, c0=c0, nk=nk, ps=ps: e.matmul(
                                ps[0:nk, :], self.ckvb2[:, j, c0:c0 + nk], wA[:, O_WUV + j * 512:O_WUV + (j + 1) * 512],
                                start=(j == 0), stop=(j == 1)), [B("ckvb2"), B("wA")], [psb])
                        self.vevac(ps, psb, nk, lambda par, ti=ti, nk=nk: self.vcur2[0:nk, par::2, ti, par * 64:par * 64 + 64],
                                   B("vcur2"), on_dve=True)
                    kt_fn = lambda h, c0, nk: self.kT2[:, h, c0:c0 + nk]
                    ktb = [B(f"kT2{hp}") for hp in range(4)]
                    v_fn = lambda h, ti, nk: self.vcur2[0:nk, h, ti, :]
                    vb_ = B("vcur2")
                import os
                AS = int(os.environ.get("MK_AS", 99))
                if AS <= 3:
                    continue
                ps, psb = self.psum()
                pT, pTb = self.next_pT()
                nt = len(tiles)
                for h in range(H):
                    for ti, (c0, nk) in enumerate(tiles):
                        col = (h * nt + ti) * 32
                        self.pe(lambda e, h=h, c0=c0, nk=nk, col=col, ps=ps, kt_fn=kt_fn, s=s: e.matmul(
                            ps[0:nk, col:col + 32], kt_fn(h, c0, nk), self.qT[:, h, s * 32:s * 32 + 32], start=True, stop=True),
                            [ktb[h // 2], B(f"qT{h // 2}")], [psb])
                if nt == 2 and tiles[0][1] != tiles[1][1]:
                    raise NotImplementedError("ragged block")
                nk = tiles[0][1]
                w = H * nt * 32
                self.act(lambda e, nk=nk, w=w, ps=ps, pT=pT: e.activation(out=pT[0:nk, 0:w], in_=ps[0:nk, 0:w], func=AF.Exp,
                                                                          scale=SCALE), [psb], [pTb])
                if AS <= 4:
                    continue
                for h in range(H):
                    for ti, (c0, nk) in enumerate(tiles):
                        col = (h * nt + ti) * 32
                        st = (bi == 0 and ti == 0 and h == 0)
                        sp_ = (r0 == "new")
                        self.pe(lambda e, h=h, ti=ti, nk=nk, col=col, pT=pT, st=st, sp_=sp_, v_fn=v_fn, po=po: e.matmul(
                            po[:, h * 32:(h + 1) * 32], v_fn(h, ti, nk), pT[0:nk, col:col + 32], start=st, stop=sp_),
                            [vb_, pTb], [pob])
            if AS <= 5:
                continue
            i = self._o % 2
            self._o += 1
            rl, rlb = self.rl[i], B(f"rl{i}")
            self.act(lambda e, rl=rl, po=po: e.activation(out=rl[:, 0:256], in_=po[:, 0:256], func=AF.Ln), [pob], [rlb])
            self.act(lambda e, rl=rl: e.activation(out=rl[:, 0:256], in_=rl[:, 0:256], func=AF.Exp, scale=-1.0), [], [rlb])
            for h in range(H):
                par = h % 2
                lo, hi = par * 64, par * 64 + 64
                olo, ohi = (1 - par) * 64, (1 - par) * 64 + 64
                self.dve(lambda e, h=h, lo=lo, hi=hi, olo=olo, ohi=ohi, rl=rl, po=po, s=s: e.tensor_tensor(
                    out=self.mixT[lo:hi, 4 + h // 2, s * 32:(s + 1) * 32], in0=po[lo:hi, h * 32:(h + 1) * 32],
                    in1=rl[olo:ohi, h * 32:(h + 1) * 32], op=ALU.mult), [pob, rlb], [B(f"mixT{4 + h // 2}")])

    def store_kv(self, l, kind, sup):
        B = self.B
        if kind == "meta":
            self.dbg(f"kTmeta{l}", self.kT[:, :, 0:16], [96, 8, 16], BF16, [B(f"kT{hp}") for hp in range(4)])
            self.dma(self.kc[l, :, :, 0:16].rearrange("h p t -> p h t"), self.kT[:, :, 0:16],
                     [B(f"kT{hp}") for hp in range(4)], [B(f"kc{l}_0")])
            self.dma(self.vc[l, :, 0:16, 0, :].rearrange("h p d -> p h d"), self.vcur[0:16, :, 0, :], [B("vcur")], [B(f"vc{l}_0")])
        else:
            t0 = 1 + sup * 2
            c0 = t0 * 128
            self.dma(self.kc[l, :, :, c0:c0 + TS].rearrange("h p t -> p h t"), self.kT[:, :, 0:TS],
                     [B(f"kT{hp}") for hp in range(4)], [B(f"kc{l}_{t0}"), B(f"kc{l}_{t0 + 1}")])
            self.dma(self.vc[l, :, :, t0:t0 + 2, :].rearrange("h p t d -> p h t d"), self.vcur[:, :, :, :],
                     [B("vcur")], [B(f"vc{l}_{t0}"), B(f"vc{l}_{t0 + 1}")])

    def out_proj(self, l, T):
        B = self.B
        mb = [B(f"mixT{k}") for k in range(8)]
        for m in range(8):
            ps, psb = self.psum()
            for k in range(8):
                self.pe(lambda e, k=k, m=m, ps=ps: e.matmul(ps[:, 0:T], self.wO[:, k * 1024 + m * 128:k * 1024 + (m + 1) * 128],
                                                            self.mixT[:, k, 0:T], start=(k == 0), stop=(k == 7)),
                        [mb[k], B("wO")], [psb])
            self.dve(lambda e, m=m, ps=ps: e.tensor_tensor(out=self.xT[:, m, 0:T], in0=ps[:, 0:T], in1=self.xT[:, m, 0:T],
                                                           op=ALU.add), [psb], [B("xT")])

    def ffn(self, l, T):
        B = self.B
        hb = self.hT_bufs()
        for c in range(NDN):
            self.load_dn(l, c)
        self.norm_x(T, 16 + l * 8)

        def down(c):
            wd = self.wDN[c % NDN]
            wdb = B(f"wDN{c % NDN}")
            for m in range(8):
                bk = 4 + m // 2
                half = m % 2
                self.pe(lambda e, c=c, m=m, bk=bk, half=half, wd=wd: e.matmul(
                    self.ps[bk][:, half * T:(half + 1) * T], wd[:, m * 128:(m + 1) * 128], self.aT[:, c, 0:T],
                    start=(c == 0 and half == 0), stop=(c == NFF - 1), skip_group_check=True),
                    [B(f"aT{c}"), wdb], [self.psb[bk]])
            if c + NDN < NFF:
                self.load_dn(l, c + NDN)

        for cc in range(NFF):
            wg = self.wGU[cc % NGU]
            wgb = B(f"wGU{cc % NGU}")
            bi = self._ps4 % 4
            self._ps4 += 1
            ps, psb = self.ps[bi], self.psb[bi]
            for gu in range(2):
                for k in range(8):
                    o = (k * 2 + gu) * 128
                    self.pe(lambda e, k=k, gu=gu, o=o, ps=ps, wg=wg: e.matmul(ps[:, gu * T:(gu + 1) * T], wg[:, o:o + 128],
                                                                              self.hT[:, k, 0:T], start=(k == 0), stop=(k == 7)),
                            [hb[k], wgb], [psb])
            i = self._sg % 2
            self._sg += 1
            sg, sgb = self.sg[i], B(f"sg{i}")
            self.act(lambda e, ps=ps, sg=sg: e.activation(out=sg[:, 0:T], in_=ps[:, 0:T], func=AF.Silu), [psb], [sgb])
            self.dve(lambda e, ps=ps, sg=sg, cc=cc: e.tensor_tensor(out=self.aT[:, cc, 0:T], in0=ps[:, T:2 * T],
                                                                    in1=sg[:, 0:T], op=ALU.mult), [psb, sgb], [B(f"aT{cc}")])
            if cc + NGU < NFF:
                self.load_gu(l, cc + NGU)
            if cc >= 1:
                down(cc - 1)
        down(NFF - 1)
        for m in range(8):
            bk = 4 + m // 2
            half = m % 2
            self.dve(lambda e, m=m, bk=bk, half=half: e.tensor_tensor(
                out=self.xT[:, m, 0:T], in0=self.ps[bk][:, half * T:(half + 1) * T], in1=self.xT[:, m, 0:T], op=ALU.add),
                [self.psb[bk]], [B("xT")])

    def tile(self, kind, sup=0):
        if kind == "meta":
            T, segs, tok_tiles, pos0 = 16, (1, 16), [(0, 16)], 0
            src = [(self.meta[:, :], 16)]
        elif kind == "sup":
            T, segs, tok_tiles = TS, (1, TS), [(0, 128), (128, 128)]
            pos0 = NMETA + sup * TS
            src = [(self.xp[sup * TS + i * 128:sup * TS + (i + 1) * 128, :], 128) for i in range(2)]
        else:
            T, segs, tok_tiles, pos0 = 128, (4, 32), [(s * 32, 32) for s in range(4)], self.LP
            src = [(self.xs[:, :], 128)]
        self.load_x(src, T)
        if kind == "meta":
            self.dbg("xT", self.xT[:, :, 0:16], [128, 8, 16], F32, [self.B("xT")])
        if not getattr(self, "_wA_first", False):
            self._wA_first = True
            self.load_wA(0)
        for l in range(2):
            self.load_wO(l)
            import os
            SS = int(os.environ.get("MK_SS", 99)) if kind == "sample" else 99
            self.norm_x(T, l * 8)
            if SS <= 1:
                return
            if kind == "meta" and l == 0:
                self.dbg("hT", self.hT[:, :, 0:16], [128, 8, 16], BF16, self.hT_bufs())
                self.dbg("rstd", self.rstd[:, 0:16], [128, 16], F32, [self.B("rstd")])
                self.dbg("wA", self.wA[:, 0:2048], [128, 2048], BF16, [self.B("wA")])
            if kind != "sample":
                self.attn_prefetch(l, kind, sup)
            self.mixer_front(l, T, segs, tok_tiles, pos0, kind, sup)
            if SS <= 2:
                return
            for c in range(NGU):
                self.load_gu(l, c)
            if kind == "meta" and l == 0:
                B = self.B
                self.dbg("uext", self.uext[:, 0:4 * 31], [128, 4 * 31], F32, [B(f"uext{g}") for g in range(4)])
                self.dbg("qn", self.qn[:, :, 0:16], [128, 3, 16], BF16, [B("qn")])
                self.dbg("ckv", self.ckv[:, :, 0:16], [128, 2, 16], F32, [B("ckv")])
                self.dbg("kper", self.kper[:, 0:16], [32, 16], F32, [B("kper")])
                self.dbg("dT", self.dT[:, :, 0:16], [128, 4, 16], BF16, [B("dT")])
                self.dbg("qT", self.qT[:, :, 0:16], [96, 8, 16], BF16, [B(f"qT{i}") for i in range(4)])
                self.dbg("kT", self.kT[:, :, 0:16], [96, 8, 16], BF16, [B(f"kT{i}") for i in range(4)])
                self.dbg("vcur", self.vcur[0:16, :, 0, :], [16, 8, 128], BF16, [B("vcur")])
            if kind == "sample":
                self.attn_sample(l)
                if SS <= 3:
                    return
            else:
                self.attn_prompt(l, T, kind, sup)
                if not (kind == "sup" and sup == self.NSUP - 1):
                    self.store_kv(l, kind, sup)
            if kind == "meta" and l == 0:
                self.dbg("mixT", self.mixT[:, :, 0:16], [128, 8, 16], BF16, [self.B(f"mixT{i}") for i in range(8)])
            for f in self._deferred:
                f()
            self._deferred = []
            self.load_wA(1 - l)
            if kind == "sup" and sup == 0:
                self.dbg(f"mix{l}", self.mixT[:, :, 0:256], [128, 8, 256], BF16, [self.B(f"mixT{i}") for i in range(8)])
            self.out_proj(l, T)
            if kind == "sup" and sup == 0:
                self.dbg(f"xo{l}", self.xT[:, :, 0:256], [128, 8, 256], F32, [self.B("xT")])
            if kind == "meta" and l == 0:
                self.dbg("x1", self.xT[:, :, 0:16], [128, 8, 16], F32, [self.B("xT")])
            self.ffn(l, T)
            if kind == "meta" and l == 0:
                self.dbg("x2", self.xT[:, :, 0:16], [128, 8, 16], F32, [self.B("xT")])
        if kind == "sup" and sup == 0:
            self.dbg("xfin", self.xT[:, :, 0:256], [128, 8, 256], F32, [self.B("xT")])
        if kind == "sup":
            dst = [(self.yp[sup * TS + i * 128:sup * TS + (i + 1) * 128, :], 128) for i in range(2)]
            self.store_y(dst, T)
        elif kind == "sample":
            self.store_y([(self.ys[:, :], 128)], T)

    def build(self):
        nc = self.nc
        self.declare()
        self.ckvb2 = self.sb("ckvb2", [128, 2, 256], BF16)
        self.kper2 = self.sb("kper2", [32, 256], F32)
        self.kT2 = self.sb("kT2", [96, 8, 256], BF16)
        self.vcur2 = self.sb("vcur2", [128, 8, 2, 128], BF16)
        self.pool(lambda e: e.memset(self.vcur2[:], 1.0), [], [self.B("vcur2")])
        self.init_consts()
        self.cast_weights()
        import os
        stg = os.environ.get("MK_STAGES", "sample,meta,sup")
        if "sample" in stg:
            self.tile("sample")
        if "meta" in stg:
            self.tile("meta")
        if "sup" in stg:
            for s in range(int(os.environ.get("MK_NSUP", self.NSUP))):
                self.tile("sup", s)
        for l in range(2):
            self.dbg(f"kcend{l}", self.kc[l, 0, :, 0:128], [96, 128], BF16, [self.B(f"kc{l}_0")])
        import contextlib
        with contextlib.ExitStack() as st:
            eng_sems = {e: st.enter_context(nc.semaphore(f"sem_{e}")) for e in Prog.ENGS}
            dma_sems = [st.enter_context(nc.semaphore(f"dsem{i}")) for i in range(40)]
            block = st.enter_context(nc.Block())
            reg = {"pe": block.tensor, "act": block.scalar, "dve": block.vector, "pool": block.gpsimd, "sp": block.sync}
            self.P.emit(nc, reg, eng_sems, dma_sems)
        return nc


def _pack_weights(w_in, w_uq, w_uk, w_uv, w_pool, w_o, w_gate, w_up, w_down):
    blob = np.zeros((2, 128, WTOT), np.float32)
    for l in range(2):
        wi = np.concatenate([w_in[l], w_in[l][:, 1152 + 16:1152 + 32], w_in[l][:, 1152:1152 + 16]], axis=1)
        blob[l, :, O_WIN:O_WIN + 8 * WIN_W] = wi.reshape(8, 128, WIN_W).transpose(1, 0, 2).reshape(128, -1)
        uq = w_uq[l].reshape(3, 128, 8, 96)
        uq2 = np.zeros((3, 128, 8, 192), np.float32)
        uq2[..., 0:96] = uq
        uq2[..., 96 + 64:96 + 80] = uq[..., 80:96]
        uq2[..., 96 + 80:96 + 96] = uq[..., 64:80]
        blob[l, :, O_WUQ:O_WUQ + 3 * 8 * 192] = uq2.transpose(1, 0, 2, 3).reshape(128, -1)
        uk = w_uk[l].reshape(2, 128, 8, 64)
        blob[l, :, O_WUK:O_WUK + 2 * 8 * 64] = uk.transpose(1, 0, 2, 3).reshape(128, -1)
        blob[l, :, O_WUV:O_WUV + 1024] = w_uv[l].reshape(2, 128, 512).transpose(1, 0, 2).reshape(128, -1)
        blob[l, :, O_WPOOL:O_WPOOL + 512] = w_pool[l].transpose(1, 0, 2).reshape(128, -1)
        blob[l, :, O_WO:O_WO + 8192] = w_o[l].reshape(8, 128, 1024).transpose(1, 0, 2).reshape(128, -1)
        g = w_gate[l].reshape(8, 128, NFF, 128)
        u = w_up[l].reshape(8, 128, NFF, 128)
        gu = np.stack([g, u], axis=3)
        blob[l, :, O_GU:O_GU + NFF * GU_PIECE] = gu.transpose(1, 2, 0, 3, 4).reshape(128, -1)
        dn = w_down[l].reshape(NFF, 128, 1024)
        blob[l, :, O_DN:O_DN + NFF * DN_PIECE] = dn.transpose(1, 0, 2).reshape(128, -1)
    return blob


def _pack_gains(norm_mix, norm_ffn, q_a_norm, kv_a_norm, pool_scale, q_norm, k_norm):
    g = np.zeros((128, NG), np.float32)
    g[:, 0:16] = norm_mix.reshape(2, 8, 128).transpose(2, 0, 1).reshape(128, 16)
    g[:, 16:32] = norm_ffn.reshape(2, 8, 128).transpose(2, 0, 1).reshape(128, 16)
    g[:, 32:38] = q_a_norm.reshape(2, 3, 128).transpose(2, 0, 1).reshape(128, 6)
    g[:, 38:42] = kv_a_norm.reshape(2, 2, 128).transpose(2, 0, 1).reshape(128, 4)
    g[:, 42:50] = pool_scale.reshape(2, 4, 128).transpose(2, 0, 1).reshape(128, 8)
    g[0:96, 50:52] = q_norm.T
    g[0:96, 52:54] = k_norm.T
    return g


def _rope_table(LP, pos_sample):
    half = DR // 2
    inv = (np.float32(10000.0) ** (-np.arange(half, dtype=np.float32) / np.float32(half))).astype(np.float32)
    pos = np.concatenate([np.arange(LP, dtype=np.float32), np.tile(pos_sample.astype(np.float32), 4)])
    ang = (pos[:, None] * inv[None, :]).astype(np.float32)
    cos = np.cos(ang).astype(np.float32).T
    sin = np.sin(ang).astype(np.float32).T
    tab = np.zeros((96, 2, pos.shape[0]), np.float32)
    tab[0:64, 0] = 1.0
    tab[64:80, 0] = cos
    tab[80:96, 0] = cos
    tab[64:80, 1] = -sin
    tab[80:96, 1] = sin
    return tab


_CACHE = {}


def kernel(x_prompt, x_sample, cache_latent, cache_krope, state_pool, meta_tokens,
           norm_mix, w_in, q_a_norm, w_uq, kv_a_norm, w_uk, w_uv, q_norm, k_norm,
           w_pool, pool_scale, w_o, norm_ffn, w_gate, w_up, w_down):
    f = lambda a: np.ascontiguousarray(np.asarray(a), dtype=np.float32)
    x_prompt, x_sample, cache_latent, cache_krope, state_pool, meta_tokens = map(
        f, (x_prompt, x_sample, cache_latent, cache_krope, state_pool, meta_tokens))
    NB, SEQ, _ = x_prompt.shape
    NROWS = cache_latent.shape[2]
    n = 8
    assert NB == n and x_sample.shape[0] == 4 * n and x_sample.shape[1] == 32 and SEQ % TS == 0
    LP = NMETA + SEQ
    key = (SEQ, NROWS)
    if key not in _CACHE:
        nc = bass.Bass("TRN2", target_bir_lowering=False)
        _CACHE[key] = Builder(SEQ, NROWS, nc).build()
    nc = _CACHE[key]
    blob = _pack_weights(*map(f, (w_in, w_uq, w_uk, w_uv, w_pool, w_o, w_gate, w_up, w_down)))
    gains = _pack_gains(*map(f, (norm_mix, norm_ffn, q_a_norm, kv_a_norm, pool_scale, q_norm, k_norm)))
    rope = _rope_table(LP, NROWS + np.arange(32))
    rcnt = np.zeros((128, 4, 16), np.float32)
    for g, w in enumerate(WINS):
        rcnt[:, g, :] = 1.0 / np.minimum(np.arange(16) + 1, w).astype(np.float32)
    ident = np.eye(128, dtype=np.float32)
    in_maps = []
    for c in range(n):
        in_maps.append({
            "xp": x_prompt[c], "meta": meta_tokens, "xs": x_sample[4 * c:4 * c + 4].reshape(128, D),
            "clat": np.ascontiguousarray(cache_latent[:, 4 * c:4 * c + 4]),
            "ckr": np.ascontiguousarray(cache_krope[:, 4 * c:4 * c + 4]),
            "spool": np.ascontiguousarray(state_pool[:, 4 * c:4 * c + 4]),
            "wblob": blob, "gains": gains, "rope": rope, "rcnt": rcnt, "ident": ident,
        })
    res = run_bass_kernel_spmd(nc, in_maps, core_ids=list(range(n)))
    R = res.results
    kernel.last = R
    y_prompt = np.stack([R[c]["yp"] for c in range(n)], axis=0)
    y_sample = np.concatenate([R[c]["ys"].reshape(4, 32, D) for c in range(n)], axis=0)
    lat_p = np.stack([R[c]["latp"] for c in range(n)], axis=1)
    kr_p = np.stack([R[c]["krp"] for c in range(n)], axis=1)
    pool_p = np.stack([R[c]["poolp"] for c in range(n)], axis=1)
    lat_s = np.concatenate([R[c]["lats"] for c in range(n)], axis=1)
    kr_s = np.concatenate([R[c]["krs"] for c in range(n)], axis=1)
    pool_s = np.concatenate([R[c]["pools"] for c in range(n)], axis=1)
    return (y_prompt, y_sample, lat_p, kr_p, pool_p, lat_s, kr_s, pool_s)
```

```python
import numpy as np
import concourse.bass as bass
import concourse.mybir as mybir
from concourse.bass_utils import run_bass_kernel_spmd

F32 = mybir.dt.float32
BF16 = mybir.dt.bfloat16
AF = mybir.ActivationFunctionType
ALU = mybir.AluOpType

D = 1024
NCH = 8
H = 8
DQK = 96
QR = 384
KVR = 256
DR = 32
DFF = 2816
NFF = 22
NMETA = 16
TS = 256
EPS = 1e-6
WINS = (2, 4, 8, 16)
SCALE = DQK ** -0.5

O_WIN = 0
WIN_W = 1216
O_WUQ = O_WIN + 8 * WIN_W
O_WUK = O_WUQ + 3 * 8 * 192
O_WUV = O_WUK + 2 * 8 * 64
O_WPOOL = O_WUV + 2 * 512
WA_TOT = O_WPOOL + 4 * 128
O_WO = WA_TOT
O_GU = O_WO + 8 * 1024
GU_PIECE = 8 * 2 * 128
NGU = 4
O_DN = O_GU + NFF * GU_PIECE
DN_PIECE = 1024
NDN = 4
NAT = 10
WTOT = O_DN + NFF * DN_PIECE
CAST_PIECE = 724
assert WTOT % CAST_PIECE == 0
NG = 54


class Buf:
    __slots__ = ("name", "w", "r_eng", "r_dma", "excl")

    def __init__(self, name):
        self.name = name
        self.excl = name.startswith("ps")
        self.w = None
        self.r_eng = {}
        self.r_dma = []


class Op:
    __slots__ = ("eng", "is_dma", "fn", "deps", "signal", "sem", "val")


class Prog:
    ENGS = ("pe", "act", "dve", "pool", "sp")

    def __init__(self):
        self.ops = []

    def add(self, eng, fn, reads=(), writes=(), dma=False):
        op = Op()
        op.eng = eng
        op.is_dma = dma
        op.fn = fn
        op.signal = False
        op.sem = None
        op.val = 0
        deps = set()
        for b in reads:
            if b.w is not None:
                deps.add(b.w)
            if b.excl:
                deps.update(o for e2, o in b.r_eng.items() if e2 != eng)
        for b in writes:
            if b.w is not None:
                deps.add(b.w)
            deps.update(b.r_eng.values())
            deps.update(b.r_dma)
        deps.discard(op)
        if eng == "pe" and not dma:
            deps = [d for d in deps if d.is_dma or d.eng != "pe"]
        op.deps = list(deps)
        wset = set(id(b) for b in writes)
        for b in writes:
            b.w = op
            b.r_eng = {}
            b.r_dma = []
        for b in reads:
            if id(b) in wset:
                continue
            if dma:
                b.r_dma.append(op)
            else:
                b.r_eng[eng] = op
        self.ops.append(op)
        return op

    def emit(self, nc, block_engines, eng_sems, dma_sems):
        ops = self.ops
        npool = len(dma_sems)
        last_on_sem = [None] * npool
        cnt_on_sem = [0] * npool
        k = 0
        for op in ops:
            if op.is_dma:
                i = k % npool
                k += 1
                cnt_on_sem[i] += 1
                op.sem = dma_sems[i]
                op.val = 16 * cnt_on_sem[i]
                if last_on_sem[i] is not None:
                    op.deps.append(last_on_sem[i])
                last_on_sem[i] = op
        for op in ops:
            for d in op.deps:
                d.signal = True
        cnt = {e: 0 for e in self.ENGS}
        for op in ops:
            if not op.is_dma and op.signal:
                cnt[op.eng] += 1
                op.sem = eng_sems[op.eng]
                op.val = cnt[op.eng]
        self.final_dma = [(dma_sems[i], 16 * cnt_on_sem[i]) for i in range(npool) if cnt_on_sem[i]]
        self.counts = cnt
        by_eng = {e: [] for e in self.ENGS}
        for op in ops:
            by_eng[op.eng].append(op)

        def make_stream(eng):
            def stream(e):
                known = {}
                for op in by_eng[eng]:
                    for d in op.deps:
                        key = id(d.sem)
                        if known.get(key, 0) < d.val:
                            e.wait_ge(d.sem, d.val)
                            known[key] = d.val
                    ins = op.fn(e)
                    if op.is_dma:
                        ins.then_inc(op.sem, 16)
                    elif op.signal:
                        ins.then_inc(op.sem, 1)
                if eng == "sp":
                    for sem, v in self.final_dma:
                        if known.get(id(sem), 0) < v:
                            e.wait_ge(sem, v)
            return stream

        for eng in self.ENGS:
            if by_eng[eng] or eng == "sp":
                block_engines[eng](make_stream(eng))


class Builder:
    def __init__(self, SEQ, NROWS, nc):
        self.nc = nc
        self.P = Prog()
        self.SEQ = SEQ
        self.NROWS = NROWS
        self.NSUP = SEQ // TS
        self.LP = NMETA + SEQ
        self.bufs = {}
        self._psrot = 0
        self._ps8 = 0
        self._ps4 = 0
        self._deferred = []

    def B(self, name):
        b = self.bufs.get(name)
        if b is None:
            b = Buf(name)
            self.bufs[name] = b
        return b

    def sb(self, name, shape, dt):
        return self.nc.alloc_sbuf_tensor(name, list(shape), dt)

    def pe(self, fn, r, w):
        return self.P.add("pe", fn, r, w)

    def act(self, fn, r, w):
        return self.P.add("act", fn, r, w)

    def dve(self, fn, r, w):
        return self.P.add("dve", fn, r, w)

    def pool(self, fn, r, w):
        return self.P.add("pool", fn, r, w)

    def dma(self, out, in_, r, w, eng="sp"):
        return self.P.add(eng, lambda e: e.dma_start(out=out, in_=in_), r, w, dma=True)

    def dbg(self, name, ap, shape, dt, bufs):
        import os
        if not os.environ.get("MK_DEBUG"):
            return
        t = self.nc.dram_tensor("dbg_" + name, list(shape), dt, kind="ExternalOutput").ap()
        self.dma(t, ap, bufs, [self.B("dbg_" + name)])

    def psum8(self):
        i = self._ps8 % 8
        self._ps8 += 1
        return self.ps[i], self.psb[i]

    def psum(self):
        i = self._psrot % 6
        self._psrot += 1
        return self.ps[i], self.psb[i]

    def declare(self):
        nc = self.nc
        SEQ, NROWS, LP = self.SEQ, self.NROWS, self.LP
        di = lambda n, s: nc.dram_tensor(n, list(s), F32, kind="ExternalInput").ap()
        do = lambda n, s: nc.dram_tensor(n, list(s), F32, kind="ExternalOutput").ap()
        self.xp = di("xp", [SEQ, D])
        self.meta = di("meta", [NMETA, D])
        self.xs = di("xs", [128, D])
        self.clat = di("clat", [2, 4, NROWS, KVR])
        self.ckr = di("ckr", [2, 4, NROWS, DR])
        self.spool = di("spool", [2, 4, 15, 512])
        self.wblob = di("wblob", [2, 128, WTOT])
        self.gains_d = di("gains", [128, NG])
        self.NPOS = LP + 128
        self.rope_d = di("rope", [96, 2, self.NPOS])
        self.rcnt_d = di("rcnt", [128, 4, 16])
        self.ident_d = di("ident", [128, 128])
        self.yp = do("yp", [SEQ, D])
        self.ys = do("ys", [128, D])
        self.latp = do("latp", [2, LP, KVR])
        self.krp = do("krp", [2, LP, DR])
        self.poolp = do("poolp", [2, 15, 512])
        self.lats = do("lats", [2, 4, 32, KVR])
        self.krs = do("krs", [2, 4, 32, DR])
        self.pools = do("pools", [2, 4, 15, 512])
        self.wsc = nc.dram_tensor("wsc", [2, 128, WTOT], BF16, kind="Internal").ap()
        self.NKT = 1 + SEQ // 128
        self.kc = nc.dram_tensor("kc", [2, H, 96, self.NKT * 128], BF16, kind="Internal").ap()
        self.vc = nc.dram_tensor("vc", [2, H, 128, self.NKT, 128], BF16, kind="Internal").ap()

        self.ident = self.sb("identb", [128, 128], F32)
        self.ones = self.sb("ones", [128, 128], BF16)
        self.epsb = self.sb("epsb", [128, 1], F32)
        self.gains = self.sb("gainsb", [128, NG], F32)
        self.rcnt = self.sb("rcntb", [128, 4, 16], F32)
        self.ps = [nc.alloc_psum_tensor(f"ps{i}", [128, 512], F32) for i in range(8)]
        self.psb = [self.B(f"ps{i}") for i in range(8)]
        self.cstf = [self.sb(f"cstf{i}", [128, CAST_PIECE], F32) for i in range(2)]
        self.wA = self.sb("wA", [128, WA_TOT], BF16)
        self.wO = self.sb("wO", [128, 8 * 1024], BF16)
        self.wGU = [self.sb(f"wGU{i}", [128, GU_PIECE], BF16) for i in range(NGU)]
        self.wDN = [self.sb(f"wDN{i}", [128, DN_PIECE], BF16) for i in range(NDN)]
        T = TS
        self.xT = self.sb("xT", [128, 8, T], F32)
        self.hT = self.sb("hT", [128, 8, T], BF16)
        self.sq = self.sb("sq", [128, 8, T], BF16)
        self.rstd = self.sb("rstd", [128, 512], F32)
        self.rstd2 = self.sb("rstd2", [128, 512], F32)
        self.xin = self.sb("xin", [128, 2, D], F32)
        self.uext = self.sb("uext", [128, 4 * (15 + T + 60)], F32)
        self.ptmp = [self.sb(f"ptmp{i}", [128, 15 + T + 60], F32) for i in range(2)]
        self.hist = [self.sb(f"hist{l}", [128, 4, 15], F32) for l in range(2)]
        self.dT = self.sb("dT", [128, 4, T], BF16)
        self.qlat = self.sb("qlat", [128, 3, T], F32)
        self.qn = self.sb("qn", [128, 3, T], BF16)
        self.ckv = self.sb("ckv", [128, 2, T], F32)
        self.ckvb = self.sb("ckvb", [128, 2, T], BF16)
        self.kpt = self.sb("kpt", [32, 2, T], F32)
        self.kper = self.sb("kper", [32, T], F32)
        self.C2 = self.sb("C2", [96, 2, T], F32)
        self.S2 = self.sb("S2", [96, 2, T], F32)
        self.kC = self.sb("kC", [32, T], F32)
        self.kS = self.sb("kS", [32, T], F32)
        self.tq = [self.sb(f"tq{i}", [96, 2, T], F32) for i in range(2)]
        self.t2 = self.sb("t2", [96, 2, T], F32)
        self.qT = self.sb("qT", [96, 8, T], BF16)
        self.kT = self.sb("kT", [96, 8, T], BF16)
        self.vcur = self.sb("vcur", [128, 8, 2, 128], BF16)
        self.vsm = self.sb("vsm", [32, 8, 128], BF16)
        self.mixT = self.sb("mixT", [128, 8, T], BF16)
        self.aT = self.sb("aT", [128, NAT, T], BF16)
        self.sg = [self.sb(f"sg{i}", [128, T], F32) for i in range(2)]
        self.kblk = [self.sb(f"kblk{i}", [96, 1024], BF16) for i in range(3)]
        self.vblk = [self.sb(f"vblk{i}", [128, 8, 128], BF16) for i in range(3)]
        self.pT = [self.sb(f"pT{i}", [128, 512], BF16) for i in range(4)]
        self.rl = [self.sb(f"rl{i}", [128, T], F32) for i in range(2)]
        self.otok = self.sb("otok", [128, 2, KVR + DR], F32)
        self.opool = self.sb("opool", [16, 512], F32)
        self.clin = self.sb("clin", [128, 2, KVR], F32)
        self.ckin = self.sb("ckin", [128, 2, DR], F32)
        self._pT = 0
        self._blk = 0
        self._sg = 0
        self._o = 0

    def init_consts(self):
        B = self.B
        self.dma(self.ident[:], self.ident_d[:, :], [], [B("ident")])
        self.dma(self.gains[:], self.gains_d[:, :], [], [B("gains")])
        self.dma(self.rcnt[:], self.rcnt_d[:, :, :], [], [B("rcnt")])
        self.pool(lambda e: e.memset(self.ones[:], 1.0), [], [B("ones")])
        self.pool(lambda e: e.memset(self.epsb[:], EPS), [], [B("epsb")])
        self.pool(lambda e: e.memset(self.vcur[:], 1.0), [], [B("vcur")])
        self.pool(lambda e: e.memset(self.vsm[:], 1.0), [], [B("vsm")])
        for i in range(2):
            self.pool(lambda e, l=i: e.memset(self.hist[l][:], 0.0), [], [B(f"hist{i}")])

    def cast_weights(self, layers=(0, 1)):
        B = self.B
        n = WTOT // CAST_PIECE
        CP = CAST_PIECE
        flat = lambda t: t[:, :, :].rearrange("p a b -> p (a b)")
        xinf, xTf, qlf = flat(self.xin), flat(self.xT), flat(self.qlat)
        fst = [(xinf[:, 0:CP], [B("xin0"), B("xin1")]), (xinf[:, CP:2 * CP], [B("xin0"), B("xin1")]),
               (xTf[:, 0:CP], [B("xT")]), (xTf[:, CP:2 * CP], [B("xT")]),
               (qlf[:, 0:CP], [B("qlat")]),
               (self.uext[:, 0:CP], [B(f"uext{g}") for g in range(4)]),
               (self.cstf[0][:, :], [B("cstf0")]), (self.cstf[1][:, :], [B("cstf1")])]
        bst = []
        aflat = flat(self.aT)
        for i in range(3):
            c0, c1 = i * CP, (i + 1) * CP
            bst.append((aflat[:, c0:c1], [B(f"aT{c}") for c in range(c0 // TS, (c1 - 1) // TS + 1)]))
        for nm, t in (("hT", self.hT), ("mixT", self.mixT)):
            f = flat(t)
            for i in range(2):
                c0, c1 = i * CP, (i + 1) * CP
                bst.append((f[:, c0:c1], [B(f"{nm}{c}") for c in range(c0 // TS, (c1 - 1) // TS + 1)]))
        sqf = flat(self.sq)
        for i in range(2):
            bst.append((sqf[:, i * CP:(i + 1) * CP], [B("sq"), B("sqh0"), B("sqh1")]))
        k = 0
        for l in layers:
            for i in range(n):
                fa, fb = fst[k % len(fst)]
                ba, bb = bst[k % len(bst)]
                c0 = i * CP
                self.dma(fa, self.wblob[l, :, c0:c0 + CP], [], fb)
                eng = ("dve", "act")[k % 2]
                if eng == "act":
                    self.act(lambda e, fa=fa, ba=ba: e.activation(out=ba, in_=fa, func=AF.Copy), fb, bb)
                else:
                    self.P.add(eng, lambda e, fa=fa, ba=ba: e.tensor_copy(out=ba, in_=fa), fb, bb)
                self.dma(self.wsc[l, :, c0:c0 + CP], ba, bb, [B(f"wsc{l}_{i}")])
                k += 1

    def wscb(self, l, c0, c1):
        return [self.B(f"wsc{l}_{i}") for i in range(c0 // CAST_PIECE, (c1 - 1) // CAST_PIECE + 1)]

    def load_wA(self, l):
        self.dma(self.wA[:], self.wsc[l, :, 0:WA_TOT], self.wscb(l, 0, WA_TOT), [self.B("wA")])

    def load_wO(self, l):
        self.dma(self.wO[:], self.wsc[l, :, O_WO:O_WO + 8192], self.wscb(l, O_WO, O_WO + 8192), [self.B("wO")])

    def load_gu(self, l, pc):
        s = pc % NGU
        o = O_GU + pc * GU_PIECE
        self.dma(self.wGU[s][:], self.wsc[l, :, o:o + GU_PIECE], self.wscb(l, o, o + GU_PIECE), [self.B(f"wGU{s}")])

    def load_dn(self, l, m):
        s = m % NDN
        o = O_DN + m * DN_PIECE
        self.dma(self.wDN[s][:], self.wsc[l, :, o:o + DN_PIECE], self.wscb(l, o, o + DN_PIECE), [self.B(f"wDN{s}")])

    def rstd_chain(self, ps_ap, psb, out_ap, outb, n, tmp_ap, tmpb):
        np_ = ps_ap.shape[0]
        self.act(lambda e: e.activation(out=tmp_ap, in_=ps_ap, func=AF.Ln, scale=1.0 / n, bias=self.epsb[0:np_, 0:1]),
                 [psb, self.B("epsb")], [tmpb])
        self.act(lambda e: e.activation(out=out_ap, in_=tmp_ap, func=AF.Exp, scale=-0.5), [tmpb], [outb])

    def norm_x(self, T, gcol):
        B = self.B
        xT, hT, sq = self.xT, self.hT, self.sq
        self.act(lambda e: e.activation(out=sq[:, :, 0:T], in_=xT[:, :, 0:T], func=AF.Square), [B("xT")], [B("sq")])
        ps, psb = self.psum()
        for k in range(8):
            self.pe(lambda e, k=k: e.matmul(ps[:, 0:T], self.ones[:, :], sq[:, k, 0:T], start=(k == 0), stop=(k == 7)),
                    [B("sq"), B("ones")], [psb])
        self.rstd_chain(ps[:, 0:T], psb, self.rstd[:, 0:T], B("rstd"), float(D), self.rstd2[:, 0:T], B("rstd2"))
        for k in range(8):
            self.dve(lambda e, k=k: e.scalar_tensor_tensor(
                out=hT[:, k, 0:T], in0=xT[:, k, 0:T], scalar=self.gains[:, gcol + k:gcol + k + 1],
                in1=self.rstd[:, 0:T], op0=ALU.mult, op1=ALU.mult), [B("xT"), B("rstd"), B("gains")], [B(f"hT{k}")])

    def hT_bufs(self):
        return [self.B(f"hT{k}") for k in range(8)]

    def load_x(self, src_rows, T):
        B = self.B
        col = 0
        for ti, (ap, n) in enumerate(src_rows):
            s = ti % 2
            self.dma(self.xin[0:n, s, :], ap, [], [B(f"xin{s}")])
            for k4 in range(2):
                ps, psb = self.psum()
                for kk in range(4):
                    k = k4 * 4 + kk
                    self.pe(lambda e, k=k, kk=kk, n=n, s=s, ps=ps: e.transpose(
                        ps[:, kk * 128:kk * 128 + n], self.xin[0:n, s, k * 128:(k + 1) * 128], self.ident[0:n, 0:n]),
                        [B(f"xin{s}"), B("ident")], [psb])
                for kk in range(4):
                    k = k4 * 4 + kk
                    self.act(lambda e, k=k, kk=kk, n=n, col=col, ps=ps: e.activation(
                        out=self.xT[:, k, col:col + n], in_=ps[:, kk * 128:kk * 128 + n], func=AF.Copy),
                        [psb], [B("xT")])
            col += n

    def store_y(self, dst_rows, T):
        B = self.B
        col = 0
        for ti, (ap, n) in enumerate(dst_rows):
            s = ti % 2
            for k4 in range(2):
                ps, psb = self.psum()
                for kk in range(4):
                    k = k4 * 4 + kk
                    self.pe(lambda e, k=k, kk=kk, n=n, col=col, ps=ps: e.transpose(
                        ps[0:n, kk * 128:(kk + 1) * 128], self.xT[:, k, col:col + n], self.ident[:, :]),
                        [B("xT"), B("ident")], [psb])
                self.act(lambda e, k4=k4, n=n, s=s, ps=ps: e.activation(
                    out=self.xin[0:n, s, k4 * 512:(k4 + 1) * 512], in_=ps[0:n, :], func=AF.Copy),
                    [psb], [B(f"xin{s}")])
            self.dbg(f"yst{self._psrot}", self.xin[0:n, s, :], [n, 1024], F32, [B(f"xin{s}")])
            self.dma(ap, self.xin[0:n, s, :], [B(f"xin{s}")], [B("yout")])
            col += n

    def mixer_front(self, l, T, segs, tok_tiles, pos0, kind, sup):
        B = self.B
        nseg, Tseg = segs
        L = 15 + Tseg
        wA, hT = self.wA, self.hT
        hb = self.hT_bufs()
        G = self.gains

        def win(k, c0, m):
            return wA[:, O_WIN + k * WIN_W + c0:O_WIN + k * WIN_W + c0 + m]

        for hh in range(2):
            self.dma(self.C2[:, hh, 0:T], self.rope_d[:, 0, pos0:pos0 + T], [], [B("C2")])
            self.dma(self.S2[:, hh, 0:T], self.rope_d[:, 1, pos0:pos0 + T], [], [B("S2")])
        self.dma(self.kC[:, 0:T], self.rope_d[64:96, 0, pos0:pos0 + T], [], [B("kC")])
        self.dma(self.kS[:, 0:T], self.rope_d[64:96, 1, pos0:pos0 + T], [], [B("kS")])

        uext = self.uext
        W4 = nseg * L

        def uview(g):
            return uext[:, g * W4:(g + 1) * W4].rearrange("p (s l) -> p s l", s=nseg)

        for g in range(4):
            ps, psb = self.psum()
            for k in range(8):
                self.pe(lambda e, k=k, g=g, ps=ps: e.matmul(ps[:, 0:T], win(k, g * 128, 128), hT[:, k, 0:T],
                                                            start=(k == 0), stop=(k == 7)), [hb[k], B("wA")], [psb])
            self.act(lambda e, g=g, ps=ps: e.activation(
                out=uview(g)[:, :, 15:L], in_=ps[:, 0:T].rearrange("p (s t) -> p s t", s=nseg), func=AF.Copy),
                [psb], [B(f"uext{g}")])
        ps_ssq, psb_ssq = self.psum()
        for j in range(3):
            ps, psb = self.psum()
            for k in range(8):
                self.pe(lambda e, k=k, j=j, ps=ps: e.matmul(ps[:, 0:T], win(k, 512 + j * 128, 128), hT[:, k, 0:T],
                                                            start=(k == 0), stop=(k == 7)), [hb[k], B("wA")], [psb])
            self.act(lambda e, j=j, ps=ps: e.activation(out=self.qlat[:, j, 0:T], in_=ps[:, 0:T], func=AF.Copy),
                     [psb], [B("qlat")])
            self.act(lambda e, j=j, ps=ps: e.activation(out=self.sq[:, j, 0:T], in_=ps[:, 0:T], func=AF.Square),
                     [psb], [B("sq")])
        for j in range(3):
            self.pe(lambda e, j=j, ps_ssq=ps_ssq: e.matmul(ps_ssq[:, 0:T], self.ones[:, :], self.sq[:, j, 0:T],
                                            start=(j == 0), stop=(j == 2)), [B("sq"), B("ones")], [psb_ssq])
        self.rstd_chain(ps_ssq[:, 0:T], psb_ssq, self.rstd[:, 0:T], B("rstd"), float(QR), self.rstd2[:, 0:T], B("rstd2"))
        for j in range(3):
            self.dve(lambda e, j=j: e.scalar_tensor_tensor(
                out=self.qn[:, j, 0:T], in0=self.qlat[:, j, 0:T], scalar=G[:, 32 + l * 3 + j:32 + l * 3 + j + 1],
                in1=self.rstd[:, 0:T], op0=ALU.mult, op1=ALU.mult), [B("qlat"), B("rstd"), B("gains")], [B("qn")])
        if kind == "meta" and l == 0:
            self.dbg("qlat", self.qlat[:, :, 0:16], [128, 3, 16], F32, [B("qlat")])
            self.dbg("rstdq", self.rstd[:, 0:16], [128, 16], F32, [B("rstd")])
            self.dbg("sqq", self.sq[:, 0:3, 0:16], [128, 3, 16], BF16, [B("sq")])
        ps_ssq, psb_ssq = self.psum()
        for j in range(2):
            ps, psb = self.psum()
            for k in range(8):
                self.pe(lambda e, k=k, j=j, ps=ps: e.matmul(ps[:, 0:T], win(k, 896 + j * 128, 128), hT[:, k, 0:T],
                                                            start=(k == 0), stop=(k == 7)), [hb[k], B("wA")], [psb])
            self.act(lambda e, j=j, ps=ps: e.activation(out=self.ckv[:, j, 0:T], in_=ps[:, 0:T], func=AF.Copy),
                     [psb], [B("ckv")])
            self.act(lambda e, j=j, ps=ps: e.activation(out=self.sq[:, 4 + j, 0:T], in_=ps[:, 0:T], func=AF.Square),
                     [psb], [B("sq")])
        for j in range(2):
            self.pe(lambda e, j=j, ps_ssq=ps_ssq: e.matmul(ps_ssq[:, 0:T], self.ones[:, :], self.sq[:, 4 + j, 0:T],
                                            start=(j == 0), stop=(j == 1)), [B("sq"), B("ones")], [psb_ssq])
        self.rstd_chain(ps_ssq[:, 0:T], psb_ssq, self.rstd[:, 0:T], B("rstd"), float(KVR), self.rstd2[:, 0:T], B("rstd2"))
        for j in range(2):
            self.dve(lambda e, j=j: e.scalar_tensor_tensor(
                out=self.ckv[:, j, 0:T], in0=self.ckv[:, j, 0:T], scalar=G[:, 38 + l * 2 + j:38 + l * 2 + j + 1],
                in1=self.rstd[:, 0:T], op0=ALU.mult, op1=ALU.mult), [B("rstd"), B("gains")], [B("ckv")])
        self.dve(lambda e: e.tensor_copy(out=self.ckvb[:, :, 0:T], in_=self.ckv[:, :, 0:T]), [B("ckv")], [B("ckvb")])
        ps, psb = self.psum()
        for j in range(2):
            for k in range(8):
                self.pe(lambda e, k=k, j=j, ps=ps: e.matmul(ps[0:32, j * T:(j + 1) * T], win(k, 1152 + 32 * j, 32),
                                                            hT[:, k, 0:T], start=(k == 0), stop=(k == 7)),
                        [hb[k], B("wA")], [psb])
        self.dve(lambda e, ps=ps: e.tensor_tensor(out=self.kpt[:, 0, 0:T], in0=ps[0:32, 0:T], in1=self.kC[:, 0:T],
                                                  op=ALU.mult), [psb, B("kC")], [B("kpt")])
        self.dve(lambda e, ps=ps: e.tensor_tensor(out=self.kpt[:, 1, 0:T], in0=ps[0:32, T:2 * T], in1=self.kS[:, 0:T],
                                                  op=ALU.mult), [psb, B("kS")], [B("kpt")])
        self.dve(lambda e: e.tensor_tensor(out=self.kper[:, 0:T], in0=self.kpt[:, 0, 0:T], in1=self.kpt[:, 1, 0:T],
                                            op=ALU.add), [B("kpt")], [B("kper")])

        if kind == "sample":
            for s_ in range(nseg):
                self.dma(self.opool[0:15, :], self.spool[l, s_, :, :], [], [B("opool")])
                ps, psb = self.psum()
                for g in range(4):
                    self.pe(lambda e, g=g, ps=ps: e.transpose(ps[:, g * 16:g * 16 + 15], self.opool[0:15, g * 128:(g + 1) * 128],
                                                              self.ident[0:15, 0:15]), [B("opool"), B("ident")], [psb])
                for g in range(4):
                    self.act(lambda e, g=g, s_=s_, ps=ps: e.activation(out=uview(g)[:, s_, 0:15], in_=ps[:, g * 16:g * 16 + 15],
                                                                       func=AF.Copy), [psb], [B(f"uext{g}")])
        else:
            for g in range(4):
                self.pool(lambda e, g=g: e.tensor_copy(out=uview(g)[:, 0, 0:15], in_=self.hist[l][:, g, :]),
                          [B(f"hist{l}")], [B(f"uext{g}")])
        for g in range(4):
            w = WINS[g]
            cur = uview(g)
            curb = B(f"uext{g}")
            sh = 1
            step = 0
            while sh < w:
                lo = 2 * sh - 1
                dst = self.ptmp[step % 2][:, 0:W4].rearrange("p (s l) -> p s l", s=nseg)
                dstb = B(f"ptmp{step % 2}")
                self.pool(lambda e, cur=cur, dst=dst, lo=lo, sh=sh: e.tensor_tensor(
                    out=dst[:, :, lo:L], in0=cur[:, :, lo:L], in1=cur[:, :, lo - sh:L - sh], op=ALU.add), [curb], [dstb])
                cur, curb = dst, dstb
                sh *= 2
                step += 1
            dview = self.dT[:, g, 0:T].rearrange("p (s t) -> p s t", s=nseg)
            if kind == "meta":
                self.pool(lambda e, cur=cur, g=g: e.tensor_tensor(out=cur[:, 0, 15:L], in0=cur[:, 0, 15:L],
                                                                   in1=self.rcnt[:, g, :], op=ALU.mult), [B("rcnt")], [curb])
            else:
                self.pool(lambda e, cur=cur, w=w: e.tensor_scalar(out=cur[:, :, 15:L], in0=cur[:, :, 15:L], scalar1=1.0 / w,
                                                                   scalar2=None, op0=ALU.mult), [], [curb])
            self.pool(lambda e, cur=cur, g=g, dview=dview: e.tensor_tensor(
                out=dview[:, :, :], in0=cur[:, :, 15:L], in1=uview(g)[:, :, 15:L], op=ALU.subtract),
                [curb, B(f"uext{g}")], [B("dT")])
        if kind != "sample":
            for g in range(4):
                self.pool(lambda e, g=g: e.tensor_copy(out=self.hist[l][:, g, :], in_=uview(g)[:, 0, L - 15:L]),
                          [B(f"uext{g}")], [B(f"hist{l}")])

        def pool_section():
            for g in range(4):
                ps, psb = self.psum()
                self.pe(lambda e, g=g, ps=ps: e.matmul(ps[:, 0:T], wA[:, O_WPOOL + g * 128:O_WPOOL + (g + 1) * 128],
                                                       self.dT[:, g, 0:T], start=True, stop=True), [B("dT"), B("wA")], [psb])
                self.dve(lambda e, g=g, ps=ps: e.tensor_scalar(out=self.mixT[:, g, 0:T], in0=ps[:, 0:T],
                                                               scalar1=G[:, 42 + l * 4 + g:42 + l * 4 + g + 1], scalar2=None,
                                                               op0=ALU.mult), [psb, B("gains")], [B(f"mixT{g}")])
            if kind == "sample" or (kind == "sup" and sup == self.NSUP - 1):
                for s_ in range(nseg):
                    ps, psb = self.psum()
                    for g in range(4):
                        self.pe(lambda e, g=g, s_=s_, ps=ps: e.transpose(ps[0:15, g * 128:(g + 1) * 128], uview(g)[:, s_, L - 15:L],
                                                                         self.ident[:, :]), [B(f"uext{g}"), B("ident")], [psb])
                    self.act(lambda e, ps=ps: e.activation(out=self.opool[0:15, :], in_=ps[0:15, :], func=AF.Copy),
                             [psb], [B("opool")])
                    dst = self.pools[l, s_, :, :] if kind == "sample" else self.poolp[l, :, :]
                    self.dma(dst, self.opool[0:15, :], [B("opool")], [B("pout")])

        self._deferred.append(pool_section)

        def wuq(j, h, sw):
            o = O_WUQ + (j * 8 + h) * 192 + sw * 96
            return wA[:, o:o + 96]

        v3 = lambda ap: ap[0:96, 0:2 * T].rearrange("p (a t) -> p a t", a=2)

        def q_mm(hp):
            a, b = (0, 1) if hp % 2 == 0 else (2, 3)
            psA, psAb, psB, psBb = self.ps[a], self.psb[a], self.ps[b], self.psb[b]
            for hh in range(2):
                h = hp * 2 + hh
                for j in range(3):
                    self.pe(lambda e, h=h, hh=hh, j=j, psA=psA: e.matmul(
                        psA[0:96, hh * T:(hh + 1) * T], wuq(j, h, 0), self.qn[:, j, 0:T], start=(j == 0), stop=(j == 2)),
                        [B("qn"), B("wA")], [psAb])
                for j in range(3):
                    self.pe(lambda e, h=h, hh=hh, j=j, psB=psB: e.matmul(
                        psB[0:96, hh * T:(hh + 1) * T], wuq(j, h, 1), self.qn[:, j, 0:T], start=(j == 0), stop=(j == 2)),
                        [B("qn"), B("wA")], [psBb])

        def q_chain(hp):
            a, b = (0, 1) if hp % 2 == 0 else (2, 3)
            psA, psAb, psB, psBb = self.ps[a], self.psb[a], self.ps[b], self.psb[b]
            tb = self.tq[0]
            tbb = B("tq0")
            self.dve(lambda e, psA=psA, tb=tb: e.tensor_tensor(out=tb[:, :, 0:T], in0=v3(psA), in1=self.C2[:, :, 0:T],
                                                               op=ALU.mult), [psAb, B("C2")], [tbb])
            self.dve(lambda e, psB=psB: e.tensor_tensor(out=self.t2[:, :, 0:T], in0=v3(psB), in1=self.S2[:, :, 0:T],
                                                        op=ALU.mult), [psBb, B("S2")], [B("t2")])
            self.dve(lambda e, tb=tb: e.tensor_tensor(out=tb[:, :, 0:T], in0=tb[:, :, 0:T], in1=self.t2[:, :, 0:T],
                                                      op=ALU.add), [B("t2")], [tbb])
            self.norm_heads(tb, tbb, self.qT, hp, T, 50 + l, B(f"qT{hp}"), 4, 0)

        k_mm, k_chain = self.kgen_fns(l, T, self.ckvb, B("ckvb"), self.kper, B("kper"), self.kT, "kT", (5, 6, 7), False)
        q_mm(0)
        k_mm(0)
        q_mm(1)
        k_mm(1)
        for p in range(4):
            q_chain(p)
            k_chain(p)
            if p + 2 < 4:
                q_mm(p + 2)
                k_mm(p + 2)

        if kind != "sample":
            for ti, (c0, n) in enumerate(tok_tiles):
                ps, psb = self.psum()
                for j in range(2):
                    self.pe(lambda e, j=j, c0=c0, n=n, ps=ps: e.matmul(ps[0:n, :], self.ckvb[:, j, c0:c0 + n],
                                                                       wA[:, O_WUV + j * 512:O_WUV + (j + 1) * 512],
                                                                       start=(j == 0), stop=(j == 1)), [B("ckvb"), B("wA")], [psb])
                self.vevac(ps, psb, n, lambda par, ti=ti, n=n: self.vcur[0:n, par::2, ti, par * 64:par * 64 + 64], B("vcur"))

        for ti, (c0, n) in enumerate(tok_tiles):
            s = ti % 2
            ps, psb = self.psum()
            for j in range(2):
                self.pe(lambda e, j=j, c0=c0, n=n, ps=ps: e.transpose(ps[0:n, j * 128:(j + 1) * 128], self.ckv[:, j, c0:c0 + n],
                                                                      self.ident[:, :]), [B("ckv"), B("ident")], [psb])
            self.pe(lambda e, c0=c0, n=n, ps=ps: e.transpose(ps[0:n, 256:288], self.kper[:, c0:c0 + n], self.ident[0:32, 0:32]),
                    [B("kper"), B("ident")], [psb])
            self.act(lambda e, n=n, s=s, ps=ps: e.activation(out=self.otok[0:n, s, :], in_=ps[0:n, 0:288], func=AF.Copy),
                     [psb], [B(f"otok{s}")])
            if kind == "sample":
                self.dma(self.lats[l, ti, :, :], self.otok[0:32, s, 0:KVR], [B(f"otok{s}")], [B("lout")])
                self.dma(self.krs[l, ti, :, :], self.otok[0:32, s, KVR:KVR + DR], [B(f"otok{s}")], [B("lout")])
            else:
                r0 = (0 if kind == "meta" else NMETA + sup * TS) + c0
                self.dma(self.latp[l, r0:r0 + n, :], self.otok[0:n, s, 0:KVR], [B(f"otok{s}")], [B("lout")])
                self.dma(self.krp[l, r0:r0 + n, :], self.otok[0:n, s, KVR:KVR + DR], [B(f"otok{s}")], [B("lout")])

    def vevac(self, ps, psb, n, dst_fn, dstb, on_dve=False):
        for par in range(2):
            src = ps[0:n, :].rearrange("p (h d) -> p h d", d=64)[:, par::2, :]
            if on_dve:
                self.dve(lambda e, src=src, par=par: e.tensor_copy(out=dst_fn(par), in_=src), [psb], [dstb])
            else:
                self.act(lambda e, src=src, par=par: e.activation(out=dst_fn(par), in_=src, func=AF.Copy), [psb], [dstb])

    def norm_heads(self, src, srcb, dst, hp, T, gcol, dstb, bank, side):
        B = self.B
        sl = 4 if side == 0 else 6
        sqb = B(f"sqh{side}")
        rs, rsb = (self.rstd, B("rstd")) if side == 0 else (self.rstd2, B("rstd2"))
        self.act(lambda e: e.activation(out=self.sq[0:96, sl:sl + 2, 0:T], in_=src[:, :, 0:T], func=AF.Square), [srcb], [sqb, B("sq")])
        ps, psb = self.ps[bank], self.psb[bank]
        for hh in range(2):
            self.pe(lambda e, hh=hh, ps=ps: e.matmul(ps[0:96, hh * T:(hh + 1) * T], self.ones[0:96, 0:96],
                                                     self.sq[0:96, sl + hh, 0:T], start=True, stop=True),
                    [sqb, B("ones")], [psb])
        self.act(lambda e, ps=ps: e.activation(out=rs[0:96, 0:2 * T], in_=ps[0:96, 0:2 * T], func=AF.Ln, scale=1.0 / DQK,
                                               bias=self.epsb[0:96, 0:1]), [psb, B("epsb")], [rsb])
        self.act(lambda e: e.activation(out=rs[0:96, 0:2 * T], in_=rs[0:96, 0:2 * T], func=AF.Exp, scale=-0.5), [], [rsb])
        self.dve(lambda e: e.scalar_tensor_tensor(
            out=dst[:, 2 * hp:2 * hp + 2, 0:T], in0=src[:, :, 0:T], scalar=self.gains[0:96, gcol:gcol + 1],
            in1=rs[0:96, 0:2 * T].rearrange("p (a t) -> p a t", a=2), op0=ALU.mult, op1=ALU.mult),
            [srcb, rsb, B("gains")], [dstb])

    def kgen_fns(self, l, T, latb_t, latb, kper_t, kperb, dst, dstname, banks, on_dve):
        B = self.B
        wA = self.wA

        def k_mm(hp):
            bk = banks[hp % 2]
            ps, psb = self.ps[bk], self.psb[bk]
            for hh in range(2):
                h = hp * 2 + hh
                for j in range(2):
                    o = O_WUK + (j * 8 + h) * 64
                    self.pe(lambda e, hh=hh, j=j, o=o, ps=ps: e.matmul(ps[0:64, hh * T:(hh + 1) * T], wA[:, o:o + 64],
                                                                       latb_t[:, j, 0:T], start=(j == 0), stop=(j == 1)),
                            [latb, B("wA")], [psb])

        def k_chain(hp):
            bk = banks[hp % 2]
            ps, psb = self.ps[bk], self.psb[bk]
            kr = self.tq[1]
            krb = B("tq1")
            src = ps[0:64, 0:2 * T].rearrange("p (a t) -> p a t", a=2)
            if on_dve:
                self.dve(lambda e, kr=kr, src=src: e.tensor_copy(out=kr[0:64, :, 0:T], in_=src), [psb], [krb])
            else:
                self.act(lambda e, kr=kr, src=src: e.activation(out=kr[0:64, :, 0:T], in_=src, func=AF.Copy), [psb], [krb])
            self.norm_heads(kr, krb, dst, hp, T, 52 + l, B(f"{dstname}{hp}"), banks[2], 1)

        for hh in range(2):
            self.dve(lambda e, hh=hh: e.tensor_copy(out=self.tq[1][64:96, hh, 0:T], in_=kper_t[0:32, 0:T]),
                     [kperb], [B("tq1")])
        return k_mm, k_chain

    def kgen(self, l, T, latb_t, latb, kper_t, kperb, dst, dstname):
        k_mm, k_chain = self.kgen_fns(l, T, latb_t, latb, kper_t, kperb, dst, dstname, (0, 1, 2), True)
        k_mm(0)
        k_mm(1)
        k_chain(0)
        k_mm(2)
        k_chain(1)
        k_mm(3)
        k_chain(2)
        k_chain(3)

    def next_pT(self):
        i = self._pT % 4
        self._pT += 1
        return self.pT[i], self.B(f"pT{i}")

    def attn_finish(self, h, T, po, pob):
        B = self.B
        par = h % 2
        lo, hi = par * 64, par * 64 + 64
        olo, ohi = (1 - par) * 64, (1 - par) * 64 + 64
        i = self._o % 2
        self._o += 1
        rl, rlb = self.rl[i], B(f"rl{i}")
        self.act(lambda e: e.activation(out=rl[olo:ohi, 0:T], in_=po[olo:ohi, 0:T], func=AF.Ln), [pob], [rlb])
        self.act(lambda e: e.activation(out=rl[olo:ohi, 0:T], in_=rl[olo:ohi, 0:T], func=AF.Exp, scale=-1.0), [], [rlb])
        self.dve(lambda e: e.tensor_tensor(out=self.mixT[lo:hi, 4 + h // 2, 0:T], in0=po[lo:hi, 0:T], in1=rl[olo:ohi, 0:T],
                                           op=ALU.mult), [pob, rlb], [B(f"mixT{4 + h // 2}")])

    def attn_prefetch(self, l, kind, sup):
        B = self.B
        nctx_tiles = 0 if kind == "meta" else 1 + sup * (TS // 128)
        NBUF = len(self.kblk)
        items = []
        for h in range(H):
            t0 = 0
            while t0 < nctx_tiles:
                nt = min(8, nctx_tiles - t0)
                items.append((h, t0, nt))
                t0 += nt
        loaded = {}

        def load(i):
            h, t0, nt = items[i]
            bi = self._blk % NBUF
            self._blk += 1
            kb, vb = self.kblk[bi], self.vblk[bi]
            kbb, vbb = B(f"kblk{bi}"), B(f"vblk{bi}")
            deps = [B(f"kc{l}_{t}") for t in range(t0, t0 + nt)]
            vdeps = [B(f"vc{l}_{t}") for t in range(t0, t0 + nt)]
            self.dma(kb[:, 0:nt * 128], self.kc[l, h, :, t0 * 128:(t0 + nt) * 128], deps, [kbb])
            self.dma(vb[:, 0:nt, :], self.vc[l, h, :, t0:t0 + nt, :], vdeps, [vbb])
            loaded[i] = (kb, vb, kbb, vbb)

        nxt = 0
        for i in range(min(NBUF - 1, len(items))):
            load(i)
            nxt = i + 1
        self._att = (items, loaded, load, nxt)

    def attn_prompt(self, l, T, kind, sup):
        B = self.B
        NBUF = len(self.kblk)
        PF = NBUF - 1
        items, loaded, load, nxt = self._att
        pending = []
        cur_blk = [None]

        def try_prefetch(i):
            nonlocal nxt
            while nxt < len(items) and nxt <= i + PF and (nxt - NBUF) not in {b for _, b in pending}:
                load(nxt)
                nxt += 1

        ii = 0
        for h in range(H):
            po, pob = self.ps[6 + h % 2], self.psb[6 + h % 2]
            qh = self.qT[:, h, 0:T]
            qb = B(f"qT{h // 2}")
            hblocks = [i for i, it in enumerate(items) if it[0] == h]
            nsteps = sum((items[i][2] + 1) // 2 for i in hblocks) + 1
            si = 0

            def do_step(grp, si):
                ps, psb = self.psum()
                pT, pTb = self.next_pT()
                for gi, (k_ap, kbuf, v_ap, vbuf, nk, m) in enumerate(grp):
                    self.pe(lambda e, k_ap=k_ap, gi=gi, nk=nk, ps=ps, qh=qh: e.matmul(ps[0:nk, gi * T:(gi + 1) * T], k_ap, qh,
                                                                                     start=True, stop=True), [kbuf, qb], [psb])
                nkmax = max(g[4] for g in grp)
                w = len(grp) * T
                if len(grp) == 2 and grp[0][4] != grp[1][4]:
                    for gi, g in enumerate(grp):
                        self.act(lambda e, gi=gi, nk=g[4], ps=ps, pT=pT: e.activation(
                            out=pT[0:nk, gi * T:(gi + 1) * T], in_=ps[0:nk, gi * T:(gi + 1) * T], func=AF.Exp, scale=SCALE),
                            [psb], [pTb])
                else:
                    self.act(lambda e, nk=nkmax, w=w, ps=ps, pT=pT: e.activation(out=pT[0:nk, 0:w], in_=ps[0:nk, 0:w],
                                                                                 func=AF.Exp, scale=SCALE), [psb], [pTb])
                for gi, (k_ap, kbuf, v_ap, vbuf, nk, m) in enumerate(grp):
                    if m == 0:
                        self.pool(lambda e, pT=pT: e.memset(pT[64:128, 0:64], 0.0), [], [pTb])
                    elif m == 1:
                        self.pool(lambda e, pT=pT: e.memset(pT[0:64, T:T + 128], 0.0), [], [pTb])
                        self.pool(lambda e, pT=pT: e.memset(pT[64:128, T:T + 192], 0.0), [], [pTb])
                prev = []
                while len(pending) > 1:
                    prev.extend(pending.pop(0)[0])

                def pv(grp=grp, si=si, pT=pT, pTb=pTb, po=po, pob=pob, nsteps=nsteps):
                    for gi, (k_ap, kbuf, v_ap, vbuf, nk, m) in enumerate(grp):
                        st = (si == 0 and gi == 0)
                        sp_ = (si == nsteps - 1 and gi == len(grp) - 1)
                        self.pe(lambda e, v_ap=v_ap, gi=gi, nk=nk, pT=pT, st=st, sp_=sp_, po=po: e.matmul(
                            po[:, 0:T], v_ap, pT[0:nk, gi * T:(gi + 1) * T], start=st, stop=sp_), [vbuf, pTb], [pob])
                for f in prev:
                    f()
                pending.append(([pv], cur_blk[0]))

            for i in hblocks:
                _, t0, nt = items[i]
                cur_blk[0] = i
                try_prefetch(i)
                kb, vb, kbb, vbb = loaded.pop(i)
                for tt in range(0, nt, 2):
                    grp = []
                    for t in range(tt, min(tt + 2, nt)):
                        nk = NMETA if (t0 + t) == 0 else 128
                        grp.append((kb[:, t * 128:t * 128 + nk], kbb, vb[0:nk, t, :], vbb, nk, None))
                    do_step(grp, si)
                    si += 1
                    try_prefetch(i)
            if kind == "meta":
                grp = [(self.kT[:, h, 0:16], B(f"kT{h // 2}"), self.vcur[0:16, h, 0, :], B("vcur"), 16, None)]
            else:
                grp = [(self.kT[:, h, 0:128], B(f"kT{h // 2}"), self.vcur[:, h, 0, :], B("vcur"), 128, 0),
                       (self.kT[:, h, 128:256], B(f"kT{h // 2}"), self.vcur[:, h, 1, :], B("vcur"), 128, 1)]
            cur_blk[0] = None
            do_step(grp, si)
            si += 1
            assert si == nsteps
            pending[-1][0].append(lambda h=h, po=po, pob=pob: self.attn_finish(h, T, po, pob))
        for grp_, _ in pending:
            for f in grp_:
                f()

    def attn_sample(self, l):
        B = self.B
        T = 128
        NR = self.NROWS
        wA = self.wA
        for s in range(4):
            po, pob = self.ps[6 + s % 2], self.psb[6 + s % 2]
            blocks = []
            r = 0
            while r < NR:
                n = min(256, NR - r)
                blocks.append((r, n))
                r += n
            nb = len(blocks)
            for bi, (r0, n) in enumerate(blocks + [("new", 32)]):
                if r0 == "new":
                    tiles = [(0, 32)]
                    ps, psb = self.psum()
                    for j in range(2):
                        self.pe(lambda e, j=j, ps=ps, s=s: e.matmul(ps[0:32, :], self.ckvb[:, j, s * 32:(s + 1) * 32],
                                                               wA[:, O_WUV + j * 512:O_WUV + (j + 1) * 512],
                                                               start=(j == 0), stop=(j == 1)), [B("ckvb"), B("wA")], [psb])
                    self.vevac(ps, psb, 32, lambda par: self.vsm[0:32, par::2, par * 64:par * 64 + 64], B("vsm"))
                    kt_fn = lambda h, c0, nk, s=s: self.kT[:, h, s * 32:s * 32 + 32]
                    ktb = [B(f"kT{hp}") for hp in range(4)]
                    v_fn = lambda h, ti, nk: self.vsm[0:32, h, :]
                    vb_ = B("vsm")
                else:
                    tiles = [(c0, min(128, n - c0)) for c0 in range(0, n, 128)]
                    for ti, (c0, nk) in enumerate(tiles):
                        self.dma(self.clin[0:nk, ti, :], self.clat[l, s, r0 + c0:r0 + c0 + nk, :], [], [B(f"clin{ti}")])
                        self.dma(self.ckin[0:nk, ti, :], self.ckr[l, s, r0 + c0:r0 + c0 + nk, :], [], [B(f"ckin{ti}")])
                        import os
                        AS = int(os.environ.get("MK_AS", 99))
                        if AS <= -1:
                            continue
                        ps, psb = self.psum()
                        for j in range(2):
                            self.pe(lambda e, j=j, ti=ti, nk=nk, ps=ps: e.transpose(
                                ps[:, j * 128:j * 128 + nk], self.clin[0:nk, ti, j * 128:(j + 1) * 128], self.ident[0:nk, 0:nk]),
                                [B(f"clin{ti}"), B("ident")], [psb])
                        self.pe(lambda e, ti=ti, nk=nk, ps=ps: e.transpose(ps[0:32, 256:256 + nk], self.ckin[0:nk, ti, :],
                                                                           self.ident[0:nk, 0:nk]), [B(f"ckin{ti}"), B("ident")], [psb])
                        if AS <= 0:
                            continue
                        self.dve(lambda e, c0=c0, nk=nk, ps=ps: e.tensor_copy(
                            out=self.ckvb2[:, :, c0:c0 + nk], in_=ps[:, 0:256].rearrange("p (j t) -> p j t", j=2)[:, :, 0:nk]),
                            [psb], [B("ckvb2")])
                        if os.environ.get("MK_NODVE"):
                            continue
                        self.dve(lambda e, c0=c0, nk=nk, ps=ps: e.tensor_copy(out=self.kper2[:, c0:c0 + nk],
                                                                             in_=ps[0:32, 256:256 + nk]), [psb], [B("kper2")])
                    import os
                    AS = int(os.environ.get("MK_AS", 99))
                    if AS <= 1:
                        continue
                    self.kgen(l, n, self.ckvb2, B("ckvb2"), self.kper2, B("kper2"), self.kT2, "kT2")
                    if AS <= 2:
                        continue
                    for ti, (c0, nk) in enumerate(tiles):
                        ps, psb = self.psum()
                        for j in range(2):
                            self.pe(lambda e, j=j, c0=c0, nk=nk, ps=ps: e.matmul(
                                ps[0:nk, :], self.ckvb2[:, j, c0:c0 + nk], wA[:, O_WUV + j * 512:O_WUV + (j + 1) * 512],
                                start=(j == 0), stop=(j == 1)), [B("ckvb2"), B("wA")], [psb])
                        self.vevac(ps, psb, nk, lambda par, ti=ti, nk=nk: self.vcur2[0:nk, par::2, ti, par * 64:par * 64 + 64],
                                   B("vcur2"), on_dve=True)
                    kt_fn = lambda h, c0, nk: self.kT2[:, h, c0:c0 + nk]
                    ktb = [B(f"kT2{hp}") for hp in range(4)]
                    v_fn = lambda h, ti, nk: self.vcur2[0:nk, h, ti, :]
                    vb_ = B("vcur2")
                import os
                AS = int(os.environ.get("MK_AS", 99))
                if AS <= 3:
                    continue
                ps, psb = self.psum()
                pT, pTb = self.next_pT()
                nt = len(tiles)
                for h in range(H):
                    for ti, (c0, nk) in enumerate(tiles):
                        col = (h * nt + ti) * 32
                        self.pe(lambda e, h=h, c0=c0, nk=nk, col=col, ps=ps, kt_fn=kt_fn, s=s: e.matmul(
                            ps[0:nk, col:col + 32], kt_fn(h, c0, nk), self.qT[:, h, s * 32:s * 32 + 32], start=True, stop=True),
                            [ktb[h // 2], B(f"qT{h // 2}")], [psb])
                if nt == 2 and tiles[0][1] != tiles[1][1]:
                    raise NotImplementedError("ragged block")
                nk = tiles[0][1]
                w = H * nt * 32
                self.act(lambda e, nk=nk, w=w, ps=ps, pT=pT: e.activation(out=pT[0:nk, 0:w], in_=ps[0:nk, 0:w], func=AF.Exp,
                                                                          scale=SCALE), [psb], [pTb])
                if AS <= 4:
                    continue
                for h in range(H):
                    for ti, (c0, nk) in enumerate(tiles):
                        col = (h * nt + ti) * 32
                        st = (bi == 0 and ti == 0 and h == 0)
                        sp_ = (r0 == "new")
                        self.pe(lambda e, h=h, ti=ti, nk=nk, col=col, pT=pT, st=st, sp_=sp_, v_fn=v_fn, po=po: e.matmul(
                            po[:, h * 32:(h + 1) * 32], v_fn(h, ti, nk), pT[0:nk, col:col + 32], start=st, stop=sp_),
                            [vb_, pTb], [pob])
            if AS <= 5:
                continue
            i = self._o % 2
            self._o += 1
            rl, rlb = self.rl[i], B(f"rl{i}")
            self.act(lambda e, rl=rl, po=po: e.activation(out=rl[:, 0:256], in_=po[:, 0:256], func=AF.Ln), [pob], [rlb])
            self.act(lambda e, rl=rl: e.activation(out=rl[:, 0:256], in_=rl[:, 0:256], func=AF.Exp, scale=-1.0), [], [rlb])
            for h in range(H):
                par = h % 2
                lo, hi = par * 64, par * 64 + 64
                olo, ohi = (1 - par) * 64, (1 - par) * 64 + 64
                self.dve(lambda e, h=h, lo=lo, hi=hi, olo=olo, ohi=ohi, rl=rl, po=po, s=s: e.tensor_tensor(
                    out=self.mixT[lo:hi, 4 + h // 2, s * 32:(s + 1) * 32], in0=po[lo:hi, h * 32:(h + 1) * 32],
                    in1=rl[olo:ohi, h * 32:(h + 1) * 32], op=ALU.mult), [pob, rlb], [B(f"mixT{4 + h // 2}")])

    def store_kv(self, l, kind, sup):
        B = self.B
        if kind == "meta":
            self.dbg(f"kTmeta{l}", self.kT[:, :, 0:16], [96, 8, 16], BF16, [B(f"kT{hp}") for hp in range(4)])
            self.dma(self.kc[l, :, :, 0:16].rearrange("h p t -> p h t"), self.kT[:, :, 0:16],
                     [B(f"kT{hp}") for hp in range(4)], [B(f"kc{l}_0")])
            self.dma(self.vc[l, :, 0:16, 0, :].rearrange("h p d -> p h d"), self.vcur[0:16, :, 0, :], [B("vcur")], [B(f"vc{l}_0")])
        else:
            t0 = 1 + sup * 2
            c0 = t0 * 128
            self.dma(self.kc[l, :, :, c0:c0 + TS].rearrange("h p t -> p h t"), self.kT[:, :, 0:TS],
                     [B(f"kT{hp}") for hp in range(4)], [B(f"kc{l}_{t0}"), B(f"kc{l}_{t0 + 1}")])
            self.dma(self.vc[l, :, :, t0:t0 + 2, :].rearrange("h p t d -> p h t d"), self.vcur[:, :, :, :],
                     [B("vcur")], [B(f"vc{l}_{t0}"), B(f"vc{l}_{t0 + 1}")])

    def out_proj(self, l, T):
        B = self.B
        mb = [B(f"mixT{k}") for k in range(8)]
        for m in range(8):
            ps, psb = self.psum()
            for k in range(8):
                self.pe(lambda e, k=k, m=m, ps=ps: e.matmul(ps[:, 0:T], self.wO[:, k * 1024 + m * 128:k * 1024 + (m + 1) * 128],
                                                            self.mixT[:, k, 0:T], start=(k == 0), stop=(k == 7)),
                        [mb[k], B("wO")], [psb])
            self.dve(lambda e, m=m, ps=ps: e.tensor_tensor(out=self.xT[:, m, 0:T], in0=ps[:, 0:T], in1=self.xT[:, m, 0:T],
                                                           op=ALU.add), [psb], [B("xT")])

    def ffn(self, l, T):
        B = self.B
        hb = self.hT_bufs()
        for c in range(NDN):
            self.load_dn(l, c)
        self.norm_x(T, 16 + l * 8)

        def down(c):
            wd = self.wDN[c % NDN]
            wdb = B(f"wDN{c % NDN}")
            for m in range(8):
                bk = 4 + m // 2
                half = m % 2
                self.pe(lambda e, c=c, m=m, bk=bk, half=half, wd=wd: e.matmul(
                    self.ps[bk][:, half * T:(half + 1) * T], wd[:, m * 128:(m + 1) * 128], self.aT[:, c % NAT, 0:T],
                    start=(c == 0 and half == 0), stop=(c == NFF - 1), skip_group_check=True),
                    [B(f"aT{c % NAT}"), wdb], [self.psb[bk]])
            if c + NDN < NFF:
                self.load_dn(l, c + NDN)

        for cc in range(NFF):
            wg = self.wGU[cc % NGU]
            wgb = B(f"wGU{cc % NGU}")
            bi = self._ps4 % 4
            self._ps4 += 1
            ps, psb = self.ps[bi], self.psb[bi]
            for gu in range(2):
                for k in range(8):
                    o = (k * 2 + gu) * 128
                    self.pe(lambda e, k=k, gu=gu, o=o, ps=ps, wg=wg: e.matmul(ps[:, gu * T:(gu + 1) * T], wg[:, o:o + 128],
                                                                              self.hT[:, k, 0:T], start=(k == 0), stop=(k == 7)),
                            [hb[k], wgb], [psb])
            i = self._sg % 2
            self._sg += 1
            sg, sgb = self.sg[i], B(f"sg{i}")
            self.act(lambda e, ps=ps, sg=sg: e.activation(out=sg[:, 0:T], in_=ps[:, 0:T], func=AF.Silu), [psb], [sgb])
            self.dve(lambda e, ps=ps, sg=sg, cc=cc: e.tensor_tensor(out=self.aT[:, cc % NAT, 0:T], in0=ps[:, T:2 * T],
                                                                    in1=sg[:, 0:T], op=ALU.mult), [psb, sgb], [B(f"aT{cc % NAT}")])
            if cc + NGU < NFF:
                self.load_gu(l, cc + NGU)
            if cc >= 1:
                down(cc - 1)
        down(NFF - 1)
        for m in range(8):
            bk = 4 + m // 2
            half = m % 2
            self.dve(lambda e, m=m, bk=bk, half=half: e.tensor_tensor(
                out=self.xT[:, m, 0:T], in0=self.ps[bk][:, half * T:(half + 1) * T], in1=self.xT[:, m, 0:T], op=ALU.add),
                [self.psb[bk]], [B("xT")])

    def tile(self, kind, sup=0):
        if kind == "meta":
            T, segs, tok_tiles, pos0 = 16, (1, 16), [(0, 16)], 0
            src = [(self.meta[:, :], 16)]
        elif kind == "sup":
            T, segs, tok_tiles = TS, (1, TS), [(0, 128), (128, 128)]
            pos0 = NMETA + sup * TS
            src = [(self.xp[sup * TS + i * 128:sup * TS + (i + 1) * 128, :], 128) for i in range(2)]
        else:
            T, segs, tok_tiles, pos0 = 128, (4, 32), [(s * 32, 32) for s in range(4)], self.LP
            src = [(self.xs[:, :], 128)]
        self.load_x(src, T)
        if kind == "meta":
            self.dbg("xT", self.xT[:, :, 0:16], [128, 8, 16], F32, [self.B("xT")])
        if not getattr(self, "_wA_first", False):
            self._wA_first = True
            self.load_wA(0)
        for l in range(2):
            self.load_wO(l)
            import os
            SS = int(os.environ.get("MK_SS", 99)) if kind == "sample" else 99
            self.norm_x(T, l * 8)
            if SS <= 1:
                return
            if kind == "meta" and l == 0:
                self.dbg("hT", self.hT[:, :, 0:16], [128, 8, 16], BF16, self.hT_bufs())
                self.dbg("rstd", self.rstd[:, 0:16], [128, 16], F32, [self.B("rstd")])
                self.dbg("wA", self.wA[:, 0:2048], [128, 2048], BF16, [self.B("wA")])
            if kind != "sample":
                self.attn_prefetch(l, kind, sup)
            self.mixer_front(l, T, segs, tok_tiles, pos0, kind, sup)
            if not self._cast1_done:
                self._cast1_done = True
                self.cast_weights(layers=(1,))
            if SS <= 2:
                return
            for c in range(NGU):
                self.load_gu(l, c)
            if kind == "meta" and l == 0:
                B = self.B
                self.dbg("uext", self.uext[:, 0:4 * 31], [128, 4 * 31], F32, [B(f"uext{g}") for g in range(4)])
                self.dbg("qn", self.qn[:, :, 0:16], [128, 3, 16], BF16, [B("qn")])
                self.dbg("ckv", self.ckv[:, :, 0:16], [128, 2, 16], F32, [B("ckv")])
                self.dbg("kper", self.kper[:, 0:16], [32, 16], F32, [B("kper")])
                self.dbg("dT", self.dT[:, :, 0:16], [128, 4, 16], BF16, [B("dT")])
                self.dbg("qT", self.qT[:, :, 0:16], [96, 8, 16], BF16, [B(f"qT{i}") for i in range(4)])
                self.dbg("kT", self.kT[:, :, 0:16], [96, 8, 16], BF16, [B(f"kT{i}") for i in range(4)])
                self.dbg("vcur", self.vcur[0:16, :, 0, :], [16, 8, 128], BF16, [B("vcur")])
            if kind == "sample":
                self.attn_sample(l)
                if SS <= 3:
                    return
            else:
                self.attn_prompt(l, T, kind, sup)
                if not (kind == "sup" and sup == self.NSUP - 1):
                    self.store_kv(l, kind, sup)
            if kind == "meta" and l == 0:
                self.dbg("mixT", self.mixT[:, :, 0:16], [128, 8, 16], BF16, [self.B(f"mixT{i}") for i in range(8)])
            for f in self._deferred:
                f()
            self._deferred = []
            self.load_wA(1 - l)
            if kind == "sup" and sup == 0:
                self.dbg(f"mix{l}", self.mixT[:, :, 0:256], [128, 8, 256], BF16, [self.B(f"mixT{i}") for i in range(8)])
            self.out_proj(l, T)
            if kind == "sup" and sup == 0:
                self.dbg(f"xo{l}", self.xT[:, :, 0:256], [128, 8, 256], F32, [self.B("xT")])
            if kind == "meta" and l == 0:
                self.dbg("x1", self.xT[:, :, 0:16], [128, 8, 16], F32, [self.B("xT")])
            self.ffn(l, T)
            if kind == "meta" and l == 0:
                self.dbg("x2", self.xT[:, :, 0:16], [128, 8, 16], F32, [self.B("xT")])
        if kind == "sup" and sup == 0:
            self.dbg("xfin", self.xT[:, :, 0:256], [128, 8, 256], F32, [self.B("xT")])
        if kind == "sup":
            dst = [(self.yp[sup * TS + i * 128:sup * TS + (i + 1) * 128, :], 128) for i in range(2)]
            self.store_y(dst, T)
        elif kind == "sample":
            self.store_y([(self.ys[:, :], 128)], T)

    def build(self):
        nc = self.nc
        self.declare()
        self.ckvb2 = self.sb("ckvb2", [128, 2, 256], BF16)
        self.kper2 = self.sb("kper2", [32, 256], F32)
        self.kT2 = self.sb("kT2", [96, 8, 256], BF16)
        self.vcur2 = self.sb("vcur2", [128, 8, 2, 128], BF16)
        self.pool(lambda e: e.memset(self.vcur2[:], 1.0), [], [self.B("vcur2")])
        self.init_consts()
        self.cast_weights(layers=(0, 1))
        self._cast1_done = True
        import os
        stg = os.environ.get("MK_STAGES", "sample,meta,sup")
        if "sample" in stg:
            self.tile("sample")
        if "meta" in stg:
            self.tile("meta")
        if "sup" in stg:
            for s in range(int(os.environ.get("MK_NSUP", self.NSUP))):
                self.tile("sup", s)
        for l in range(2):
            self.dbg(f"kcend{l}", self.kc[l, 0, :, 0:128], [96, 128], BF16, [self.B(f"kc{l}_0")])
        import contextlib
        with contextlib.ExitStack() as st:
            eng_sems = {e: st.enter_context(nc.semaphore(f"sem_{e}")) for e in Prog.ENGS}
            dma_sems = [st.enter_context(nc.semaphore(f"dsem{i}")) for i in range(40)]
            block = st.enter_context(nc.Block())
            reg = {"pe": block.tensor, "act": block.scalar, "dve": block.vector, "pool": block.gpsimd, "sp": block.sync}
            self.P.emit(nc, reg, eng_sems, dma_sems)
        return nc


def _pack_weights(w_in, w_uq, w_uk, w_uv, w_pool, w_o, w_gate, w_up, w_down):
    blob = np.zeros((2, 128, WTOT), np.float32)
    for l in range(2):
        wi = np.concatenate([w_in[l], w_in[l][:, 1152 + 16:1152 + 32], w_in[l][:, 1152:1152 + 16]], axis=1)
        blob[l, :, O_WIN:O_WIN + 8 * WIN_W] = wi.reshape(8, 128, WIN_W).transpose(1, 0, 2).reshape(128, -1)
        uq = w_uq[l].reshape(3, 128, 8, 96)
        uq2 = np.zeros((3, 128, 8, 192), np.float32)
        uq2[..., 0:96] = uq
        uq2[..., 96 + 64:96 + 80] = uq[..., 80:96]
        uq2[..., 96 + 80:96 + 96] = uq[..., 64:80]
        blob[l, :, O_WUQ:O_WUQ + 3 * 8 * 192] = uq2.transpose(1, 0, 2, 3).reshape(128, -1)
        uk = w_uk[l].reshape(2, 128, 8, 64)
        blob[l, :, O_WUK:O_WUK + 2 * 8 * 64] = uk.transpose(1, 0, 2, 3).reshape(128, -1)
        blob[l, :, O_WUV:O_WUV + 1024] = w_uv[l].reshape(2, 128, 512).transpose(1, 0, 2).reshape(128, -1)
        blob[l, :, O_WPOOL:O_WPOOL + 512] = w_pool[l].transpose(1, 0, 2).reshape(128, -1)
        blob[l, :, O_WO:O_WO + 8192] = w_o[l].reshape(8, 128, 1024).transpose(1, 0, 2).reshape(128, -1)
        g = w_gate[l].reshape(8, 128, NFF, 128)
        u = w_up[l].reshape(8, 128, NFF, 128)
        gu = np.stack([g, u], axis=3)
        blob[l, :, O_GU:O_GU + NFF * GU_PIECE] = gu.transpose(1, 2, 0, 3, 4).reshape(128, -1)
        dn = w_down[l].reshape(NFF, 128, 1024)
        blob[l, :, O_DN:O_DN + NFF * DN_PIECE] = dn.transpose(1, 0, 2).reshape(128, -1)
    return blob


def _pack_gains(norm_mix, norm_ffn, q_a_norm, kv_a_norm, pool_scale, q_norm, k_norm):
    g = np.zeros((128, NG), np.float32)
    g[:, 0:16] = norm_mix.reshape(2, 8, 128).transpose(2, 0, 1).reshape(128, 16)
    g[:, 16:32] = norm_ffn.reshape(2, 8, 128).transpose(2, 0, 1).reshape(128, 16)
    g[:, 32:38] = q_a_norm.reshape(2, 3, 128).transpose(2, 0, 1).reshape(128, 6)
    g[:, 38:42] = kv_a_norm.reshape(2, 2, 128).transpose(2, 0, 1).reshape(128, 4)
    g[:, 42:50] = pool_scale.reshape(2, 4, 128).transpose(2, 0, 1).reshape(128, 8)
    g[0:96, 50:52] = q_norm.T
    g[0:96, 52:54] = k_norm.T
    return g


def _rope_table(LP, pos_sample):
    half = DR // 2
    inv = (np.float32(10000.0) ** (-np.arange(half, dtype=np.float32) / np.float32(half))).astype(np.float32)
    pos = np.concatenate([np.arange(LP, dtype=np.float32), np.tile(pos_sample.astype(np.float32), 4)])
    ang = (pos[:, None] * inv[None, :]).astype(np.float32)
    cos = np.cos(ang).astype(np.float32).T
    sin = np.sin(ang).astype(np.float32).T
    tab = np.zeros((96, 2, pos.shape[0]), np.float32)
    tab[0:64, 0] = 1.0
    tab[64:80, 0] = cos
    tab[80:96, 0] = cos
    tab[64:80, 1] = -sin
    tab[80:96, 1] = sin
    return tab


_CACHE = {}


def kernel(x_prompt, x_sample, cache_latent, cache_krope, state_pool, meta_tokens,
           norm_mix, w_in, q_a_norm, w_uq, kv_a_norm, w_uk, w_uv, q_norm, k_norm,
           w_pool, pool_scale, w_o, norm_ffn, w_gate, w_up, w_down):
    f = lambda a: np.ascontiguousarray(np.asarray(a), dtype=np.float32)
    x_prompt, x_sample, cache_latent, cache_krope, state_pool, meta_tokens = map(
        f, (x_prompt, x_sample, cache_latent, cache_krope, state_pool, meta_tokens))
    NB, SEQ, _ = x_prompt.shape
    NROWS = cache_latent.shape[2]
    n = 8
    assert NB == n and x_sample.shape[0] == 4 * n and x_sample.shape[1] == 32 and SEQ % TS == 0
    LP = NMETA + SEQ
    key = (SEQ, NROWS)
    if key not in _CACHE:
        nc = bass.Bass("TRN2", target_bir_lowering=False)
        _CACHE[key] = Builder(SEQ, NROWS, nc).build()
    nc = _CACHE[key]
    blob = _pack_weights(*map(f, (w_in, w_uq, w_uk, w_uv, w_pool, w_o, w_gate, w_up, w_down)))
    gains = _pack_gains(*map(f, (norm_mix, norm_ffn, q_a_norm, kv_a_norm, pool_scale, q_norm, k_norm)))
    rope = _rope_table(LP, NROWS + np.arange(32))
    rcnt = np.zeros((128, 4, 16), np.float32)
    for g, w in enumerate(WINS):
        rcnt[:, g, :] = 1.0 / np.minimum(np.arange(16) + 1, w).astype(np.float32)
    ident = np.eye(128, dtype=np.float32)
    in_maps = []
    for c in range(n):
        in_maps.append({
            "xp": x_prompt[c], "meta": meta_tokens, "xs": x_sample[4 * c:4 * c + 4].reshape(128, D),
            "clat": np.ascontiguousarray(cache_latent[:, 4 * c:4 * c + 4]),
            "ckr": np.ascontiguousarray(cache_krope[:, 4 * c:4 * c + 4]),
            "spool": np.ascontiguousarray(state_pool[:, 4 * c:4 * c + 4]),
            "wblob": blob, "gains": gains, "rope": rope, "rcnt": rcnt, "ident": ident,
        })
    res = run_bass_kernel_spmd(nc, in_maps, core_ids=list(range(n)))
    R = res.results
    kernel.last = R
    y_prompt = np.stack([R[c]["yp"] for c in range(n)], axis=0)
    y_sample = np.concatenate([R[c]["ys"].reshape(4, 32, D) for c in range(n)], axis=0)
    lat_p = np.stack([R[c]["latp"] for c in range(n)], axis=1)
    kr_p = np.stack([R[c]["krp"] for c in range(n)], axis=1)
    pool_p = np.stack([R[c]["poolp"] for c in range(n)], axis=1)
    lat_s = np.concatenate([R[c]["lats"] for c in range(n)], axis=1)
    kr_s = np.concatenate([R[c]["krs"] for c in range(n)], axis=1)
    pool_s = np.concatenate([R[c]["pools"] for c in range(n)], axis=1)
    return (y_prompt, y_sample, lat_p, kr_p, pool_p, lat_s, kr_s, pool_s)
```

```python
import numpy as np
import concourse.bass as bass
import concourse.mybir as mybir
from concourse.bass_utils import run_bass_kernel_spmd

F32 = mybir.dt.float32
BF16 = mybir.dt.bfloat16
AF = mybir.ActivationFunctionType
ALU = mybir.AluOpType

D = 1024
NCH = 8
H = 8
DQK = 96
QR = 384
KVR = 256
DR = 32
DFF = 2816
NFF = 22
NMETA = 16
TS = 256
EPS = 1e-6
WINS = (2, 4, 8, 16)
SCALE = DQK ** -0.5

O_WIN = 0
WIN_W = 1216
O_WUQ = O_WIN + 8 * WIN_W
O_WUK = O_WUQ + 3 * 8 * 192
O_WUV = O_WUK + 2 * 8 * 64
O_WPOOL = O_WUV + 2 * 512
WA_TOT = O_WPOOL + 4 * 128
O_WO = WA_TOT
O_GU = O_WO + 8 * 1024
GU_PIECE = 8 * 2 * 128
NGU = 4
O_DN = O_GU + NFF * GU_PIECE
DN_PIECE = 1024
NDN = 4
NAT = 10
WTOT = O_DN + NFF * DN_PIECE
CAST_PIECE = 724
assert WTOT % CAST_PIECE == 0
NG = 54


class Buf:
    __slots__ = ("name", "w", "r_eng", "r_dma", "excl")

    def __init__(self, name):
        self.name = name
        self.excl = name.startswith("ps")
        self.w = None
        self.r_eng = {}
        self.r_dma = []


class Op:
    __slots__ = ("eng", "is_dma", "fn", "deps", "signal", "sem", "val")


class Prog:
    ENGS = ("pe", "act", "dve", "pool", "sp")

    def __init__(self):
        self.ops = []

    def add(self, eng, fn, reads=(), writes=(), dma=False):
        op = Op()
        op.eng = eng
        op.is_dma = dma
        op.fn = fn
        op.signal = False
        op.sem = None
        op.val = 0
        deps = set()
        for b in reads:
            if b.w is not None:
                deps.add(b.w)
            if b.excl:
                deps.update(o for e2, o in b.r_eng.items() if e2 != eng)
        for b in writes:
            if b.w is not None:
                deps.add(b.w)
            deps.update(b.r_eng.values())
            deps.update(b.r_dma)
        deps.discard(op)
        if eng == "pe" and not dma:
            deps = [d for d in deps if d.is_dma or d.eng != "pe"]
        op.deps = list(deps)
        wset = set(id(b) for b in writes)
        for b in writes:
            b.w = op
            b.r_eng = {}
            b.r_dma = []
        for b in reads:
            if id(b) in wset:
                continue
            if dma:
                b.r_dma.append(op)
            else:
                b.r_eng[eng] = op
        self.ops.append(op)
        return op

    def emit(self, nc, block_engines, eng_sems, dma_sems):
        ops = self.ops
        npool = len(dma_sems)
        last_on_sem = [None] * npool
        cnt_on_sem = [0] * npool
        k = 0
        for op in ops:
            if op.is_dma:
                i = k % npool
                k += 1
                cnt_on_sem[i] += 1
                op.sem = dma_sems[i]
                op.val = 16 * cnt_on_sem[i]
                if last_on_sem[i] is not None:
                    op.deps.append(last_on_sem[i])
                last_on_sem[i] = op
        for op in ops:
            for d in op.deps:
                d.signal = True
        cnt = {e: 0 for e in self.ENGS}
        for op in ops:
            if not op.is_dma and op.signal:
                cnt[op.eng] += 1
                op.sem = eng_sems[op.eng]
                op.val = cnt[op.eng]
        self.final_dma = [(dma_sems[i], 16 * cnt_on_sem[i]) for i in range(npool) if cnt_on_sem[i]]
        self.counts = cnt
        by_eng = {e: [] for e in self.ENGS}
        for op in ops:
            by_eng[op.eng].append(op)

        def make_stream(eng):
            def stream(e):
                known = {}
                for op in by_eng[eng]:
                    for d in op.deps:
                        key = id(d.sem)
                        if known.get(key, 0) < d.val:
                            e.wait_ge(d.sem, d.val)
                            known[key] = d.val
                    ins = op.fn(e)
                    if op.is_dma:
                        ins.then_inc(op.sem, 16)
                    elif op.signal:
                        ins.then_inc(op.sem, 1)
                if eng == "sp":
                    for sem, v in self.final_dma:
                        if known.get(id(sem), 0) < v:
                            e.wait_ge(sem, v)
            return stream

        for eng in self.ENGS:
            if by_eng[eng] or eng == "sp":
                block_engines[eng](make_stream(eng))


class Builder:
    def __init__(self, SEQ, NROWS, nc):
        self.nc = nc
        self.P = Prog()
        self.SEQ = SEQ
        self.NROWS = NROWS
        self.NSUP = SEQ // TS
        self.LP = NMETA + SEQ
        self.bufs = {}
        self._psrot = 0
        self._ps8 = 0
        self._ps4 = 0
        self._deferred = []

    def B(self, name):
        b = self.bufs.get(name)
        if b is None:
            b = Buf(name)
            self.bufs[name] = b
        return b

    def sb(self, name, shape, dt):
        return self.nc.alloc_sbuf_tensor(name, list(shape), dt)

    def pe(self, fn, r, w):
        return self.P.add("pe", fn, r, w)

    def act(self, fn, r, w):
        return self.P.add("act", fn, r, w)

    def dve(self, fn, r, w):
        return self.P.add("dve", fn, r, w)

    def pool(self, fn, r, w):
        return self.P.add("pool", fn, r, w)

    def dma(self, out, in_, r, w, eng="sp"):
        return self.P.add(eng, lambda e: e.dma_start(out=out, in_=in_), r, w, dma=True)

    def dbg(self, name, ap, shape, dt, bufs):
        import os
        if not os.environ.get("MK_DEBUG"):
            return
        t = self.nc.dram_tensor("dbg_" + name, list(shape), dt, kind="ExternalOutput").ap()
        self.dma(t, ap, bufs, [self.B("dbg_" + name)])

    def psum8(self):
        i = self._ps8 % 8
        self._ps8 += 1
        return self.ps[i], self.psb[i]

    def psum(self):
        i = self._psrot % 6
        self._psrot += 1
        return self.ps[i], self.psb[i]

    def declare(self):
        nc = self.nc
        SEQ, NROWS, LP = self.SEQ, self.NROWS, self.LP
        di = lambda n, s: nc.dram_tensor(n, list(s), F32, kind="ExternalInput").ap()
        do = lambda n, s: nc.dram_tensor(n, list(s), F32, kind="ExternalOutput").ap()
        self.xp = di("xp", [SEQ, D])
        self.meta = di("meta", [NMETA, D])
        self.xs = di("xs", [128, D])
        self.clat = di("clat", [2, 4, NROWS, KVR])
        self.ckr = di("ckr", [2, 4, NROWS, DR])
        self.spool = di("spool", [2, 4, 15, 512])
        self.wblob = di("wblob", [2, 128, WTOT])
        self.gains_d = di("gains", [128, NG])
        self.NPOS = LP + 128
        self.rope_d = di("rope", [96, 2, self.NPOS])
        self.rcnt_d = di("rcnt", [128, 4, 16])
        self.ident_d = di("ident", [128, 128])
        self.yp = do("yp", [SEQ, D])
        self.ys = do("ys", [128, D])
        self.latp = do("latp", [2, LP, KVR])
        self.krp = do("krp", [2, LP, DR])
        self.poolp = do("poolp", [2, 15, 512])
        self.lats = do("lats", [2, 4, 32, KVR])
        self.krs = do("krs", [2, 4, 32, DR])
        self.pools = do("pools", [2, 4, 15, 512])
        self.wsc = nc.dram_tensor("wsc", [2, 128, WTOT], BF16, kind="Internal").ap()
        self.NKT = 1 + SEQ // 128
        self.kc = nc.dram_tensor("kc", [2, H, 96, self.NKT * 128], BF16, kind="Internal").ap()
        self.vc = nc.dram_tensor("vc", [2, H, 128, self.NKT, 128], BF16, kind="Internal").ap()

        self.ident = self.sb("identb", [128, 128], F32)
        self.ones = self.sb("ones", [128, 128], BF16)
        self.epsb = self.sb("epsb", [128, 1], F32)
        self.gains = self.sb("gainsb", [128, NG], F32)
        self.rcnt = self.sb("rcntb", [128, 4, 16], F32)
        self.ps = [nc.alloc_psum_tensor(f"ps{i}", [128, 512], F32) for i in range(8)]
        self.psb = [self.B(f"ps{i}") for i in range(8)]
        self.cstf = [self.sb(f"cstf{i}", [128, CAST_PIECE], F32) for i in range(2)]
        self.wA = self.sb("wA", [128, WA_TOT], BF16)
        self.wO = self.sb("wO", [128, 8 * 1024], BF16)
        self.wGU = [self.sb(f"wGU{i}", [128, GU_PIECE], BF16) for i in range(NGU)]
        self.wDN = [self.sb(f"wDN{i}", [128, DN_PIECE], BF16) for i in range(NDN)]
        T = TS
        self.xT = self.sb("xT", [128, 8, T], F32)
        self.hT = self.sb("hT", [128, 8, T], BF16)
        self.sq = self.sb("sq", [128, 8, T], BF16)
        self.rstd = self.sb("rstd", [128, 512], F32)
        self.rstd2 = self.sb("rstd2", [128, 512], F32)
        self.xin = self.sb("xin", [128, 2, D], F32)
        self.uext = self.sb("uext", [128, 4 * (15 + T + 60)], F32)
        self.ptmp = [self.sb(f"ptmp{i}", [128, 15 + T + 60], F32) for i in range(2)]
        self.hist = [self.sb(f"hist{l}", [128, 4, 15], F32) for l in range(2)]
        self.dT = self.sb("dT", [128, 4, T], BF16)
        self.qlat = self.sb("qlat", [128, 3, T], F32)
        self.qn = self.sb("qn", [128, 3, T], BF16)
        self.ckv = self.sb("ckv", [128, 2, T], F32)
        self.ckvb = self.sb("ckvb", [128, 2, T], BF16)
        self.kpt = self.sb("kpt", [32, 2, T], F32)
        self.kper = self.sb("kper", [32, T], F32)
        self.C2 = self.sb("C2", [96, 2, T], F32)
        self.S2 = self.sb("S2", [96, 2, T], F32)
        self.kC = self.sb("kC", [32, T], F32)
        self.kS = self.sb("kS", [32, T], F32)
        self.tq = [self.sb(f"tq{i}", [96, 2, T], F32) for i in range(2)]
        self.t2 = self.sb("t2", [96, 2, T], F32)
        self.qT = self.sb("qT", [96, 8, T], BF16)
        self.kT = self.sb("kT", [96, 8, T], BF16)
        self.vcur = self.sb("vcur", [128, 8, 2, 128], BF16)
        self.vsm = self.sb("vsm", [32, 8, 128], BF16)
        self.mixT = self.sb("mixT", [128, 8, T], BF16)
        self.aT = self.sb("aT", [128, NAT, T], BF16)
        self.sg = [self.sb(f"sg{i}", [128, T], F32) for i in range(2)]
        self.kblk = [self.sb(f"kblk{i}", [96, 1024], BF16) for i in range(3)]
        self.vblk = [self.sb(f"vblk{i}", [128, 8, 128], BF16) for i in range(3)]
        self.pT = [self.sb(f"pT{i}", [128, 512], BF16) for i in range(4)]
        self.rl = [self.sb(f"rl{i}", [128, T], F32) for i in range(2)]
        self.otok = self.sb("otok", [128, 2, KVR + DR], F32)
        self.opool = self.sb("opool", [16, 512], F32)
        self.clin = self.sb("clin", [128, 2, KVR], F32)
        self.ckin = self.sb("ckin", [128, 2, DR], F32)
        self._pT = 0
        self._blk = 0
        self._sg = 0
        self._o = 0

    def init_consts(self):
        B = self.B
        self.dma(self.ident[:], self.ident_d[:, :], [], [B("ident")])
        self.dma(self.gains[:], self.gains_d[:, :], [], [B("gains")])
        self.dma(self.rcnt[:], self.rcnt_d[:, :, :], [], [B("rcnt")])
        self.pool(lambda e: e.memset(self.ones[:], 1.0), [], [B("ones")])
        self.pool(lambda e: e.memset(self.epsb[:], EPS), [], [B("epsb")])
        self.pool(lambda e: e.memset(self.vcur[:], 1.0), [], [B("vcur")])
        self.pool(lambda e: e.memset(self.vsm[:], 1.0), [], [B("vsm")])
        for i in range(2):
            self.pool(lambda e, l=i: e.memset(self.hist[l][:], 0.0), [], [B(f"hist{i}")])

    def cast_weights(self, layers=(0, 1)):
        B = self.B
        n = WTOT // CAST_PIECE
        CP = CAST_PIECE
        flat = lambda t: t[:, :, :].rearrange("p a b -> p (a b)")
        xinf, xTf, qlf = flat(self.xin), flat(self.xT), flat(self.qlat)
        fst = [(xinf[:, 0:CP], [B("xin0"), B("xin1")]), (xinf[:, CP:2 * CP], [B("xin0"), B("xin1")]),
               (xTf[:, 0:CP], [B("xT")]), (xTf[:, CP:2 * CP], [B("xT")]),
               (qlf[:, 0:CP], [B("qlat")]),
               (self.uext[:, 0:CP], [B(f"uext{g}") for g in range(4)]),
               (self.cstf[0][:, :], [B("cstf0")]), (self.cstf[1][:, :], [B("cstf1")])]
        bst = []
        aflat = flat(self.aT)
        for i in range(3):
            c0, c1 = i * CP, (i + 1) * CP
            bst.append((aflat[:, c0:c1], [B(f"aT{c}") for c in range(c0 // TS, (c1 - 1) // TS + 1)]))
        for nm, t in (("hT", self.hT), ("mixT", self.mixT)):
            f = flat(t)
            for i in range(2):
                c0, c1 = i * CP, (i + 1) * CP
                bst.append((f[:, c0:c1], [B(f"{nm}{c}") for c in range(c0 // TS, (c1 - 1) // TS + 1)]))
        sqf = flat(self.sq)
        for i in range(2):
            bst.append((sqf[:, i * CP:(i + 1) * CP], [B("sq"), B("sqh0"), B("sqh1")]))
        k = 0
        for l in layers:
            for i in range(n):
                fa, fb = fst[k % len(fst)]
                ba, bb = bst[k % len(bst)]
                c0 = i * CP
                self.dma(fa, self.wblob[l, :, c0:c0 + CP], [], fb)
                eng = ("dve", "act")[k % 2]
                if eng == "act":
                    self.act(lambda e, fa=fa, ba=ba: e.activation(out=ba, in_=fa, func=AF.Copy), fb, bb)
                else:
                    self.P.add(eng, lambda e, fa=fa, ba=ba: e.tensor_copy(out=ba, in_=fa), fb, bb)
                self.dma(self.wsc[l, :, c0:c0 + CP], ba, bb, [B(f"wsc{l}_{i}")])
                k += 1

    def wscb(self, l, c0, c1):
        return [self.B(f"wsc{l}_{i}") for i in range(c0 // CAST_PIECE, (c1 - 1) // CAST_PIECE + 1)]

    def load_wA(self, l):
        self.dma(self.wA[:], self.wsc[l, :, 0:WA_TOT], self.wscb(l, 0, WA_TOT), [self.B("wA")])

    def load_wO(self, l):
        self.dma(self.wO[:], self.wsc[l, :, O_WO:O_WO + 8192], self.wscb(l, O_WO, O_WO + 8192), [self.B("wO")])

    def load_gu(self, l, pc):
        s = pc % NGU
        o = O_GU + pc * GU_PIECE
        self.dma(self.wGU[s][:], self.wsc[l, :, o:o + GU_PIECE], self.wscb(l, o, o + GU_PIECE), [self.B(f"wGU{s}")])

    def load_dn(self, l, m):
        s = m % NDN
        o = O_DN + m * DN_PIECE
        self.dma(self.wDN[s][:], self.wsc[l, :, o:o + DN_PIECE], self.wscb(l, o, o + DN_PIECE), [self.B(f"wDN{s}")])

    def rstd_chain(self, ps_ap, psb, out_ap, outb, n, tmp_ap, tmpb):
        np_ = ps_ap.shape[0]
        self.act(lambda e: e.activation(out=tmp_ap, in_=ps_ap, func=AF.Ln, scale=1.0 / n, bias=self.epsb[0:np_, 0:1]),
                 [psb, self.B("epsb")], [tmpb])
        self.act(lambda e: e.activation(out=out_ap, in_=tmp_ap, func=AF.Exp, scale=-0.5), [tmpb], [outb])

    def norm_x(self, T, gcol):
        B = self.B
        xT, hT, sq = self.xT, self.hT, self.sq
        self.act(lambda e: e.activation(out=sq[:, :, 0:T], in_=xT[:, :, 0:T], func=AF.Square), [B("xT")], [B("sq")])
        ps, psb = self.psum()
        for k in range(8):
            self.pe(lambda e, k=k: e.matmul(ps[:, 0:T], self.ones[:, :], sq[:, k, 0:T], start=(k == 0), stop=(k == 7)),
                    [B("sq"), B("ones")], [psb])
        self.rstd_chain(ps[:, 0:T], psb, self.rstd[:, 0:T], B("rstd"), float(D), self.rstd2[:, 0:T], B("rstd2"))
        for k in range(8):
            self.dve(lambda e, k=k: e.scalar_tensor_tensor(
                out=hT[:, k, 0:T], in0=xT[:, k, 0:T], scalar=self.gains[:, gcol + k:gcol + k + 1],
                in1=self.rstd[:, 0:T], op0=ALU.mult, op1=ALU.mult), [B("xT"), B("rstd"), B("gains")], [B(f"hT{k}")])

    def hT_bufs(self):
        return [self.B(f"hT{k}") for k in range(8)]

    def load_x(self, src_rows, T):
        B = self.B
        col = 0
        for ti, (ap, n) in enumerate(src_rows):
            s = ti % 2
            self.dma(self.xin[0:n, s, :], ap, [], [B(f"xin{s}")])
            for k4 in range(2):
                ps, psb = self.psum()
                for kk in range(4):
                    k = k4 * 4 + kk
                    self.pe(lambda e, k=k, kk=kk, n=n, s=s, ps=ps: e.transpose(
                        ps[:, kk * 128:kk * 128 + n], self.xin[0:n, s, k * 128:(k + 1) * 128], self.ident[0:n, 0:n]),
                        [B(f"xin{s}"), B("ident")], [psb])
                for kk in range(4):
                    k = k4 * 4 + kk
                    self.act(lambda e, k=k, kk=kk, n=n, col=col, ps=ps: e.activation(
                        out=self.xT[:, k, col:col + n], in_=ps[:, kk * 128:kk * 128 + n], func=AF.Copy),
                        [psb], [B("xT")])
            col += n

    def store_y(self, dst_rows, T):
        B = self.B
        col = 0
        for ti, (ap, n) in enumerate(dst_rows):
            s = ti % 2
            for k4 in range(2):
                ps, psb = self.psum()
                for kk in range(4):
                    k = k4 * 4 + kk
                    self.pe(lambda e, k=k, kk=kk, n=n, col=col, ps=ps: e.transpose(
                        ps[0:n, kk * 128:(kk + 1) * 128], self.xT[:, k, col:col + n], self.ident[:, :]),
                        [B("xT"), B("ident")], [psb])
                self.act(lambda e, k4=k4, n=n, s=s, ps=ps: e.activation(
                    out=self.xin[0:n, s, k4 * 512:(k4 + 1) * 512], in_=ps[0:n, :], func=AF.Copy),
                    [psb], [B(f"xin{s}")])
            self.dbg(f"yst{self._psrot}", self.xin[0:n, s, :], [n, 1024], F32, [B(f"xin{s}")])
            self.dma(ap, self.xin[0:n, s, :], [B(f"xin{s}")], [B("yout")])
            col += n

    def mixer_front(self, l, T, segs, tok_tiles, pos0, kind, sup):
        B = self.B
        nseg, Tseg = segs
        L = 15 + Tseg
        wA, hT = self.wA, self.hT
        hb = self.hT_bufs()
        G = self.gains

        def win(k, c0, m):
            return wA[:, O_WIN + k * WIN_W + c0:O_WIN + k * WIN_W + c0 + m]

        for hh in range(2):
            self.dma(self.C2[:, hh, 0:T], self.rope_d[:, 0, pos0:pos0 + T], [], [B("C2")])
            self.dma(self.S2[:, hh, 0:T], self.rope_d[:, 1, pos0:pos0 + T], [], [B("S2")])
        self.dma(self.kC[:, 0:T], self.rope_d[64:96, 0, pos0:pos0 + T], [], [B("kC")])
        self.dma(self.kS[:, 0:T], self.rope_d[64:96, 1, pos0:pos0 + T], [], [B("kS")])

        uext = self.uext
        W4 = nseg * L

        def uview(g):
            return uext[:, g * W4:(g + 1) * W4].rearrange("p (s l) -> p s l", s=nseg)

        for g in range(4):
            ps, psb = self.psum()
            for k in range(8):
                self.pe(lambda e, k=k, g=g, ps=ps: e.matmul(ps[:, 0:T], win(k, g * 128, 128), hT[:, k, 0:T],
                                                            start=(k == 0), stop=(k == 7)), [hb[k], B("wA")], [psb])
            self.act(lambda e, g=g, ps=ps: e.activation(
                out=uview(g)[:, :, 15:L], in_=ps[:, 0:T].rearrange("p (s t) -> p s t", s=nseg), func=AF.Copy),
                [psb], [B(f"uext{g}")])
        ps_ssq, psb_ssq = self.psum()
        for j in range(3):
            ps, psb = self.psum()
            for k in range(8):
                self.pe(lambda e, k=k, j=j, ps=ps: e.matmul(ps[:, 0:T], win(k, 512 + j * 128, 128), hT[:, k, 0:T],
                                                            start=(k == 0), stop=(k == 7)), [hb[k], B("wA")], [psb])
            self.act(lambda e, j=j, ps=ps: e.activation(out=self.qlat[:, j, 0:T], in_=ps[:, 0:T], func=AF.Copy),
                     [psb], [B("qlat")])
            self.act(lambda e, j=j, ps=ps: e.activation(out=self.sq[:, j, 0:T], in_=ps[:, 0:T], func=AF.Square),
                     [psb], [B("sq")])
        for j in range(3):
            self.pe(lambda e, j=j, ps_ssq=ps_ssq: e.matmul(ps_ssq[:, 0:T], self.ones[:, :], self.sq[:, j, 0:T],
                                            start=(j == 0), stop=(j == 2)), [B("sq"), B("ones")], [psb_ssq])
        self.rstd_chain(ps_ssq[:, 0:T], psb_ssq, self.rstd[:, 0:T], B("rstd"), float(QR), self.rstd2[:, 0:T], B("rstd2"))
        for j in range(3):
            self.dve(lambda e, j=j: e.scalar_tensor_tensor(
                out=self.qn[:, j, 0:T], in0=self.qlat[:, j, 0:T], scalar=G[:, 32 + l * 3 + j:32 + l * 3 + j + 1],
                in1=self.rstd[:, 0:T], op0=ALU.mult, op1=ALU.mult), [B("qlat"), B("rstd"), B("gains")], [B("qn")])
        if kind == "meta" and l == 0:
            self.dbg("qlat", self.qlat[:, :, 0:16], [128, 3, 16], F32, [B("qlat")])
            self.dbg("rstdq", self.rstd[:, 0:16], [128, 16], F32, [B("rstd")])
            self.dbg("sqq", self.sq[:, 0:3, 0:16], [128, 3, 16], BF16, [B("sq")])
        ps_ssq, psb_ssq = self.psum()
        for j in range(2):
            ps, psb = self.psum()
            for k in range(8):
                self.pe(lambda e, k=k, j=j, ps=ps: e.matmul(ps[:, 0:T], win(k, 896 + j * 128, 128), hT[:, k, 0:T],
                                                            start=(k == 0), stop=(k == 7)), [hb[k], B("wA")], [psb])
            self.act(lambda e, j=j, ps=ps: e.activation(out=self.ckv[:, j, 0:T], in_=ps[:, 0:T], func=AF.Copy),
                     [psb], [B("ckv")])
            self.act(lambda e, j=j, ps=ps: e.activation(out=self.sq[:, 4 + j, 0:T], in_=ps[:, 0:T], func=AF.Square),
                     [psb], [B("sq")])
        for j in range(2):
            self.pe(lambda e, j=j, ps_ssq=ps_ssq: e.matmul(ps_ssq[:, 0:T], self.ones[:, :], self.sq[:, 4 + j, 0:T],
                                            start=(j == 0), stop=(j == 1)), [B("sq"), B("ones")], [psb_ssq])
        self.rstd_chain(ps_ssq[:, 0:T], psb_ssq, self.rstd[:, 0:T], B("rstd"), float(KVR), self.rstd2[:, 0:T], B("rstd2"))
        for j in range(2):
            self.dve(lambda e, j=j: e.scalar_tensor_tensor(
                out=self.ckv[:, j, 0:T], in0=self.ckv[:, j, 0:T], scalar=G[:, 38 + l * 2 + j:38 + l * 2 + j + 1],
                in1=self.rstd[:, 0:T], op0=ALU.mult, op1=ALU.mult), [B("rstd"), B("gains")], [B("ckv")])
        self.dve(lambda e: e.tensor_copy(out=self.ckvb[:, :, 0:T], in_=self.ckv[:, :, 0:T]), [B("ckv")], [B("ckvb")])
        ps, psb = self.psum()
        for j in range(2):
            for k in range(8):
                self.pe(lambda e, k=k, j=j, ps=ps: e.matmul(ps[0:32, j * T:(j + 1) * T], win(k, 1152 + 32 * j, 32),
                                                            hT[:, k, 0:T], start=(k == 0), stop=(k == 7)),
                        [hb[k], B("wA")], [psb])
        self.dve(lambda e, ps=ps: e.tensor_tensor(out=self.kpt[:, 0, 0:T], in0=ps[0:32, 0:T], in1=self.kC[:, 0:T],
                                                  op=ALU.mult), [psb, B("kC")], [B("kpt")])
        self.dve(lambda e, ps=ps: e.tensor_tensor(out=self.kpt[:, 1, 0:T], in0=ps[0:32, T:2 * T], in1=self.kS[:, 0:T],
                                                  op=ALU.mult), [psb, B("kS")], [B("kpt")])
        self.dve(lambda e: e.tensor_tensor(out=self.kper[:, 0:T], in0=self.kpt[:, 0, 0:T], in1=self.kpt[:, 1, 0:T],
                                            op=ALU.add), [B("kpt")], [B("kper")])

        if kind == "sample":
            for s_ in range(nseg):
                self.dma(self.opool[0:15, :], self.spool[l, s_, :, :], [], [B("opool")])
                ps, psb = self.psum()
                for g in range(4):
                    self.pe(lambda e, g=g, ps=ps: e.transpose(ps[:, g * 16:g * 16 + 15], self.opool[0:15, g * 128:(g + 1) * 128],
                                                              self.ident[0:15, 0:15]), [B("opool"), B("ident")], [psb])
                for g in range(4):
                    self.act(lambda e, g=g, s_=s_, ps=ps: e.activation(out=uview(g)[:, s_, 0:15], in_=ps[:, g * 16:g * 16 + 15],
                                                                       func=AF.Copy), [psb], [B(f"uext{g}")])
        else:
            for g in range(4):
                self.pool(lambda e, g=g: e.tensor_copy(out=uview(g)[:, 0, 0:15], in_=self.hist[l][:, g, :]),
                          [B(f"hist{l}")], [B(f"uext{g}")])
        for g in range(4):
            w = WINS[g]
            cur = uview(g)
            curb = B(f"uext{g}")
            sh = 1
            step = 0
            while sh < w:
                lo = 2 * sh - 1
                dst = self.ptmp[step % 2][:, 0:W4].rearrange("p (s l) -> p s l", s=nseg)
                dstb = B(f"ptmp{step % 2}")
                self.pool(lambda e, cur=cur, dst=dst, lo=lo, sh=sh: e.tensor_tensor(
                    out=dst[:, :, lo:L], in0=cur[:, :, lo:L], in1=cur[:, :, lo - sh:L - sh], op=ALU.add), [curb], [dstb])
                cur, curb = dst, dstb
                sh *= 2
                step += 1
            dview = self.dT[:, g, 0:T].rearrange("p (s t) -> p s t", s=nseg)
            if kind == "meta":
                self.pool(lambda e, cur=cur, g=g: e.tensor_tensor(out=cur[:, 0, 15:L], in0=cur[:, 0, 15:L],
                                                                   in1=self.rcnt[:, g, :], op=ALU.mult), [B("rcnt")], [curb])
            else:
                self.pool(lambda e, cur=cur, w=w: e.tensor_scalar(out=cur[:, :, 15:L], in0=cur[:, :, 15:L], scalar1=1.0 / w,
                                                                   scalar2=None, op0=ALU.mult), [], [curb])
            self.pool(lambda e, cur=cur, g=g, dview=dview: e.tensor_tensor(
                out=dview[:, :, :], in0=cur[:, :, 15:L], in1=uview(g)[:, :, 15:L], op=ALU.subtract),
                [curb, B(f"uext{g}")], [B("dT")])
        if kind != "sample":
            for g in range(4):
                self.pool(lambda e, g=g: e.tensor_copy(out=self.hist[l][:, g, :], in_=uview(g)[:, 0, L - 15:L]),
                          [B(f"uext{g}")], [B(f"hist{l}")])

        def pool_section():
            for g in range(4):
                ps, psb = self.psum()
                self.pe(lambda e, g=g, ps=ps: e.matmul(ps[:, 0:T], wA[:, O_WPOOL + g * 128:O_WPOOL + (g + 1) * 128],
                                                       self.dT[:, g, 0:T], start=True, stop=True), [B("dT"), B("wA")], [psb])
                self.dve(lambda e, g=g, ps=ps: e.tensor_scalar(out=self.mixT[:, g, 0:T], in0=ps[:, 0:T],
                                                               scalar1=G[:, 42 + l * 4 + g:42 + l * 4 + g + 1], scalar2=None,
                                                               op0=ALU.mult), [psb, B("gains")], [B(f"mixT{g}")])
            if kind == "sample" or (kind == "sup" and sup == self.NSUP - 1):
                for s_ in range(nseg):
                    ps, psb = self.psum()
                    for g in range(4):
                        self.pe(lambda e, g=g, s_=s_, ps=ps: e.transpose(ps[0:15, g * 128:(g + 1) * 128], uview(g)[:, s_, L - 15:L],
                                                                         self.ident[:, :]), [B(f"uext{g}"), B("ident")], [psb])
                    self.act(lambda e, ps=ps: e.activation(out=self.opool[0:15, :], in_=ps[0:15, :], func=AF.Copy),
                             [psb], [B("opool")])
                    dst = self.pools[l, s_, :, :] if kind == "sample" else self.poolp[l, :, :]
                    self.dma(dst, self.opool[0:15, :], [B("opool")], [B("pout")])

        self._deferred.append(pool_section)

        def wuq(j, h, sw):
            o = O_WUQ + (j * 8 + h) * 192 + sw * 96
            return wA[:, o:o + 96]

        v3 = lambda ap: ap[0:96, 0:2 * T].rearrange("p (a t) -> p a t", a=2)

        def q_mm(hp):
            a, b = (0, 1) if hp % 2 == 0 else (2, 3)
            psA, psAb, psB, psBb = self.ps[a], self.psb[a], self.ps[b], self.psb[b]
            for hh in range(2):
                h = hp * 2 + hh
                for j in range(3):
                    self.pe(lambda e, h=h, hh=hh, j=j, psA=psA: e.matmul(
                        psA[0:96, hh * T:(hh + 1) * T], wuq(j, h, 0), self.qn[:, j, 0:T], start=(j == 0), stop=(j == 2)),
                        [B("qn"), B("wA")], [psAb])
                for j in range(3):
                    self.pe(lambda e, h=h, hh=hh, j=j, psB=psB: e.matmul(
                        psB[0:96, hh * T:(hh + 1) * T], wuq(j, h, 1), self.qn[:, j, 0:T], start=(j == 0), stop=(j == 2)),
                        [B("qn"), B("wA")], [psBb])

        def q_chain(hp):
            a, b = (0, 1) if hp % 2 == 0 else (2, 3)
            psA, psAb, psB, psBb = self.ps[a], self.psb[a], self.ps[b], self.psb[b]
            tb = self.tq[0]
            tbb = B("tq0")
            self.dve(lambda e, psA=psA, tb=tb: e.tensor_tensor(out=tb[:, :, 0:T], in0=v3(psA), in1=self.C2[:, :, 0:T],
                                                               op=ALU.mult), [psAb, B("C2")], [tbb])
            self.dve(lambda e, psB=psB: e.tensor_tensor(out=self.t2[:, :, 0:T], in0=v3(psB), in1=self.S2[:, :, 0:T],
                                                        op=ALU.mult), [psBb, B("S2")], [B("t2")])
            self.dve(lambda e, tb=tb: e.tensor_tensor(out=tb[:, :, 0:T], in0=tb[:, :, 0:T], in1=self.t2[:, :, 0:T],
                                                      op=ALU.add), [B("t2")], [tbb])
            self.norm_heads(tb, tbb, self.qT, hp, T, 50 + l, B(f"qT{hp}"), 4, 0)

        k_mm, k_chain = self.kgen_fns(l, T, self.ckvb, B("ckvb"), self.kper, B("kper"), self.kT, "kT", (5, 6, 7), False)
        q_mm(0)
        k_mm(0)
        q_mm(1)
        k_mm(1)
        for p in range(4):
            q_chain(p)
            k_chain(p)
            if p + 2 < 4:
                q_mm(p + 2)
                k_mm(p + 2)

        if kind != "sample":
            for ti, (c0, n) in enumerate(tok_tiles):
                ps, psb = self.psum()
                for j in range(2):
                    self.pe(lambda e, j=j, c0=c0, n=n, ps=ps: e.matmul(ps[0:n, :], self.ckvb[:, j, c0:c0 + n],
                                                                       wA[:, O_WUV + j * 512:O_WUV + (j + 1) * 512],
                                                                       start=(j == 0), stop=(j == 1)), [B("ckvb"), B("wA")], [psb])
                self.vevac(ps, psb, n, lambda par, ti=ti, n=n: self.vcur[0:n, par::2, ti, par * 64:par * 64 + 64], B("vcur"))

        for ti, (c0, n) in enumerate(tok_tiles):
            s = ti % 2
            ps, psb = self.psum()
            for j in range(2):
                self.pe(lambda e, j=j, c0=c0, n=n, ps=ps: e.transpose(ps[0:n, j * 128:(j + 1) * 128], self.ckv[:, j, c0:c0 + n],
                                                                      self.ident[:, :]), [B("ckv"), B("ident")], [psb])
            self.pe(lambda e, c0=c0, n=n, ps=ps: e.transpose(ps[0:n, 256:288], self.kper[:, c0:c0 + n], self.ident[0:32, 0:32]),
                    [B("kper"), B("ident")], [psb])
            self.act(lambda e, n=n, s=s, ps=ps: e.activation(out=self.otok[0:n, s, :], in_=ps[0:n, 0:288], func=AF.Copy),
                     [psb], [B(f"otok{s}")])
            if kind == "sample":
                self.dma(self.lats[l, ti, :, :], self.otok[0:32, s, 0:KVR], [B(f"otok{s}")], [B("lout")])
                self.dma(self.krs[l, ti, :, :], self.otok[0:32, s, KVR:KVR + DR], [B(f"otok{s}")], [B("lout")])
            else:
                r0 = (0 if kind == "meta" else NMETA + sup * TS) + c0
                self.dma(self.latp[l, r0:r0 + n, :], self.otok[0:n, s, 0:KVR], [B(f"otok{s}")], [B("lout")])
                self.dma(self.krp[l, r0:r0 + n, :], self.otok[0:n, s, KVR:KVR + DR], [B(f"otok{s}")], [B("lout")])

    def vevac(self, ps, psb, n, dst_fn, dstb, on_dve=False):
        for par in range(2):
            src = ps[0:n, :].rearrange("p (h d) -> p h d", d=64)[:, par::2, :]
            if on_dve:
                self.dve(lambda e, src=src, par=par: e.tensor_copy(out=dst_fn(par), in_=src), [psb], [dstb])
            else:
                self.act(lambda e, src=src, par=par: e.activation(out=dst_fn(par), in_=src, func=AF.Copy), [psb], [dstb])

    def norm_heads(self, src, srcb, dst, hp, T, gcol, dstb, bank, side):
        B = self.B
        sl = 4 if side == 0 else 6
        sqb = B(f"sqh{side}")
        rs, rsb = (self.rstd, B("rstd")) if side == 0 else (self.rstd2, B("rstd2"))
        self.act(lambda e: e.activation(out=self.sq[0:96, sl:sl + 2, 0:T], in_=src[:, :, 0:T], func=AF.Square), [srcb], [sqb, B("sq")])
        ps, psb = self.ps[bank], self.psb[bank]
        for hh in range(2):
            self.pe(lambda e, hh=hh, ps=ps: e.matmul(ps[0:96, hh * T:(hh + 1) * T], self.ones[0:96, 0:96],
                                                     self.sq[0:96, sl + hh, 0:T], start=True, stop=True),
                    [sqb, B("ones")], [psb])
        self.act(lambda e, ps=ps: e.activation(out=rs[0:96, 0:2 * T], in_=ps[0:96, 0:2 * T], func=AF.Ln, scale=1.0 / DQK,
                                               bias=self.epsb[0:96, 0:1]), [psb, B("epsb")], [rsb])
        self.act(lambda e: e.activation(out=rs[0:96, 0:2 * T], in_=rs[0:96, 0:2 * T], func=AF.Exp, scale=-0.5), [], [rsb])
        self.dve(lambda e: e.scalar_tensor_tensor(
            out=dst[:, 2 * hp:2 * hp + 2, 0:T], in0=src[:, :, 0:T], scalar=self.gains[0:96, gcol:gcol + 1],
            in1=rs[0:96, 0:2 * T].rearrange("p (a t) -> p a t", a=2), op0=ALU.mult, op1=ALU.mult),
            [srcb, rsb, B("gains")], [dstb])

    def kgen_fns(self, l, T, latb_t, latb, kper_t, kperb, dst, dstname, banks, on_dve):
        B = self.B
        wA = self.wA

        def k_mm(hp):
            bk = banks[hp % 2]
            ps, psb = self.ps[bk], self.psb[bk]
            for hh in range(2):
                h = hp * 2 + hh
                for j in range(2):
                    o = O_WUK + (j * 8 + h) * 64
                    self.pe(lambda e, hh=hh, j=j, o=o, ps=ps: e.matmul(ps[0:64, hh * T:(hh + 1) * T], wA[:, o:o + 64],
                                                                       latb_t[:, j, 0:T], start=(j == 0), stop=(j == 1)),
                            [latb, B("wA")], [psb])

        def k_chain(hp):
            bk = banks[hp % 2]
            ps, psb = self.ps[bk], self.psb[bk]
            kr = self.tq[1]
            krb = B("tq1")
            src = ps[0:64, 0:2 * T].rearrange("p (a t) -> p a t", a=2)
            if on_dve:
                self.dve(lambda e, kr=kr, src=src: e.tensor_copy(out=kr[0:64, :, 0:T], in_=src), [psb], [krb])
            else:
                self.act(lambda e, kr=kr, src=src: e.activation(out=kr[0:64, :, 0:T], in_=src, func=AF.Copy), [psb], [krb])
            self.norm_heads(kr, krb, dst, hp, T, 52 + l, B(f"{dstname}{hp}"), banks[2], 1)

        for hh in range(2):
            self.dve(lambda e, hh=hh: e.tensor_copy(out=self.tq[1][64:96, hh, 0:T], in_=kper_t[0:32, 0:T]),
                     [kperb], [B("tq1")])
        return k_mm, k_chain

    def kgen(self, l, T, latb_t, latb, kper_t, kperb, dst, dstname):
        k_mm, k_chain = self.kgen_fns(l, T, latb_t, latb, kper_t, kperb, dst, dstname, (0, 1, 2), True)
        k_mm(0)
        k_mm(1)
        k_chain(0)
        k_mm(2)
        k_chain(1)
        k_mm(3)
        k_chain(2)
        k_chain(3)

    def next_pT(self):
        i = self._pT % 4
        self._pT += 1
        return self.pT[i], self.B(f"pT{i}")

    def attn_finish(self, h, T, po, pob):
        B = self.B
        par = h % 2
        lo, hi = par * 64, par * 64 + 64
        olo, ohi = (1 - par) * 64, (1 - par) * 64 + 64
        i = self._o % 2
        self._o += 1
        rl, rlb = self.rl[i], B(f"rl{i}")
        self.act(lambda e: e.activation(out=rl[olo:ohi, 0:T], in_=po[olo:ohi, 0:T], func=AF.Ln), [pob], [rlb])
        self.act(lambda e: e.activation(out=rl[olo:ohi, 0:T], in_=rl[olo:ohi, 0:T], func=AF.Exp, scale=-1.0), [], [rlb])
        self.dve(lambda e: e.tensor_tensor(out=self.mixT[lo:hi, 4 + h // 2, 0:T], in0=po[lo:hi, 0:T], in1=rl[olo:ohi, 0:T],
                                           op=ALU.mult), [pob, rlb], [B(f"mixT{4 + h // 2}")])

    def attn_prefetch(self, l, kind, sup):
        B = self.B
        nctx_tiles = 0 if kind == "meta" else 1 + sup * (TS // 128)
        NBUF = len(self.kblk)
        items = []
        for h in range(H):
            t0 = 0
            while t0 < nctx_tiles:
                nt = min(8, nctx_tiles - t0)
                items.append((h, t0, nt))
                t0 += nt
        loaded = {}

        def load(i):
            h, t0, nt = items[i]
            bi = self._blk % NBUF
            self._blk += 1
            kb, vb = self.kblk[bi], self.vblk[bi]
            kbb, vbb = B(f"kblk{bi}"), B(f"vblk{bi}")
            deps = [B(f"kc{l}_{t}") for t in range(t0, t0 + nt)]
            vdeps = [B(f"vc{l}_{t}") for t in range(t0, t0 + nt)]
            self.dma(kb[:, 0:nt * 128], self.kc[l, h, :, t0 * 128:(t0 + nt) * 128], deps, [kbb])
            self.dma(vb[:, 0:nt, :], self.vc[l, h, :, t0:t0 + nt, :], vdeps, [vbb])
            loaded[i] = (kb, vb, kbb, vbb)

        nxt = 0
        for i in range(min(NBUF - 1, len(items))):
            load(i)
            nxt = i + 1
        self._att = (items, loaded, load, nxt)

    def attn_prompt(self, l, T, kind, sup):
        B = self.B
        NBUF = len(self.kblk)
        PF = NBUF - 1
        items, loaded, load, nxt = self._att
        pending = []
        cur_blk = [None]

        def try_prefetch(i):
            nonlocal nxt
            while nxt < len(items) and nxt <= i + PF and (nxt - NBUF) not in {b for _, b in pending}:
                load(nxt)
                nxt += 1

        ii = 0
        for h in range(H):
            po, pob = self.ps[6 + h % 2], self.psb[6 + h % 2]
            qh = self.qT[:, h, 0:T]
            qb = B(f"qT{h // 2}")
            hblocks = [i for i, it in enumerate(items) if it[0] == h]
            nsteps = sum((items[i][2] + 1) // 2 for i in hblocks) + 1
            si = 0

            def do_step(grp, si):
                ps, psb = self.psum()
                pT, pTb = self.next_pT()
                for gi, (k_ap, kbuf, v_ap, vbuf, nk, m) in enumerate(grp):
                    self.pe(lambda e, k_ap=k_ap, gi=gi, nk=nk, ps=ps, qh=qh: e.matmul(ps[0:nk, gi * T:(gi + 1) * T], k_ap, qh,
                                                                                     start=True, stop=True), [kbuf, qb], [psb])
                nkmax = max(g[4] for g in grp)
                w = len(grp) * T
                if len(grp) == 2 and grp[0][4] != grp[1][4]:
                    for gi, g in enumerate(grp):
                        self.act(lambda e, gi=gi, nk=g[4], ps=ps, pT=pT: e.activation(
                            out=pT[0:nk, gi * T:(gi + 1) * T], in_=ps[0:nk, gi * T:(gi + 1) * T], func=AF.Exp, scale=SCALE),
                            [psb], [pTb])
                else:
                    self.act(lambda e, nk=nkmax, w=w, ps=ps, pT=pT: e.activation(out=pT[0:nk, 0:w], in_=ps[0:nk, 0:w],
                                                                                 func=AF.Exp, scale=SCALE), [psb], [pTb])
                for gi, (k_ap, kbuf, v_ap, vbuf, nk, m) in enumerate(grp):
                    if m == 0:
                        self.pool(lambda e, pT=pT: e.memset(pT[64:128, 0:64], 0.0), [], [pTb])
                    elif m == 1:
                        self.pool(lambda e, pT=pT: e.memset(pT[0:64, T:T + 128], 0.0), [], [pTb])
                        self.pool(lambda e, pT=pT: e.memset(pT[64:128, T:T + 192], 0.0), [], [pTb])
                prev = []
                while len(pending) > 1:
                    prev.extend(pending.pop(0)[0])

                def pv(grp=grp, si=si, pT=pT, pTb=pTb, po=po, pob=pob, nsteps=nsteps):
                    for gi, (k_ap, kbuf, v_ap, vbuf, nk, m) in enumerate(grp):
                        st = (si == 0 and gi == 0)
                        sp_ = (si == nsteps - 1 and gi == len(grp) - 1)
                        self.pe(lambda e, v_ap=v_ap, gi=gi, nk=nk, pT=pT, st=st, sp_=sp_, po=po: e.matmul(
                            po[:, 0:T], v_ap, pT[0:nk, gi * T:(gi + 1) * T], start=st, stop=sp_), [vbuf, pTb], [pob])
                for f in prev:
                    f()
                pending.append(([pv], cur_blk[0]))

            for i in hblocks:
                _, t0, nt = items[i]
                cur_blk[0] = i
                try_prefetch(i)
                kb, vb, kbb, vbb = loaded.pop(i)
                for tt in range(0, nt, 2):
                    grp = []
                    for t in range(tt, min(tt + 2, nt)):
                        nk = NMETA if (t0 + t) == 0 else 128
                        grp.append((kb[:, t * 128:t * 128 + nk], kbb, vb[0:nk, t, :], vbb, nk, None))
                    do_step(grp, si)
                    si += 1
                    try_prefetch(i)
            if kind == "meta":
                grp = [(self.kT[:, h, 0:16], B(f"kT{h // 2}"), self.vcur[0:16, h, 0, :], B("vcur"), 16, None)]
            else:
                grp = [(self.kT[:, h, 0:128], B(f"kT{h // 2}"), self.vcur[:, h, 0, :], B("vcur"), 128, 0),
                       (self.kT[:, h, 128:256], B(f"kT{h // 2}"), self.vcur[:, h, 1, :], B("vcur"), 128, 1)]
            cur_blk[0] = None
            do_step(grp, si)
            si += 1
            assert si == nsteps
            pending[-1][0].append(lambda h=h, po=po, pob=pob: self.attn_finish(h, T, po, pob))
        for grp_, _ in pending:
            for f in grp_:
                f()

    def attn_sample(self, l):
        B = self.B
        T = 128
        NR = self.NROWS
        wA = self.wA
        for s in range(4):
            po, pob = self.ps[6 + s % 2], self.psb[6 + s % 2]
            blocks = []
            r = 0
            while r < NR:
                n = min(256, NR - r)
                blocks.append((r, n))
                r += n
            nb = len(blocks)
            for bi, (r0, n) in enumerate(blocks + [("new", 32)]):
                if r0 == "new":
                    tiles = [(0, 32)]
                    ps, psb = self.psum()
                    for j in range(2):
                        self.pe(lambda e, j=j, ps=ps, s=s: e.matmul(ps[0:32, :], self.ckvb[:, j, s * 32:(s + 1) * 32],
                                                               wA[:, O_WUV + j * 512:O_WUV + (j + 1) * 512],
                                                               start=(j == 0), stop=(j == 1)), [B("ckvb"), B("wA")], [psb])
                    self.vevac(ps, psb, 32, lambda par: self.vsm[0:32, par::2, par * 64:par * 64 + 64], B("vsm"))
                    kt_fn = lambda h, c0, nk, s=s: self.kT[:, h, s * 32:s * 32 + 32]
                    ktb = [B(f"kT{hp}") for hp in range(4)]
                    v_fn = lambda h, ti, nk: self.vsm[0:32, h, :]
                    vb_ = B("vsm")
                else:
                    tiles = [(c0, min(128, n - c0)) for c0 in range(0, n, 128)]
                    for ti, (c0, nk) in enumerate(tiles):
                        self.dma(self.clin[0:nk, ti, :], self.clat[l, s, r0 + c0:r0 + c0 + nk, :], [], [B(f"clin{ti}")])
                        self.dma(self.ckin[0:nk, ti, :], self.ckr[l, s, r0 + c0:r0 + c0 + nk, :], [], [B(f"ckin{ti}")])
                        import os
                        AS = int(os.environ.get("MK_AS", 99))
                        if AS <= -1:
                            continue
                        ps, psb = self.psum()
                        for j in range(2):
                            self.pe(lambda e, j=j, ti=ti, nk=nk, ps=ps: e.transpose(
                                ps[:, j * 128:j * 128 + nk], self.clin[0:nk, ti, j * 128:(j + 1) * 128], self.ident[0:nk, 0:nk]),
                                [B(f"clin{ti}"), B("ident")], [psb])
                        self.pe(lambda e, ti=ti, nk=nk, ps=ps: e.transpose(ps[0:32, 256:256 + nk], self.ckin[0:nk, ti, :],
                                                                           self.ident[0:nk, 0:nk]), [B(f"ckin{ti}"), B("ident")], [psb])
                        if AS <= 0:
                            continue
                        self.dve(lambda e, c0=c0, nk=nk, ps=ps: e.tensor_copy(
                            out=self.ckvb2[:, :, c0:c0 + nk], in_=ps[:, 0:256].rearrange("p (j t) -> p j t", j=2)[:, :, 0:nk]),
                            [psb], [B("ckvb2")])
                        if os.environ.get("MK_NODVE"):
                            continue
                        self.dve(lambda e, c0=c0, nk=nk, ps=ps: e.tensor_copy(out=self.kper2[:, c0:c0 + nk],
                                                                             in_=ps[0:32, 256:256 + nk]), [psb], [B("kper2")])
                    import os
                    AS = int(os.environ.get("MK_AS", 99))
                    if AS <= 1:
                        continue
                    self.kgen(l, n, self.ckvb2, B("ckvb2"), self.kper2, B("kper2"), self.kT2, "kT2")
                    if AS <= 2:
                        continue
                    for ti, (c0, nk) in enumerate(tiles):
                        ps, psb = self.psum()
                        for j in range(2):
                            self.pe(lambda e, j=j, c0=c0, nk=nk, ps=ps: e.matmul(
                                ps[0:nk, :], self.ckvb2[:, j, c0:c0 + nk], wA[:, O_WUV + j * 512:O_WUV + (j + 1) * 512],
                                start=(j == 0), stop=(j == 1)), [B("ckvb2"), B("wA")], [psb])
                        self.vevac(ps, psb, nk, lambda par, ti=ti, nk=nk: self.vcur2[0:nk, par::2, ti, par * 64:par * 64 + 64],
                                   B("vcur2"), on_dve=True)
                    kt_fn = lambda h, c0, nk: self.kT2[:, h, c0:c0 + nk]
                    ktb = [B(f"kT2{hp}") for hp in range(4)]
                    v_fn = lambda h, ti, nk: self.vcur2[0:nk, h, ti, :]
                    vb_ = B("vcur2")
                import os
                AS = int(os.environ.get("MK_AS", 99))
                if AS <= 3:
                    continue
                ps, psb = self.psum()
                pT, pTb = self.next_pT()
                nt = len(tiles)
                for h in range(H):
                    for ti, (c0, nk) in enumerate(tiles):
                        col = (h * nt + ti) * 32
                        self.pe(lambda e, h=h, c0=c0, nk=nk, col=col, ps=ps, kt_fn=kt_fn, s=s: e.matmul(
                            ps[0:nk, col:col + 32], kt_fn(h, c0, nk), self.qT[:, h, s * 32:s * 32 + 32], start=True, stop=True),
                            [ktb[h // 2], B(f"qT{h // 2}")], [psb])
                if nt == 2 and tiles[0][1] != tiles[1][1]:
                    raise NotImplementedError("ragged block")
                nk = tiles[0][1]
                w = H * nt * 32
                self.act(lambda e, nk=nk, w=w, ps=ps, pT=pT: e.activation(out=pT[0:nk, 0:w], in_=ps[0:nk, 0:w], func=AF.Exp,
                                                                          scale=SCALE), [psb], [pTb])
                if AS <= 4:
                    continue
                for h in range(H):
                    for ti, (c0, nk) in enumerate(tiles):
                        col = (h * nt + ti) * 32
                        st = (bi == 0 and ti == 0 and h == 0)
                        sp_ = (r0 == "new")
                        self.pe(lambda e, h=h, ti=ti, nk=nk, col=col, pT=pT, st=st, sp_=sp_, v_fn=v_fn, po=po: e.matmul(
                            po[:, h * 32:(h + 1) * 32], v_fn(h, ti, nk), pT[0:nk, col:col + 32], start=st, stop=sp_,
                            skip_group_check=True), [vb_, pTb], [pob])
            if AS <= 5:
                continue
            i = self._o % 2
            self._o += 1
            rl, rlb = self.rl[i], B(f"rl{i}")
            self.act(lambda e, rl=rl, po=po: e.activation(out=rl[:, 0:256], in_=po[:, 0:256], func=AF.Ln), [pob], [rlb])
            self.act(lambda e, rl=rl: e.activation(out=rl[:, 0:256], in_=rl[:, 0:256], func=AF.Exp, scale=-1.0), [], [rlb])
            for h in range(H):
                par = h % 2
                lo, hi = par * 64, par * 64 + 64
                olo, ohi = (1 - par) * 64, (1 - par) * 64 + 64
                self.dve(lambda e, h=h, lo=lo, hi=hi, olo=olo, ohi=ohi, rl=rl, po=po, s=s: e.tensor_tensor(
                    out=self.mixT[lo:hi, 4 + h // 2, s * 32:(s + 1) * 32], in0=po[lo:hi, h * 32:(h + 1) * 32],
                    in1=rl[olo:ohi, h * 32:(h + 1) * 32], op=ALU.mult), [pob, rlb], [B(f"mixT{4 + h // 2}")])

    def store_kv(self, l, kind, sup):
        B = self.B
        if kind == "meta":
            self.dbg(f"kTmeta{l}", self.kT[:, :, 0:16], [96, 8, 16], BF16, [B(f"kT{hp}") for hp in range(4)])
            self.dma(self.kc[l, :, :, 0:16].rearrange("h p t -> p h t"), self.kT[:, :, 0:16],
                     [B(f"kT{hp}") for hp in range(4)], [B(f"kc{l}_0")])
            self.dma(self.vc[l, :, 0:16, 0, :].rearrange("h p d -> p h d"), self.vcur[0:16, :, 0, :], [B("vcur")], [B(f"vc{l}_0")])
        else:
            t0 = 1 + sup * 2
            c0 = t0 * 128
            self.dma(self.kc[l, :, :, c0:c0 + TS].rearrange("h p t -> p h t"), self.kT[:, :, 0:TS],
                     [B(f"kT{hp}") for hp in range(4)], [B(f"kc{l}_{t0}"), B(f"kc{l}_{t0 + 1}")])
            self.dma(self.vc[l, :, :, t0:t0 + 2, :].rearrange("h p t d -> p h t d"), self.vcur[:, :, :, :],
                     [B("vcur")], [B(f"vc{l}_{t0}"), B(f"vc{l}_{t0 + 1}")])

    def out_proj(self, l, T):
        B = self.B
        mb = [B(f"mixT{k}") for k in range(8)]
        for m in range(8):
            ps, psb = self.psum()
            for k in range(8):
                self.pe(lambda e, k=k, m=m, ps=ps: e.matmul(ps[:, 0:T], self.wO[:, k * 1024 + m * 128:k * 1024 + (m + 1) * 128],
                                                            self.mixT[:, k, 0:T], start=(k == 0), stop=(k == 7)),
                        [mb[k], B("wO")], [psb])
            self.dve(lambda e, m=m, ps=ps: e.tensor_tensor(out=self.xT[:, m, 0:T], in0=ps[:, 0:T], in1=self.xT[:, m, 0:T],
                                                           op=ALU.add), [psb], [B("xT")])

    def ffn(self, l, T):
        B = self.B
        hb = self.hT_bufs()
        for c in range(NDN):
            self.load_dn(l, c)
        self.norm_x(T, 16 + l * 8)

        def down(c):
            wd = self.wDN[c % NDN]
            wdb = B(f"wDN{c % NDN}")
            for m in range(8):
                bk = 4 + m // 2
                half = m % 2
                self.pe(lambda e, c=c, m=m, bk=bk, half=half, wd=wd: e.matmul(
                    self.ps[bk][:, half * T:(half + 1) * T], wd[:, m * 128:(m + 1) * 128], self.aT[:, c % NAT, 0:T],
                    start=(c == 0 and half == 0), stop=(c == NFF - 1), skip_group_check=True),
                    [B(f"aT{c % NAT}"), wdb], [self.psb[bk]])
            if c + NDN < NFF:
                self.load_dn(l, c + NDN)

        for cc in range(NFF):
            wg = self.wGU[cc % NGU]
            wgb = B(f"wGU{cc % NGU}")
            bi = self._ps4 % 4
            self._ps4 += 1
            ps, psb = self.ps[bi], self.psb[bi]
            for gu in range(2):
                for k in range(8):
                    o = (k * 2 + gu) * 128
                    self.pe(lambda e, k=k, gu=gu, o=o, ps=ps, wg=wg: e.matmul(ps[:, gu * T:(gu + 1) * T], wg[:, o:o + 128],
                                                                              self.hT[:, k, 0:T], start=(k == 0), stop=(k == 7)),
                            [hb[k], wgb], [psb])
            i = self._sg % 2
            self._sg += 1
            sg, sgb = self.sg[i], B(f"sg{i}")
            self.act(lambda e, ps=ps, sg=sg: e.activation(out=sg[:, 0:T], in_=ps[:, 0:T], func=AF.Silu), [psb], [sgb])
            self.dve(lambda e, ps=ps, sg=sg, cc=cc: e.tensor_tensor(out=self.aT[:, cc % NAT, 0:T], in0=ps[:, T:2 * T],
                                                                    in1=sg[:, 0:T], op=ALU.mult), [psb, sgb], [B(f"aT{cc % NAT}")])
            if cc + NGU < NFF:
                self.load_gu(l, cc + NGU)
            if cc >= 1:
                down(cc - 1)
        down(NFF - 1)
        for m in range(8):
            bk = 4 + m // 2
            half = m % 2
            self.dve(lambda e, m=m, bk=bk, half=half: e.tensor_tensor(
                out=self.xT[:, m, 0:T], in0=self.ps[bk][:, half * T:(half + 1) * T], in1=self.xT[:, m, 0:T], op=ALU.add),
                [self.psb[bk]], [B("xT")])

    def tile(self, kind, sup=0):
        if kind == "meta":
            T, segs, tok_tiles, pos0 = 16, (1, 16), [(0, 16)], 0
            src = [(self.meta[:, :], 16)]
        elif kind == "sup":
            T, segs, tok_tiles = TS, (1, TS), [(0, 128), (128, 128)]
            pos0 = NMETA + sup * TS
            src = [(self.xp[sup * TS + i * 128:sup * TS + (i + 1) * 128, :], 128) for i in range(2)]
        else:
            T, segs, tok_tiles, pos0 = 128, (4, 32), [(s * 32, 32) for s in range(4)], self.LP
            src = [(self.xs[:, :], 128)]
        self.load_x(src, T)
        if kind == "meta":
            self.dbg("xT", self.xT[:, :, 0:16], [128, 8, 16], F32, [self.B("xT")])
        if not getattr(self, "_wA_first", False):
            self._wA_first = True
            self.load_wA(0)
        for l in range(2):
            self.load_wO(l)
            import os
            SS = int(os.environ.get("MK_SS", 99)) if kind == "sample" else 99
            self.norm_x(T, l * 8)
            if SS <= 1:
                return
            if kind == "meta" and l == 0:
                self.dbg("hT", self.hT[:, :, 0:16], [128, 8, 16], BF16, self.hT_bufs())
                self.dbg("rstd", self.rstd[:, 0:16], [128, 16], F32, [self.B("rstd")])
                self.dbg("wA", self.wA[:, 0:2048], [128, 2048], BF16, [self.B("wA")])
            if kind != "sample":
                self.attn_prefetch(l, kind, sup)
            self.mixer_front(l, T, segs, tok_tiles, pos0, kind, sup)
            if not self._cast1_done:
                self._cast1_done = True
                self.cast_weights(layers=(1,))
            if SS <= 2:
                return
            for c in range(NGU):
                self.load_gu(l, c)
            if kind == "meta" and l == 0:
                B = self.B
                self.dbg("uext", self.uext[:, 0:4 * 31], [128, 4 * 31], F32, [B(f"uext{g}") for g in range(4)])
                self.dbg("qn", self.qn[:, :, 0:16], [128, 3, 16], BF16, [B("qn")])
                self.dbg("ckv", self.ckv[:, :, 0:16], [128, 2, 16], F32, [B("ckv")])
                self.dbg("kper", self.kper[:, 0:16], [32, 16], F32, [B("kper")])
                self.dbg("dT", self.dT[:, :, 0:16], [128, 4, 16], BF16, [B("dT")])
                self.dbg("qT", self.qT[:, :, 0:16], [96, 8, 16], BF16, [B(f"qT{i}") for i in range(4)])
                self.dbg("kT", self.kT[:, :, 0:16], [96, 8, 16], BF16, [B(f"kT{i}") for i in range(4)])
                self.dbg("vcur", self.vcur[0:16, :, 0, :], [16, 8, 128], BF16, [B("vcur")])
            if kind == "sample":
                self.attn_sample(l)
                if SS <= 3:
                    return
            else:
                self.attn_prompt(l, T, kind, sup)
                if not (kind == "sup" and sup == self.NSUP - 1):
                    self.store_kv(l, kind, sup)
            if kind == "meta" and l == 0:
                self.dbg("mixT", self.mixT[:, :, 0:16], [128, 8, 16], BF16, [self.B(f"mixT{i}") for i in range(8)])
            for f in self._deferred:
                f()
            self._deferred = []
            self.load_wA(1 - l)
            if kind == "sup" and sup == 0:
                self.dbg(f"mix{l}", self.mixT[:, :, 0:256], [128, 8, 256], BF16, [self.B(f"mixT{i}") for i in range(8)])
            self.out_proj(l, T)
            if kind == "sup" and sup == 0:
                self.dbg(f"xo{l}", self.xT[:, :, 0:256], [128, 8, 256], F32, [self.B("xT")])
            if kind == "meta" and l == 0:
                self.dbg("x1", self.xT[:, :, 0:16], [128, 8, 16], F32, [self.B("xT")])
            self.ffn(l, T)
            if kind == "meta" and l == 0:
                self.dbg("x2", self.xT[:, :, 0:16], [128, 8, 16], F32, [self.B("xT")])
        if kind == "sup" and sup == 0:
            self.dbg("xfin", self.xT[:, :, 0:256], [128, 8, 256], F32, [self.B("xT")])
        if kind == "sup":
            dst = [(self.yp[sup * TS + i * 128:sup * TS + (i + 1) * 128, :], 128) for i in range(2)]
            self.store_y(dst, T)
        elif kind == "sample":
            self.store_y([(self.ys[:, :], 128)], T)

    def build(self):
        nc = self.nc
        self.declare()
        self.ckvb2 = self.sb("ckvb2", [128, 2, 256], BF16)
        self.kper2 = self.sb("kper2", [32, 256], F32)
        self.kT2 = self.sb("kT2", [96, 8, 256], BF16)
        self.vcur2 = self.sb("vcur2", [128, 8, 2, 128], BF16)
        self.pool(lambda e: e.memset(self.vcur2[:], 1.0), [], [self.B("vcur2")])
        self.init_consts()
        self.cast_weights(layers=(0, 1))
        self._cast1_done = True
        import os
        stg = os.environ.get("MK_STAGES", "sample,meta,sup")
        if "sample" in stg:
            self.tile("sample")
        if "meta" in stg:
            self.tile("meta")
        if "sup" in stg:
            for s in range(int(os.environ.get("MK_NSUP", self.NSUP))):
                self.tile("sup", s)
        for l in range(2):
            self.dbg(f"kcend{l}", self.kc[l, 0, :, 0:128], [96, 128], BF16, [self.B(f"kc{l}_0")])
        import contextlib
        with contextlib.ExitStack() as st:
            eng_sems = {e: st.enter_context(nc.semaphore(f"sem_{e}")) for e in Prog.ENGS}
            dma_sems = [st.enter_context(nc.semaphore(f"dsem{i}")) for i in range(40)]
            block = st.enter_context(nc.Block())
            reg = {"pe": block.tensor, "act": block.scalar, "dve": block.vector, "pool": block.gpsimd, "sp": block.sync}
            self.P.emit(nc, reg, eng_sems, dma_sems)
        return nc


def _pack_weights(w_in, w_uq, w_uk, w_uv, w_pool, w_o, w_gate, w_up, w_down):
    blob = np.zeros((2, 128, WTOT), np.float32)
    for l in range(2):
        wi = np.concatenate([w_in[l], w_in[l][:, 1152 + 16:1152 + 32], w_in[l][:, 1152:1152 + 16]], axis=1)
        blob[l, :, O_WIN:O_WIN + 8 * WIN_W] = wi.reshape(8, 128, WIN_W).transpose(1, 0, 2).reshape(128, -1)
        uq = w_uq[l].reshape(3, 128, 8, 96)
        uq2 = np.zeros((3, 128, 8, 192), np.float32)
        uq2[..., 0:96] = uq
        uq2[..., 96 + 64:96 + 80] = uq[..., 80:96]
        uq2[..., 96 + 80:96 + 96] = uq[..., 64:80]
        blob[l, :, O_WUQ:O_WUQ + 3 * 8 * 192] = uq2.transpose(1, 0, 2, 3).reshape(128, -1)
        uk = w_uk[l].reshape(2, 128, 8, 64)
        blob[l, :, O_WUK:O_WUK + 2 * 8 * 64] = uk.transpose(1, 0, 2, 3).reshape(128, -1)
        blob[l, :, O_WUV:O_WUV + 1024] = w_uv[l].reshape(2, 128, 512).transpose(1, 0, 2).reshape(128, -1)
        blob[l, :, O_WPOOL:O_WPOOL + 512] = w_pool[l].transpose(1, 0, 2).reshape(128, -1)
        blob[l, :, O_WO:O_WO + 8192] = w_o[l].reshape(8, 128, 1024).transpose(1, 0, 2).reshape(128, -1)
        g = w_gate[l].reshape(8, 128, NFF, 128)
        u = w_up[l].reshape(8, 128, NFF, 128)
        gu = np.stack([g, u], axis=3)
        blob[l, :, O_GU:O_GU + NFF * GU_PIECE] = gu.transpose(1, 2, 0, 3, 4).reshape(128, -1)
        dn = w_down[l].reshape(NFF, 128, 1024)
        blob[l, :, O_DN:O_DN + NFF * DN_PIECE] = dn.transpose(1, 0, 2).reshape(128, -1)
    return blob


def _pack_gains(norm_mix, norm_ffn, q_a_norm, kv_a_norm, pool_scale, q_norm, k_norm):
    g = np.zeros((128, NG), np.float32)
    g[:, 0:16] = norm_mix.reshape(2, 8, 128).transpose(2, 0, 1).reshape(128, 16)
    g[:, 16:32] = norm_ffn.reshape(2, 8, 128).transpose(2, 0, 1).reshape(128, 16)
    g[:, 32:38] = q_a_norm.reshape(2, 3, 128).transpose(2, 0, 1).reshape(128, 6)
    g[:, 38:42] = kv_a_norm.reshape(2, 2, 128).transpose(2, 0, 1).reshape(128, 4)
    g[:, 42:50] = pool_scale.reshape(2, 4, 128).transpose(2, 0, 1).reshape(128, 8)
    g[0:96, 50:52] = q_norm.T
    g[0:96, 52:54] = k_norm.T
    return g


def _rope_table(LP, pos_sample):
    half = DR // 2
    inv = (np.float32(10000.0) ** (-np.arange(half, dtype=np.float32) / np.float32(half))).astype(np.float32)
    pos = np.concatenate([np.arange(LP, dtype=np.float32), np.tile(pos_sample.astype(np.float32), 4)])
    ang = (pos[:, None] * inv[None, :]).astype(np.float32)
    cos = np.cos(ang).astype(np.float32).T
    sin = np.sin(ang).astype(np.float32).T
    tab = np.zeros((96, 2, pos.shape[0]), np.float32)
    tab[0:64, 0] = 1.0
    tab[64:80, 0] = cos
    tab[80:96, 0] = cos
    tab[64:80, 1] = -sin
    tab[80:96, 1] = sin
    return tab


_CACHE = {}


def kernel(x_prompt, x_sample, cache_latent, cache_krope, state_pool, meta_tokens,
           norm_mix, w_in, q_a_norm, w_uq, kv_a_norm, w_uk, w_uv, q_norm, k_norm,
           w_pool, pool_scale, w_o, norm_ffn, w_gate, w_up, w_down):
    f = lambda a: np.ascontiguousarray(np.asarray(a), dtype=np.float32)
    x_prompt, x_sample, cache_latent, cache_krope, state_pool, meta_tokens = map(
        f, (x_prompt, x_sample, cache_latent, cache_krope, state_pool, meta_tokens))
    NB, SEQ, _ = x_prompt.shape
    NROWS = cache_latent.shape[2]
    n = 8
    assert NB == n and x_sample.shape[0] == 4 * n and x_sample.shape[1] == 32 and SEQ % TS == 0
    LP = NMETA + SEQ
    key = (SEQ, NROWS)
    if key not in _CACHE:
        nc = bass.Bass("TRN2", target_bir_lowering=False)
        _CACHE[key] = Builder(SEQ, NROWS, nc).build()
    nc = _CACHE[key]
    blob = _pack_weights(*map(f, (w_in, w_uq, w_uk, w_uv, w_pool, w_o, w_gate, w_up, w_down)))
    gains = _pack_gains(*map(f, (norm_mix, norm_ffn, q_a_norm, kv_a_norm, pool_scale, q_norm, k_norm)))
    rope = _rope_table(LP, NROWS + np.arange(32))
    rcnt = np.zeros((128, 4, 16), np.float32)
    for g, w in enumerate(WINS):
        rcnt[:, g, :] = 1.0 / np.minimum(np.arange(16) + 1, w).astype(np.float32)
    ident = np.eye(128, dtype=np.float32)
    in_maps = []
    for c in range(n):
        in_maps.append({
            "xp": x_prompt[c], "meta": meta_tokens, "xs": x_sample[4 * c:4 * c + 4].reshape(128, D),
            "clat": np.ascontiguousarray(cache_latent[:, 4 * c:4 * c + 4]),
            "ckr": np.ascontiguousarray(cache_krope[:, 4 * c:4 * c + 4]),
            "spool": np.ascontiguousarray(state_pool[:, 4 * c:4 * c + 4]),
            "wblob": blob, "gains": gains, "rope": rope, "rcnt": rcnt, "ident": ident,
        })
    res = run_bass_kernel_spmd(nc, in_maps, core_ids=list(range(n)))
    R = res.results
    kernel.last = R
    y_prompt = np.stack([R[c]["yp"] for c in range(n)], axis=0)
    y_sample = np.concatenate([R[c]["ys"].reshape(4, 32, D) for c in range(n)], axis=0)
    lat_p = np.stack([R[c]["latp"] for c in range(n)], axis=1)
    kr_p = np.stack([R[c]["krp"] for c in range(n)], axis=1)
    pool_p = np.stack([R[c]["poolp"] for c in range(n)], axis=1)
    lat_s = np.concatenate([R[c]["lats"] for c in range(n)], axis=1)
    kr_s = np.concatenate([R[c]["krs"] for c in range(n)], axis=1)
    pool_s = np.concatenate([R[c]["pools"] for c in range(n)], axis=1)
    return (y_prompt, y_sample, lat_p, kr_p, pool_p, lat_s, kr_s, pool_s)
```
